# Optimizing a Trainium2 kernel written in Bass

```python
import math
import jax
import jax.numpy as jnp
from jax import lax
import numpy as np

D_MODEL = 1024
BATCH = 16
SEQ = 2048
DEPTH = 4

CHUNK = 64
N_MIXERS = 3
S5_GROUP = 16
S5_GROUPS = D_MODEL // S5_GROUP
S5_STATE = 64
S5_DT_MIN = 0.001
S5_DT_MAX = 0.1
DA_HEAD_DIM = 64
DA_HEADS = D_MODEL // (2 * DA_HEAD_DIM)
ROPE_THETA = 10000.0
Q_BLOCK = 128
POOL_WINDOWS = (2, 4, 8, 16)
POOL_GROUP = D_MODEL // len(POOL_WINDOWS)
D_FF = 4 * D_MODEL
EPS = 1e-6
N_S5 = (DEPTH + 2) // 3
N_DA = (DEPTH + 1) // 3
N_POOL = DEPTH // 3

kernel_name = 'hybrid_s5_diffattn_pool_stream_encoder'


def rmsnorm(x, g):
    xf = x.astype(jnp.float32)
    y = xf * lax.rsqrt(jnp.mean(xf * xf, axis=-1, keepdims=True) + EPS)
    return y * g.astype(jnp.float32)


def cmul(ar, ai, br, bi):
    return ar * br - ai * bi, ar * bi + ai * br


def s5_mixer(h, w_in, lam_re, lam_im, log_step, b_re, b_im, c_re, c_im, d_skip, w_gate, w_out):
    bsz, seq, _ = h.shape
    f32 = jnp.float32
    u = (h @ w_in).astype(f32)
    lr = jnp.minimum(lam_re.astype(f32), -1e-4)
    li = lam_im.astype(f32)
    dt = jnp.exp(log_step.astype(f32))[:, None]
    mag = jnp.exp(lr * dt)
    abar_re = mag * jnp.cos(li * dt)
    abar_im = mag * jnp.sin(li * dt)
    den = lr * lr + li * li
    nr = abar_re - 1.0
    f_re = (nr * lr + abar_im * li) / den
    f_im = (abar_im * lr - nr * li) / den
    bbar_re, bbar_im = cmul(f_re[..., None], f_im[..., None], b_re, b_im)
    n_chunks = seq // CHUNK
    ug = u.reshape(bsz, n_chunks, CHUNK, S5_GROUPS, S5_GROUP)
    ug = jnp.moveaxis(ug, 1, 0)
    a_re = jnp.broadcast_to(abar_re, (bsz, CHUNK, S5_GROUPS, S5_STATE))
    a_im = jnp.broadcast_to(abar_im, (bsz, CHUNK, S5_GROUPS, S5_STATE))

    def combine(e1, e2):
        a1r, a1i, b1r, b1i = e1
        a2r, a2i, b2r, b2i = e2
        ar, ai = cmul(a2r, a2i, a1r, a1i)
        br, bi = cmul(a2r, a2i, b1r, b1i)
        return ar, ai, br + b2r, bi + b2i

    def chunk_step(carry, u_c):
        h_re, h_im = carry
        bu_re = jnp.einsum('btgh,gph->btgp', u_c, bbar_re)
        bu_im = jnp.einsum('btgh,gph->btgp', u_c, bbar_im)
        pw_re, pw_im, xr, xi = lax.associative_scan(combine, (a_re, a_im, bu_re, bu_im), axis=1)
        cr, ci = cmul(pw_re, pw_im, h_re[:, None], h_im[:, None])
        xr = xr + cr
        xi = xi + ci
        y = jnp.einsum('btgp,ghp->btgh', xr, c_re) - jnp.einsum('btgp,ghp->btgh', xi, c_im)
        return (xr[:, -1], xi[:, -1]), y

    h0 = jnp.zeros((bsz, S5_GROUPS, S5_STATE), f32)
    _, ys = lax.scan(chunk_step, (h0, h0), ug)
    y = jnp.moveaxis(ys, 0, 1).reshape(bsz, seq, D_MODEL) + d_skip * u
    z = jax.nn.gelu(y)
    z = z * jax.nn.sigmoid(z @ w_gate)
    return z @ w_out


def apply_rope(t, cos, sin):
    half = t.shape[-1] // 2
    t1, t2 = t[..., :half], t[..., half:]
    c = cos[None, :, None, None, :]
    s = sin[None, :, None, None, :]
    return jnp.concatenate([t1 * c - t2 * s, t2 * c + t1 * s], axis=-1)


def diff_attn_mixer(h, w_qkv, lq1, lk1, lq2, lk2, subln, w_o, lam_init):
    bsz, seq, _ = h.shape
    f32 = jnp.float32
    qkv = h @ w_qkv
    q, k, v = jnp.split(qkv, 3, axis=-1)
    q = q.reshape(bsz, seq, DA_HEADS, 2, DA_HEAD_DIM).astype(f32)
    k = k.reshape(bsz, seq, DA_HEADS, 2, DA_HEAD_DIM).astype(f32)
    v = v.reshape(bsz, seq, DA_HEADS, 2 * DA_HEAD_DIM).astype(f32)
    pos = jnp.arange(seq, dtype=f32)
    inv_freq = ROPE_THETA ** (-jnp.arange(0, DA_HEAD_DIM, 2, dtype=f32) / DA_HEAD_DIM)
    ang = pos[:, None] * inv_freq[None, :]
    cos, sin = jnp.cos(ang), jnp.sin(ang)
    q = apply_rope(q, cos, sin) * (DA_HEAD_DIM ** -0.5)
    k = apply_rope(k, cos, sin)
    lam = (jnp.exp(jnp.sum(lq1.astype(f32) * lk1.astype(f32)))
           - jnp.exp(jnp.sum(lq2.astype(f32) * lk2.astype(f32))) + lam_init)
    n_blk = seq // Q_BLOCK
    qb = jnp.moveaxis(q.reshape(bsz, n_blk, Q_BLOCK, DA_HEADS, 2, DA_HEAD_DIM), 1, 0)
    k_chunk = jnp.arange(seq) // CHUNK

    def block(args):
        q_blk, j = args
        s = jnp.einsum('bqhmd,bkhmd->bhmqk', q_blk, k)
        q_chunk = (j * Q_BLOCK + jnp.arange(Q_BLOCK)) // CHUNK
        mask = k_chunk[None, :] <= q_chunk[:, None]
        p = jax.nn.softmax(jnp.where(mask, s, -jnp.inf), axis=-1)
        a = p[:, :, 0] - lam * p[:, :, 1]
        return jnp.einsum('bhqk,bkhe->bqhe', a, v)

    o = lax.map(block, (qb, jnp.arange(n_blk)))
    o = jnp.moveaxis(o, 0, 1).reshape(bsz, seq, DA_HEADS, 2 * DA_HEAD_DIM)
    o = o * lax.rsqrt(jnp.mean(o * o, axis=-1, keepdims=True) + EPS) * subln.astype(f32) * (1.0 - lam_init)
    return o.reshape(bsz, seq, D_MODEL) @ w_o


def pool_mixer(h, w_pool, scale):
    bsz, seq, _ = h.shape
    cs = jnp.cumsum(h, axis=1)
    t = jnp.arange(1, seq + 1, dtype=jnp.float32)[None, :, None]
    outs = []
    for g, w in enumerate(POOL_WINDOWS):
        sl = slice(g * POOL_GROUP, (g + 1) * POOL_GROUP)
        c = cs[..., sl]
        prev = jnp.pad(c, ((0, 0), (w, 0), (0, 0)))[:, :seq]
        mean = (c - prev) / jnp.minimum(t, float(w))
        outs.append(mean - h[..., sl])
    p = jnp.stack(outs, axis=2)
    y = jnp.einsum('bsgc,gcd->bsgd', p, w_pool).reshape(bsz, seq, D_MODEL)
    return y * scale


def sqrelu_mlp(h, w1, w2):
    return jnp.square(jax.nn.relu(h @ w1)) @ w2


def setup_inputs(seed: int = 0) -> dict:
    key = jax.random.key(seed)
    ks = iter(jax.random.split(key, 32))
    f32 = jnp.float32

    def nrm(shape, scale):
        return jax.random.normal(next(ks), shape, f32) * scale

    D, G, P, H = D_MODEL, S5_GROUPS, S5_STATE, S5_GROUP
    x = nrm((BATCH, SEQ, D), 1.0)
    norm_mix = 1.0 + nrm((DEPTH, D), 0.02)
    norm_mlp = 1.0 + nrm((DEPTH, D), 0.02)
    norm_final = 1.0 + nrm((D,), 0.02)
    s5_w_in = nrm((N_S5, D, D), D ** -0.5)
    s5_lam_re = -0.5 + nrm((N_S5, G, P), 0.01)
    s5_lam_im = math.pi * jnp.arange(P, dtype=f32) + nrm((N_S5, G, P), 0.01)
    s5_log_step = jax.random.uniform(next(ks), (N_S5, G), f32, math.log(S5_DT_MIN), math.log(S5_DT_MAX))
    s5_b_re = nrm((N_S5, G, P, H), (2 * H) ** -0.5)
    s5_b_im = nrm((N_S5, G, P, H), (2 * H) ** -0.5)
    s5_c_re = nrm((N_S5, G, H, P), 0.5)
    s5_c_im = nrm((N_S5, G, H, P), 0.5)
    s5_d = nrm((N_S5, D), 1.0)
    s5_w_gate = nrm((N_S5, D, D), D ** -0.5)
    s5_w_out = nrm((N_S5, D, D), D ** -0.5)
    da_w_qkv = nrm((N_DA, D, 3 * D), D ** -0.5)
    da_lam_q1 = nrm((N_DA, DA_HEAD_DIM), 0.1)
    da_lam_k1 = nrm((N_DA, DA_HEAD_DIM), 0.1)
    da_lam_q2 = nrm((N_DA, DA_HEAD_DIM), 0.1)
    da_lam_k2 = nrm((N_DA, DA_HEAD_DIM), 0.1)
    da_subln = 1.0 + nrm((N_DA, 2 * DA_HEAD_DIM), 0.02)
    da_w_o = nrm((N_DA, D, D), D ** -0.5)
    pool_w = nrm((N_POOL, len(POOL_WINDOWS), POOL_GROUP, POOL_GROUP), POOL_GROUP ** -0.5)
    pool_scale = 1.0 + nrm((N_POOL, D), 0.1)
    mlp_w1 = nrm((DEPTH, D, D_FF), D ** -0.5)
    mlp_w2 = nrm((DEPTH, D_FF, D), D_FF ** -0.5)
    return {'x': x, 'norm_mix': norm_mix, 'norm_mlp': norm_mlp, 'norm_final': norm_final,
            's5_w_in': s5_w_in, 's5_lam_re': s5_lam_re, 's5_lam_im': s5_lam_im, 's5_log_step': s5_log_step,
            's5_b_re': s5_b_re, 's5_b_im': s5_b_im, 's5_c_re': s5_c_re, 's5_c_im': s5_c_im, 's5_d': s5_d,
            's5_w_gate': s5_w_gate, 's5_w_out': s5_w_out,
            'da_w_qkv': da_w_qkv, 'da_lam_q1': da_lam_q1, 'da_lam_k1': da_lam_k1, 'da_lam_q2': da_lam_q2,
            'da_lam_k2': da_lam_k2, 'da_subln': da_subln, 'da_w_o': da_w_o,
            'pool_w': pool_w, 'pool_scale': pool_scale, 'mlp_w1': mlp_w1, 'mlp_w2': mlp_w2}


def reference(x, norm_mix, norm_mlp, norm_final,
              s5_w_in, s5_lam_re, s5_lam_im, s5_log_step, s5_b_re, s5_b_im, s5_c_re, s5_c_im, s5_d,
              s5_w_gate, s5_w_out,
              da_w_qkv, da_lam_q1, da_lam_k1, da_lam_q2, da_lam_k2, da_subln, da_w_o,
              pool_w, pool_scale, mlp_w1, mlp_w2):
    for i in range(DEPTH):
        kind, j = i % N_MIXERS, i // N_MIXERS
        h = rmsnorm(x, norm_mix[i])
        if kind == 0:
            m = s5_mixer(h, s5_w_in[j], s5_lam_re[j], s5_lam_im[j], s5_log_step[j], s5_b_re[j], s5_b_im[j],
                         s5_c_re[j], s5_c_im[j], s5_d[j], s5_w_gate[j], s5_w_out[j])
        elif kind == 1:
            lam_init = 0.8 - 0.6 * math.exp(-0.3 * i)
            m = diff_attn_mixer(h, da_w_qkv[j], da_lam_q1[j], da_lam_k1[j], da_lam_q2[j], da_lam_k2[j],
                                da_subln[j], da_w_o[j], lam_init)
        else:
            m = pool_mixer(h, pool_w[j], pool_scale[j])
        x = x + m.astype(x.dtype)
        h = rmsnorm(x, norm_mlp[i])
        x = x + sqrelu_mlp(h, mlp_w1[i], mlp_w2[i]).astype(x.dtype)
    return rmsnorm(x, norm_final).astype(x.dtype)
```

```python
import contextlib
import math
import numpy as np
import concourse.bass as bass
import concourse.mybir as mybir
from concourse.bass_utils import run_bass_kernel_spmd

F32 = mybir.dt.float32
BF16 = mybir.dt.bfloat16
I32 = mybir.dt.int32
AF = mybir.ActivationFunctionType
ALU = mybir.AluOpType

D = 1024
S = 2048
NFT = 8
TT = 512
NTT = S // TT
DFF = 4096
EPS = 1e-6
DEPTH = 4
N_CORES = 8
SEQ_PER_CORE = 2
V_PSCALE = 72
V_SUBLN = 80
V_LAM = 81
NV = 81 + 256
S5NC = 96 + 2048 + 64
LAM_INIT = 0.8 - 0.6 * math.exp(-0.3 * 1)
POOL_W = (2, 4, 8, 16)


class Buf:
    __slots__ = ("name", "w", "r")

    def __init__(self, name):
        self.name = name
        self.w = None
        self.r = {}


class Eng:
    def __init__(self, name, e, sem):
        self.name = name
        self.e = e
        self.sem = sem
        self.count = 0
        self.waited = {}


class Sync:
    def __init__(self, nc, n_dma_sems=12):
        self.nc = nc
        self.engs = {}
        for name, e in (("pe", nc.tensor), ("act", nc.scalar), ("dve", nc.vector),
                        ("pool", nc.gpsimd), ("sp", nc.sync)):
            self.engs[name] = Eng(name, e, nc.alloc_semaphore(name="s_" + name))
        self.dma_sems = {q: [[nc.alloc_semaphore(name=f"d{q}{i}"), 0] for i in range(n_dma_sems)] for q in ("sp", "pool")}
        self.dma_rr = {"sp": 0, "pool": 0}
        self.ninst = 0

    def _wait(self, eng, ticket):
        sem, val, src = ticket
        if src == "pe" and eng.name == "pe":
            return
        k = id(sem)
        if eng.waited.get(k, 0) >= val:
            return
        eng.e.wait_ge(sem, val)
        eng.waited[k] = val

    def _deps(self, eng, reads, writes):
        need = {}

        def add(t):
            if t is None:
                return
            k = id(t[0])
            if k not in need or need[k][1] < t[1]:
                need[k] = t
        for b in reads:
            add(b.w)
        for b in writes:
            add(b.w)
            for t in b.r.values():
                add(t)
        for t in need.values():
            self._wait(eng, t)

    @staticmethod
    def _mark(ticket, key, reads, writes):
        for b in reads:
            b.r[key] = ticket
        for b in writes:
            b.w = ticket
            b.r = {}

    def op(self, engname, fn, reads=(), writes=()):
        eng = self.engs[engname]
        self._deps(eng, reads, writes)
        inst = fn(eng.e)
        eng.count += 1
        inst.then_inc(eng.sem, 1)
        t = (eng.sem, eng.count, eng.name)
        self._mark(t, eng.name, reads, writes)
        self.ninst += 1
        return t

    def group(self, engname, fns, reads=(), writes=()):
        eng = self.engs[engname]
        self._deps(eng, reads, writes)
        inst = None
        for fn in fns:
            inst = fn(eng.e)
            self.ninst += 1
        eng.count += 1
        inst.then_inc(eng.sem, 1)
        t = (eng.sem, eng.count, eng.name)
        self._mark(t, eng.name, reads, writes)
        return t

    def dma(self, engname, fns, reads=(), writes=()):
        eng = self.engs[engname]
        self._deps(eng, reads, writes)
        pool_ = self.dma_sems[engname]
        slot = pool_[self.dma_rr[engname]]
        self.dma_rr[engname] = (self.dma_rr[engname] + 1) % len(pool_)
        sem, total = slot
        if total > 0:
            self._wait(eng, (sem, total, None))
        for fn in fns:
            fn(eng.e).then_inc(sem, 16)
            total += 16
            self.ninst += 1
        slot[1] = total
        t = (sem, total, None)
        self._mark(t, "dma%d" % id(sem), reads, writes)
        return t

    def barrier(self, names=("pe", "act", "dve", "pool", "sp")):
        for n in names:
            eng = self.engs[n]
            for m in names:
                if m != n:
                    o = self.engs[m]
                    if o.count > 0:
                        self._wait(eng, (o.sem, o.count, o.name))
            for q in self.dma_sems:
                for sem, total in self.dma_sems[q]:
                    if total > 0:
                        self._wait(eng, (sem, total, None))


class Prog:
    def __init__(self, n_seq=SEQ_PER_CORE, layers=None):
        if layers is None:
            layers = [(i, i % 3, True) for i in range(DEPTH)]
        self.layers = layers
        self.n_seq = n_seq
        self.nc = bass.Bass("TRN2", target_bir_lowering=False)
        self.es = contextlib.ExitStack()
        self._uid = 0

    def dram_in(self, name, shape):
        return self.nc.dram_tensor(name, list(shape), F32, kind="ExternalInput").ap()

    def sb(self, es, name, shape, dt=F32):
        self._uid += 1
        return es.enter_context(self.nc.sbuf_tensor(f"{name}_{self._uid}", list(shape), dt))

    @contextlib.contextmanager
    def scope(self):
        with contextlib.ExitStack() as es:
            yield es
            self.sy.barrier()

    def build(self):
        nc = self.nc
        ns = self.n_seq
        self.xT = self.dram_in("xT", [ns, D, S])
        self.gall = self.dram_in("vecs", [128, NV])
        self.w1 = self.dram_in("w1", [DEPTH, D, DFF])
        self.w2 = self.dram_in("w2", [DEPTH, DFF, D])
        self.poolw = self.dram_in("poolw", [4, 256, 256])
        self.wqkv = self.dram_in("wqkv", [D, 3 * D])
        self.wo = self.dram_in("wo", [D, D])
        self.s5w = self.dram_in("s5w", [2, 3, D, D])
        self.s5p = self.dram_in("s5p", [2, 128, S5NC])
        self.yT = nc.dram_tensor("yT", [ns, D, S], F32, kind="ExternalOutput").ap()
        with self.es as es:
            self.sy = Sync(nc)
            sy = self.sy
            self.ps = [es.enter_context(nc.psum_tensor(f"psb{i}", [128, 512], F32)) for i in range(8)]
            self.psB = [Buf(f"ps{i}") for i in range(8)]
            self.ones_bf = self.sb(es, "ones", [128, 128], BF16)
            self.onesB = Buf("ones")
            sy.op("dve", lambda e: e.memset(self.ones_bf[:], 1.0), writes=[self.onesB])
            self.idb = self.sb(es, "idb", [128, 128], BF16)
            self.idbB = Buf("idb")
            with self.scope() as es0:
                idf = self.sb(es0, "idf0", [128, 128])
                idfB = Buf("idf0")
                sy.op("pool", lambda e: e.memset(idf[:], 0.0), writes=[idfB])
                sy.op("pool", lambda e: e.affine_select(out=idf[:], in_=idf[:], pattern=[[-1, 128]], compare_op=ALU.not_equal, fill=1.0, base=0, channel_multiplier=1),
                      reads=[idfB], writes=[idfB])
                sy.op("dve", lambda e: e.tensor_copy(out=self.idb[:], in_=idf[:]), reads=[idfB], writes=[self.idbB])
            self.epsT = self.sb(es, "eps", [128, 1])
            self.epsB = Buf("eps")
            sy.op("dve", lambda e: e.memset(self.epsT[:], EPS), writes=[self.epsB])
            self.g = self.sb(es, "gall", [128, NV])
            self.gB = Buf("g")
            sy.dma("sp", [lambda e: e.dma_start(out=self.g[:], in_=self.gall)], writes=[self.gB])
            self.x = self.sb(es, "x", [128, NFT, S])
            self.xB = [[Buf(f"x{ft}_{tt}") for tt in range(NTT)] for ft in range(NFT)]
            self.load_x(0)
            self.s5c = {}
            self.rope_c = None
            for (li_, kind_, _m) in self.layers:
                if kind_ == 0 and (li_ // 3) not in self.s5c:
                    self.s5_precompute(li_ // 3)
            for s in range(ns):
                self.seq(s, s + 1 if s + 1 < ns else None)
            sy.barrier()
        return nc

    def gcol(self, idx, ft):
        c = idx * NFT + ft
        return self.g[:, c:c + 1]

    def xt(self, ft, tt):
        return self.x[:, ft, tt * TT:(tt + 1) * TT]

    def load_x(self, s):
        sy = self.sy
        for tt in range(NTT):
            for ft in range(NFT):
                sy.dma("sp", [lambda e, ft=ft, tt=tt: e.dma_start(out=self.xt(ft, tt), in_=self.xT[s, ft * 128:(ft + 1) * 128, tt * TT:(tt + 1) * TT])],
                       writes=[self.xB[ft][tt]])

    def seq(self, s, next_s=None):
        sy = self.sy
        for (li, kind, do_mlp) in self.layers:
            if kind == 0:
                self.s5(li, li // 3)
            elif kind == 1:
                self.attn(li)
            elif kind == 2:
                self.pool(li)
            if do_mlp:
                self.mlp(li)
        self.final(s, next_s)

    def rmsnorm(self, es, gidx, h, hB, inline=False):
        sy = self.sy
        if inline:
            self._rmsnorm(es, gidx, h, hB)
            return
        with self.scope() as es2:
            self._rmsnorm(es2, gidx, h, hB)

    def _rmsnorm(self, es, gidx, h, hB):
        sy = self.sy
        sq = [self.sb(es, "sq", [128, NFT, TT], BF16) for _ in range(2)]
        sqB = [Buf("sq0"), Buf("sq1")]
        rs = [self.sb(es, "rs", [128, TT]) for _ in range(2)]
        rsB = [Buf("rs0"), Buf("rs1")]
        pbank = [6, 7]

        def square(tt):
            b = tt % 2
            xin = self.x[:, :, tt * TT:(tt + 1) * TT]
            sy.op("act", lambda e: e.activation(out=sq[b][:], in_=xin, func=AF.Square),
                  reads=[self.xB[ft][tt] for ft in range(NFT)], writes=[sqB[b]])

        square(0)
        for tt in range(NTT):
            b = tt % 2
            if tt + 1 < NTT:
                square(tt + 1)
            pb = pbank[b]
            sy.group("pe", [lambda e, b=b, ft=ft, pb=pb: e.matmul(self.ps[pb][:], lhsT=self.ones_bf[:], rhs=sq[b][:, ft, :],
                                                                start=(ft == 0), stop=(ft == NFT - 1)) for ft in range(NFT)],
                     reads=[sqB[b], self.onesB], writes=[self.psB[pb]])
            sy.op("act", lambda e, b=b, pb=pb: e.activation(out=rs[b][:], in_=self.ps[pb][:], func=AF.Ln, scale=1.0 / D, bias=self.epsT[:]),
                  reads=[self.psB[pb], self.epsB], writes=[rsB[b]])
            sy.op("act", lambda e, b=b: e.activation(out=rs[b][:], in_=rs[b][:], func=AF.Exp, scale=-0.5), reads=[rsB[b]], writes=[rsB[b]])
            for ft in range(NFT):
                sy.op("dve", lambda e, b=b, ft=ft, tt=tt: e.scalar_tensor_tensor(
                    out=h[:, ft, tt * TT:(tt + 1) * TT], in0=self.xt(ft, tt), scalar=self.gcol(gidx, ft), in1=rs[b][:],
                    op0=ALU.mult, op1=ALU.mult),
                    reads=[self.xB[ft][tt], rsB[b], self.gB], writes=[hB[ft][tt]])

    def mlp(self, li):
        sy = self.sy
        HC = 4
        NHC = DFF // (HC * 128)
        with self.scope() as es:
            h = self.sb(es, "h", [128, NFT, S], BF16)
            hB = [[Buf(f"h{ft}_{tt}") for tt in range(NTT)] for ft in range(NFT)]
            hid = [self.sb(es, "hid", [128, HC, S], BF16) for _ in range(2)]
            hidB = [[[Buf(f"hid{b}_{hi}_{tt}") for tt in range(NTT)] for hi in range(HC)] for b in range(2)]
            w1c = [self.sb(es, "w1c", [128, NFT, HC * 128], BF16) for _ in range(2)]
            w1B = [Buf("w1c0"), Buf("w1c1")]
            w2c = [self.sb(es, "w2c", [128, HC, D], BF16) for _ in range(2)]
            w2B = [Buf("w2c0"), Buf("w2c1")]
            rt = [self.sb(es, "rt", [128, TT]) for _ in range(3)]
            rtB = [Buf(f"rt{i}") for i in range(3)]
            cnt = {"f": 0, "s": 0, "r": 0}

            def load(hc):
                b = hc % 2
                c0 = hc * HC * 128
                sy.dma("pool", [lambda e: e.dma_start(out=w1c[b][:], in_=self.w1[li, :, c0:c0 + HC * 128].rearrange("(ft p) n -> p ft n", p=128))],
                       writes=[w1B[b]])
                sy.dma("pool", [lambda e: e.dma_start(out=w2c[b][:], in_=self.w2[li, c0:c0 + HC * 128, :].rearrange("(hi p) n -> p hi n", p=128))],
                       writes=[w2B[b]])

            def first(hc):
                b = hc % 2
                for tt in range(NTT):
                    for h2 in range(HC // 2):
                        pbs = [2 * (cnt["f"] % 2), 2 * (cnt["f"] % 2) + 1]
                        cnt["f"] += 1
                        fns = []
                        for q, pb in enumerate(pbs):
                            hi = 2 * h2 + q
                            for ft in range(NFT):
                                fns.append(lambda e, ft=ft, pb=pb, hi=hi: e.matmul(
                                    self.ps[pb][:], lhsT=w1c[b][:, ft, hi * 128:(hi + 1) * 128], rhs=h[:, ft, tt * TT:(tt + 1) * TT],
                                    start=(ft == 0), stop=(ft == NFT - 1)))
                        sy.group("pe", fns, reads=[w1B[b]] + [hB[ft][tt] for ft in range(NFT)], writes=[self.psB[pb] for pb in pbs])
                        for q, pb in enumerate(pbs):
                            hi = 2 * h2 + q
                            r = cnt["r"] % 3
                            cnt["r"] += 1
                            sy.op("act", lambda e, pb=pb, r=r: e.activation(out=rt[r][:], in_=self.ps[pb][:], func=AF.Relu),
                                  reads=[self.psB[pb]], writes=[rtB[r]])
                            sy.op("pool", lambda e, r=r, hi=hi: e.tensor_tensor(
                                out=hid[b][:, hi, tt * TT:(tt + 1) * TT], in0=rt[r][:], in1=rt[r][:], op=ALU.mult),
                                reads=[rtB[r]], writes=[hidB[b][hi][tt]])

            def second(hc):
                b = hc % 2
                for tt in range(NTT):
                    for f2 in range(NFT // 2):
                        pbs = [4 + 2 * (cnt["s"] % 2), 5 + 2 * (cnt["s"] % 2)]
                        cnt["s"] += 1
                        fns = []
                        for q, pb in enumerate(pbs):
                            fo = 2 * f2 + q
                            for hi in range(HC):
                                fns.append(lambda e, hi=hi, pb=pb, fo=fo: e.matmul(
                                    self.ps[pb][:], lhsT=w2c[b][:, hi, fo * 128:(fo + 1) * 128], rhs=hid[b][:, hi, tt * TT:(tt + 1) * TT],
                                    start=(hi == 0), stop=(hi == HC - 1)))
                        sy.group("pe", fns, reads=[w2B[b]] + [hidB[b][hi][tt] for hi in range(HC)], writes=[self.psB[pb] for pb in pbs])
                        for q, pb in enumerate(pbs):
                            fo = 2 * f2 + q
                            sy.op("dve", lambda e, pb=pb, fo=fo: e.tensor_tensor(
                                out=self.xt(fo, tt), in0=self.xt(fo, tt), in1=self.ps[pb][:], op=ALU.add),
                                reads=[self.psB[pb], self.xB[fo][tt]], writes=[self.xB[fo][tt]])

            import os
            dbg = int(os.environ.get("MLPDBG", "9"))
            load(0)
            load(1)
            self.rmsnorm(es, 4 + li, h, hB, inline=True)
            if dbg >= 2:
                first(0)
            for hc in range(NHC):
                if hc + 1 < NHC and dbg >= 2:
                    first(hc + 1)
                if dbg >= 3:
                    second(hc)
                if hc + 2 < NHC:
                    load(hc + 2)

    def pool(self, li):
        sy = self.sy
        with self.scope() as es:
            h = self.sb(es, "hp", [128, NFT, S])
            hB = [[Buf(f"hp{ft}_{tt}") for tt in range(NTT)] for ft in range(NFT)]
            self.rmsnorm(es, li, h, hB)
            p = self.sb(es, "pp", [128, NFT, S], BF16)
            pB = [Buf(f"pp{ft}") for ft in range(NFT)]
            wp = self.sb(es, "wp", [128, 4, 2, 256], BF16)
            wpB = Buf("wp")
            sy.dma("pool", [lambda e: e.dma_start(out=wp[:], in_=self.poolw.rearrange("g (kt p) d -> p g kt d", p=128))], writes=[wpB])
            rc = self.sb(es, "rc", [128, 16])
            rcB = Buf("rc")
            sy.op("pool", lambda e: e.iota(rc[:], pattern=[[1, 16]], base=1, channel_multiplier=0, allow_small_or_imprecise_dtypes=True), writes=[rcB])
            sy.op("dve", lambda e: e.reciprocal(out=rc[:], in_=rc[:]), reads=[rcB], writes=[rcB])
            ab = [[self.sb(es, "pa", [128, S]) for _ in range(2)] for _ in range(2)]
            abB = [[Buf("pa"), Buf("pb")] for _ in range(2)]
            tmpc = [self.sb(es, "ptc", [128, 16]) for _ in range(2)]
            tmpB = [Buf("ptc0"), Buf("ptc1")]
            for ft in range(NFT):
                w = POOL_W[ft // 2]
                eng = "dve" if ft % 2 == 0 else "pool"
                k = ft % 2
                hall = [hB[ft][tt] for tt in range(NTT)]
                src, srcB = h[:, ft, :], hall
                sh = 1
                i = 0
                while sh < w:
                    dst, dstB = ab[k][i % 2], [abB[k][i % 2]]
                    sy.op(eng, lambda e, dst=dst, src=src, sh=sh: e.tensor_tensor(out=dst[:, sh:], in0=src[:, sh:], in1=src[:, :S - sh], op=ALU.add),
                          reads=srcB, writes=dstB)
                    sy.op(eng, lambda e, dst=dst, src=src, sh=sh: e.tensor_copy(out=dst[:, 0:sh], in_=src[:, 0:sh]), reads=srcB, writes=dstB)
                    src, srcB = dst[:], dstB
                    sh *= 2
                    i += 1
                sy.op("dve", lambda e, src=src, ft=ft, w=w: e.scalar_tensor_tensor(out=p[:, ft, :], in0=src, scalar=1.0 / w, in1=h[:, ft, :],
                                                                                 op0=ALU.mult, op1=ALU.subtract),
                      reads=srcB + hall, writes=[pB[ft]])
                sy.op("dve", lambda e, src=src, k=k, w=w: e.tensor_tensor(out=tmpc[k][:, 0:w - 1], in0=src[:, 0:w - 1], in1=rc[:, 0:w - 1], op=ALU.mult),
                      reads=srcB + [rcB], writes=[tmpB[k]])
                sy.op("dve", lambda e, k=k, ft=ft, w=w: e.tensor_tensor(out=p[:, ft, 0:w - 1], in0=tmpc[k][:, 0:w - 1], in1=h[:, ft, 0:w - 1], op=ALU.subtract),
                      reads=[tmpB[k]] + hall, writes=[pB[ft]])
            cnt = 0
            for tt in range(NTT):
                for g in range(4):
                    for oc in range(2):
                        pb = cnt % 4
                        cnt += 1
                        fo = 2 * g + oc
                        sy.group("pe", [lambda e, kt=kt, pb=pb, g=g, oc=oc, tt=tt: e.matmul(
                            self.ps[pb][:], lhsT=wp[:, g, kt, oc * 128:(oc + 1) * 128], rhs=p[:, 2 * g + kt, tt * TT:(tt + 1) * TT],
                            start=(kt == 0), stop=(kt == 1)) for kt in range(2)],
                            reads=[wpB, pB[2 * g], pB[2 * g + 1]], writes=[self.psB[pb]])
                        sy.op("dve", lambda e, pb=pb, fo=fo, tt=tt: e.scalar_tensor_tensor(
                            out=self.xt(fo, tt), in0=self.ps[pb][:], scalar=self.g[:, V_PSCALE + fo:V_PSCALE + fo + 1], in1=self.xt(fo, tt),
                            op0=ALU.mult, op1=ALU.add),
                            reads=[self.psB[pb], self.xB[fo][tt], self.gB], writes=[self.xB[fo][tt]])

    def attn(self, li):
        sy = self.sy
        NH = 8
        TWO_PI = 2.0 * math.pi
        with self.scope() as es:
            h = self.sb(es, "ha", [128, NFT, S], BF16)
            hB = [[Buf(f"ha{ft}_{tt}") for tt in range(NTT)] for ft in range(NFT)]
            hall = [hB[ft][tt] for ft in range(NFT) for tt in range(NTT)]
            cosT = self.sb(es, "cosT", [128, S])
            sinS = self.sb(es, "sinS", [128, S])
            cosB, sinB = Buf("cosT"), Buf("sinS")
            perm = self.sb(es, "perm", [128, 128])
            permB = Buf("perm")
            nlam = self.sb(es, "nlam", [128, 1])
            nlamB = Buf("nlam")
            subs = self.sb(es, "subs", [128, 1])
            subsB = Buf("subs")
            with self.scope() as es2:
                if self.rope_c is None:
                    jf = self.sb(es2, "jf", [128, 1])
                    jB = Buf("jf")
                    pi_ = self.sb(es2, "pi", [128, 1])
                    piB = Buf("pi")
                    sy.op("pool", lambda e: e.iota(pi_[:], pattern=[[0, 1]], base=0, channel_multiplier=1, allow_small_or_imprecise_dtypes=True), writes=[piB])
                    qi = self.sb(es2, "qi", [128, 1], I32)
                    qB = Buf("qi")
                    sy.op("dve", lambda e: e.tensor_scalar(out=jf[:], in0=pi_[:], scalar1=-15.5, scalar2=1.0 / 32, op0=ALU.add, op1=ALU.mult), reads=[piB], writes=[jB])
                    sy.op("dve", lambda e: e.tensor_copy(out=qi[:], in_=jf[:]), reads=[jB], writes=[qB])
                    sy.op("dve", lambda e: e.tensor_copy(out=jf[:], in_=qi[:]), reads=[qB], writes=[jB])
                    sy.op("dve", lambda e: e.scalar_tensor_tensor(out=jf[:], in0=jf[:], scalar=-32.0, in1=pi_[:], op0=ALU.mult, op1=ALU.add), reads=[jB, piB], writes=[jB])
                    invf = self.sb(es2, "invf", [128, 1])
                    ifB = Buf("invf")
                    sy.op("act", lambda e: e.activation(out=invf[:], in_=jf[:], func=AF.Exp, scale=-math.log(10000.0) * 2.0 / 64.0), reads=[jB], writes=[ifB])
                    sgn = self.sb(es2, "sgn", [128, 1])
                    sgB = Buf("sgn")
                    sy.op("pool", lambda e: e.memset(sgn[:], 1.0), writes=[sgB])
                    sy.op("pool", lambda e: e.memset(sgn[0:32, :], -1.0), reads=[], writes=[sgB])
                    sy.op("pool", lambda e: e.memset(sgn[64:96, :], -1.0), reads=[], writes=[sgB])
                    ang = self.sb(es2, "ang", [128, S])
                    angB = Buf("ang")
                    ni = self.sb(es2, "ni", [128, S], I32)
                    niB = Buf("ni")
                    nf = self.sb(es2, "nf", [128, S])
                    nfB = Buf("nf")
                    sy.op("pool", lambda e: e.iota(ang[:], pattern=[[1, S]], base=0, channel_multiplier=0, allow_small_or_imprecise_dtypes=True), writes=[angB])
                    sy.op("dve", lambda e: e.tensor_scalar(out=ang[:], in0=ang[:], scalar1=invf[:], scalar2=None, op0=ALU.mult), reads=[angB, ifB], writes=[angB])

                    def sin_of(dst, dstB, shift, signed):
                        sy.op("dve", lambda e: e.tensor_scalar(out=nf[:], in0=ang[:], scalar1=shift, scalar2=1.0 / TWO_PI, op0=ALU.add, op1=ALU.mult), reads=[angB], writes=[nfB])
                        sy.op("dve", lambda e: e.tensor_copy(out=ni[:], in_=nf[:]), reads=[nfB], writes=[niB])
                        sy.op("dve", lambda e: e.tensor_copy(out=nf[:], in_=ni[:]), reads=[niB], writes=[nfB])
                        sy.op("dve", lambda e: e.scalar_tensor_tensor(out=nf[:], in0=nf[:], scalar=-TWO_PI, in1=ang[:], op0=ALU.mult, op1=ALU.add), reads=[nfB, angB], writes=[nfB])
                        sy.op("dve", lambda e: e.tensor_scalar(out=nf[:], in0=nf[:], scalar1=shift, scalar2=math.pi, op0=ALU.add, op1=ALU.min), reads=[nfB], writes=[nfB])
                        sy.op("dve", lambda e: e.tensor_scalar(out=nf[:], in0=nf[:], scalar1=-math.pi, scalar2=None, op0=ALU.max), reads=[nfB], writes=[nfB])
                        if signed:
                            sy.op("act", lambda e: e.activation(out=dst[:], in_=nf[:], func=AF.Sin), reads=[nfB], writes=[dstB])
                            sy.op("dve", lambda e: e.tensor_scalar(out=dst[:], in0=dst[:], scalar1=sgn[:], scalar2=None, op0=ALU.mult), reads=[dstB, sgB], writes=[dstB])
                        else:
                            sy.op("act", lambda e: e.activation(out=dst[:], in_=nf[:], func=AF.Sin), reads=[nfB], writes=[dstB])
                    sin_of(sinS, sinB, 0.0, True)
                    sin_of(cosT, cosB, math.pi / 2, False)
                    self.rope_c = self.nc.dram_tensor("rope_c", [128, 2 * S], F32, kind="Internal").ap()
                    sy.dma("sp", [lambda e: e.dma_start(out=self.rope_c[:, 0:S], in_=cosT[:]),
                                  lambda e: e.dma_start(out=self.rope_c[:, S:2 * S], in_=sinS[:])], reads=[cosB, sinB])
                else:
                    sy.dma("sp", [lambda e: e.dma_start(out=cosT[:], in_=self.rope_c[:, 0:S]),
                                  lambda e: e.dma_start(out=sinS[:], in_=self.rope_c[:, S:2 * S])], writes=[cosB, sinB])
                sy.op("pool", lambda e: e.memset(perm[:], 0.0), writes=[permB])
                for (c0, off) in ((0, 32), (32, -32), (64, 32), (96, -32)):
                    sy.op("pool", lambda e, c0=c0, off=off: e.affine_select(
                        out=perm[:, c0:c0 + 32], in_=perm[:, c0:c0 + 32], pattern=[[-1, 32]], compare_op=ALU.not_equal, fill=1.0,
                        base=-(c0 + off), channel_multiplier=1), reads=[permB], writes=[permB])
                lt = self.sb(es2, "lt", [128, 2, 64])
                ltB = Buf("lt")
                ls = self.sb(es2, "ls", [128, 2])
                lsB = Buf("ls")
                lv = self.g[:, V_LAM:V_LAM + 256].rearrange("p (a b c) -> p a b c", a=2, b=2)
                sy.op("dve", lambda e: e.tensor_tensor(out=lt[:], in0=lv[:, :, 0, :], in1=lv[:, :, 1, :], op=ALU.mult), reads=[self.gB], writes=[ltB])
                sy.op("dve", lambda e: e.reduce_sum(out=ls[:], in_=lt[:], axis=mybir.AxisListType.X), reads=[ltB], writes=[lsB])
                sy.op("act", lambda e: e.activation(out=ls[:], in_=ls[:], func=AF.Exp), reads=[lsB], writes=[lsB])
                sy.op("dve", lambda e: e.tensor_tensor(out=nlam[:], in0=ls[:, 1:2], in1=ls[:, 0:1], op=ALU.subtract), reads=[lsB], writes=[nlamB])
                sy.op("dve", lambda e: e.tensor_scalar(out=nlam[:], in0=nlam[:], scalar1=-LAM_INIT, scalar2=None, op0=ALU.add), reads=[nlamB], writes=[nlamB])
                sy.op("dve", lambda e: e.tensor_scalar(out=subs[:], in0=self.g[:, V_SUBLN:V_SUBLN + 1], scalar1=1.0 - LAM_INIT, scalar2=None, op0=ALU.mult),
                      reads=[self.gB], writes=[subsB])
            wq = [self.sb(es, "wq", [128, NFT, 3, 128], BF16) for _ in range(2)]
            wqB = [Buf("wq0"), Buf("wq1")]
            woh = [self.sb(es, "woh", [128, D], BF16) for _ in range(4)]
            woB = [Buf(f"wo{i}") for i in range(4)]

            def load_w(hd):
                b = hd % 2
                fns = []
                for j in range(3):
                    c0 = j * D + hd * 128
                    fns.append(lambda e, j=j, c0=c0: e.dma_start(out=wq[b][:, :, j, :], in_=self.wqkv[:, c0:c0 + 128].rearrange("(kt p) n -> p kt n", p=128)))
                sy.dma("pool", fns, writes=[wqB[b]])

            def load_wo(hd):
                b = hd % 4
                sy.dma("pool", [lambda e: e.dma_start(out=woh[b][:], in_=self.wo[hd * 128:(hd + 1) * 128, :])], writes=[woB[b]])

            load_w(0)
            load_w(1)
            for h_ in range(4):
                load_wo(h_)
            self.rmsnorm(es, li, h, hB)
            qh = [self.sb(es, "qh", [128, S], BF16) for _ in range(2)]
            kh = [self.sb(es, "kh", [128, S], BF16) for _ in range(2)]
            vh = [self.sb(es, "vh", [128, 16, 128], BF16) for _ in range(2)]
            oth = [self.sb(es, "oth", [128, S], BF16) for _ in range(4)]
            qB = [[Buf(f"qh{b}_{tt}") for tt in range(NTT)] for b in range(2)]
            kB = [[Buf(f"kh{b}_{tt}") for tt in range(NTT)] for b in range(2)]
            vB = [[Buf(f"vh{b}_{j}") for j in range(4)] for b in range(2)]
            oB = [[Buf(f"oth{b}_{tt}") for tt in range(NTT)] for b in range(4)]
            qf = [self.sb(es, "qf", [128, TT]) for _ in range(2)]
            qfB = [Buf("qf0"), Buf("qf1")]
            ta = [self.sb(es, "ta", [128, TT]) for _ in range(2)]
            taB = [Buf("ta0"), Buf("ta1")]
            tb = [self.sb(es, "tb", [128, TT]) for _ in range(2)]
            tbB = [Buf("tb0"), Buf("tb1")]
            pT = [self.sb(es, "pT", [128, TT], BF16) for _ in range(4)]
            pTB = [Buf(f"pT{i}") for i in range(4)]
            t1 = self.sb(es, "t1", [128, TT])
            t1B = Buf("t1")
            rr = [self.sb(es, "rr", [128, TT]) for _ in range(2)]
            rrB = [Buf("rr0"), Buf("rr1")]
            rs = self.sb(es, "rs_a", [128, TT])
            rsB = Buf("rs_a")
            pcp = [self.sb(es, "pcp", [128, TT]) for _ in range(2)]
            pcpB = [Buf("pcp0"), Buf("pcp1")]
            of = self.sb(es, "of", [128, TT])
            ofB = Buf("of")
            osq = self.sb(es, "osq", [128, TT], BF16)
            osqB = Buf("osq")
            cnt = {"m": 0, "s": 0, "o": 0, "p": 0, "q": 0}

            def misc_bank():
                cnt["m"] += 1
                return cnt["m"] % 2

            def proj(hd):
                b = hd % 2
                for j, (dst, dB, sc) in enumerate(((qh[b], qB[b], 0.125), (kh[b], kB[b], 1.0))):
                    for tt in range(NTT):
                        pb = misc_bank()
                        i2 = cnt["q"] % 2
                        cnt["q"] += 1
                        sy.group("pe", [lambda e, kt=kt, pb=pb, j=j, tt=tt: e.matmul(
                            self.ps[pb][:], lhsT=wq[b][:, kt, j, :], rhs=h[:, kt, tt * TT:(tt + 1) * TT], start=(kt == 0), stop=(kt == NFT - 1))
                            for kt in range(NFT)], reads=[wqB[b]] + [hB[kt][tt] for kt in range(NFT)], writes=[self.psB[pb]])
                        sy.op("dve", lambda e, pb=pb, i2=i2, sc=sc: e.tensor_scalar(out=qf[i2][:], in0=self.ps[pb][:], scalar1=sc, scalar2=None, op0=ALU.mult),
                              reads=[self.psB[pb]], writes=[qfB[i2]])
                        yield
                        pb2 = misc_bank()
                        sy.op("pe", lambda e, pb2=pb2, i2=i2: e.matmul(self.ps[pb2][:], lhsT=perm[:], rhs=qf[i2][:], start=True, stop=True),
                              reads=[permB, qfB[i2]], writes=[self.psB[pb2]])
                        sy.op("dve", lambda e, i2=i2, tt=tt: e.tensor_tensor(out=ta[i2][:], in0=qf[i2][:], in1=cosT[:, tt * TT:(tt + 1) * TT], op=ALU.mult),
                              reads=[qfB[i2], cosB], writes=[taB[i2]])
                        sy.op("dve", lambda e, i2=i2, tt=tt, pb2=pb2: e.tensor_tensor(out=tb[i2][:], in0=self.ps[pb2][:], in1=sinS[:, tt * TT:(tt + 1) * TT], op=ALU.mult),
                              reads=[self.psB[pb2], sinB], writes=[tbB[i2]])
                        sy.op("pool", lambda e, i2=i2, tt=tt, dst=dst: e.tensor_tensor(out=dst[:, tt * TT:(tt + 1) * TT], in0=ta[i2][:], in1=tb[i2][:], op=ALU.add),
                              reads=[taB[i2], tbB[i2]], writes=[dB[tt]])
                        yield
                for jq in range(4):
                    pb = misc_bank()
                    fns = []
                    for jj in range(4):
                        t0 = (jq * 4 + jj) * 128
                        for kt in range(NFT):
                            fns.append(lambda e, kt=kt, jj=jj, t0=t0, pb=pb: e.matmul(
                                self.ps[pb][:, jj * 128:(jj + 1) * 128], lhsT=h[:, kt, t0:t0 + 128], rhs=wq[b][:, kt, 2, :],
                                start=(kt == 0), stop=(kt == NFT - 1)))
                    sy.group("pe", fns, reads=[wqB[b]] + [hB[kt][jq] for kt in range(NFT)], writes=[self.psB[pb]])
                    sy.op("dve", lambda e, pb=pb, jq=jq: e.tensor_copy(out=vh[b][:, jq * 4:(jq + 1) * 4, :].rearrange("p a b -> p (a b)"), in_=self.ps[pb][:]),
                          reads=[self.psB[pb]], writes=[vB[b][jq]])
                    yield

            G = 2

            def core(hd, extra=None):
                b = hd % 2
                ob = hd % 4
                batches = []
                for qt in range(NTT):
                    nkt = 4 * qt + 4
                    for m in range(2):
                        for k0 in range(0, nkt, G):
                            batches.append((qt, m, k0, nkt))
                n = len(batches)
                po, pd = 6, 7
                deferred = []

                def qk(i):
                    qt, m, k0, nkt = batches[i]
                    r0, r1 = 64 * m, 64 * m + 64
                    sset = cnt["s"] % 2
                    cnt["s"] += 1
                    fns, tl = [], []
                    for g_ in range(G):
                        kt = k0 + g_
                        r = kt - 4 * qt
                        c0 = 128 * r if r > 0 else 0
                        pst = 2 + 2 * sset + g_
                        ip = cnt["p"] % 4
                        cnt["p"] += 1
                        tl.append((kt, r, c0, pst, ip))
                        fns.append(lambda e, kt=kt, c0=c0, pst=pst: e.matmul(
                            self.ps[pst][:, c0:TT], lhsT=kh[b][r0:r1, kt * 128:(kt + 1) * 128], rhs=qh[b][r0:r1, qt * TT + c0:(qt + 1) * TT],
                            start=True, stop=True))
                    sy.group("pe", fns, reads=[kB[b][k0 // 4], qB[b][qt]], writes=[self.psB[t[3]] for t in tl])
                    for (kt, r, c0, pst, ip) in tl:
                        sy.op("act", lambda e, c0=c0, pst=pst, ip=ip: e.activation(out=pT[ip][:, c0:TT], in_=self.ps[pst][:, c0:TT], func=AF.Exp),
                              reads=[self.psB[pst]], writes=[pTB[ip]])
                        if r >= 0:
                            sy.op("pool", lambda e, ip=ip, c0=c0: e.memset(pT[ip][64:128, c0:c0 + 64], 0.0), reads=[], writes=[pTB[ip]])
                    return tl

                def av(i, tl):
                    qt, m, k0, nkt = batches[i]
                    fns = []
                    for (kt, r, c0, pst, ip) in tl:
                        fns.append(lambda e, kt=kt, c0=c0, ip=ip: e.matmul(self.ps[po][:, c0:TT], lhsT=vh[b][:, kt, :], rhs=pT[ip][:, c0:TT],
                                                                         start=(kt == 0), stop=(kt == nkt - 1)))
                        fns.append(lambda e, kt=kt, c0=c0, ip=ip: e.matmul(self.ps[pd][:, c0:TT], lhsT=self.ones_bf[:], rhs=pT[ip][:, c0:TT],
                                                                         start=(kt == 0), stop=(kt == nkt - 1)))
                    sy.group("pe", fns, reads=[vB[b][k0 // 4], self.onesB] + [pTB[t[4]] for t in tl], writes=[self.psB[po], self.psB[pd]])
                    if k0 + G >= nkt:
                        epi(qt, m, po, pd)

                def epi(qt, m, po, pd):
                    k = m
                    sy.op("dve", lambda e: e.tensor_copy(out=pcp[k][:], in_=self.ps[po][:]), reads=[self.psB[po]], writes=[pcpB[k]])
                    sy.op("act", lambda e: e.activation(out=rr[k][:], in_=self.ps[pd][:], func=AF.Ln), reads=[self.psB[pd]], writes=[rrB[k]])
                    sy.op("act", lambda e: e.activation(out=rr[k][:], in_=rr[k][:], func=AF.Exp, scale=-1.0), reads=[rrB[k]], writes=[rrB[k]])
                    if m == 0:
                        sy.op("dve", lambda e: e.tensor_tensor(out=t1[:], in0=pcp[k][:], in1=rr[k][:], op=ALU.mult),
                              reads=[pcpB[k], rrB[k]], writes=[t1B])
                        return
                    sy.op("dve", lambda e: e.tensor_tensor(out=rr[k][:], in0=pcp[k][:], in1=rr[k][:], op=ALU.mult),
                          reads=[pcpB[k], rrB[k]], writes=[rrB[k]])
                    sy.op("dve", lambda e: e.scalar_tensor_tensor(out=of[:], in0=rr[k][:], scalar=nlam[:], in1=t1[:], op0=ALU.mult, op1=ALU.add),
                          reads=[rrB[k], t1B, nlamB], writes=[ofB])
                    deferred.append([2, lambda: subln_sq(qt)])
                    deferred.append([4, lambda: subln(qt)])

                def subln_sq(qt):
                    sy.op("dve", lambda e: e.tensor_tensor(out=osq[:], in0=of[:], in1=of[:], op=ALU.mult), reads=[ofB], writes=[osqB])

                def subln(qt):
                    pb = misc_bank()
                    sy.op("pe", lambda e: e.matmul(self.ps[pb][:], lhsT=self.ones_bf[:], rhs=osq[:], start=True, stop=True),
                          reads=[osqB, self.onesB], writes=[self.psB[pb]])
                    sy.op("act", lambda e: e.activation(out=rs[:], in_=self.ps[pb][:], func=AF.Ln, scale=1.0 / 128, bias=self.epsT[:]),
                          reads=[self.psB[pb], self.epsB], writes=[rsB])
                    sy.op("act", lambda e: e.activation(out=rs[:], in_=rs[:], func=AF.Exp, scale=-0.5), reads=[rsB], writes=[rsB])
                    sy.op("dve", lambda e: e.scalar_tensor_tensor(out=oth[ob][:, qt * TT:(qt + 1) * TT], in0=of[:], scalar=subs[:], in1=rs[:],
                                                                 op0=ALU.mult, op1=ALU.mult),
                          reads=[ofB, rsB, subsB], writes=[oB[ob][qt]])

                pend = {}
                for i in range(n + 1):
                    if i < n:
                        pend[i] = qk(i)
                    if i >= 1:
                        for d_ in list(deferred):
                            d_[0] -= 1
                            if d_[0] <= 0:
                                deferred.remove(d_)
                                d_[1]()
                        av(i - 1, pend.pop(i - 1))
                        if extra is not None:
                            next(extra, None)
                for d_ in deferred:
                    d_[1]()
                if extra is not None:
                    for _ in extra:
                        pass

            def outp(hd):
                hs = [hd - 1, hd]
                for tt in range(NTT):
                    for f2 in range(NFT // 2):
                        pbs = [0, 1]
                        fns = []
                        for q, pb in enumerate(pbs):
                            fo = 2 * f2 + q
                            for i_, h_ in enumerate(hs):
                                bb = h_ % 4
                                fns.append(lambda e, pb=pb, fo=fo, bb=bb, i_=i_: e.matmul(
                                    self.ps[pb][:], lhsT=woh[bb][:, fo * 128:(fo + 1) * 128], rhs=oth[bb][:, tt * TT:(tt + 1) * TT],
                                    start=(i_ == 0), stop=(i_ == 1)))
                        sy.group("pe", fns, reads=[woB[h_ % 4] for h_ in hs] + [oB[h_ % 4][tt] for h_ in hs], writes=[self.psB[0], self.psB[1]])
                        for q, pb in enumerate(pbs):
                            fo = 2 * f2 + q
                            sy.op("dve", lambda e, pb=pb, fo=fo: e.tensor_tensor(out=self.xt(fo, tt), in0=self.xt(fo, tt), in1=self.ps[pb][:], op=ALU.add),
                                  reads=[self.psB[pb], self.xB[fo][tt]], writes=[self.xB[fo][tt]])
                        yield

            def chain(*gens):
                for g_ in gens:
                    if g_ is not None:
                        yield from g_

            for _ in proj(0):
                pass
            load_w(2)
            for hd in range(NH):
                pj = proj(hd + 1) if hd + 1 < NH else None
                op_ = outp(hd - 1) if (hd % 2 == 0 and hd >= 2) else None
                core(hd, chain(pj, op_))
                if hd + 3 < NH:
                    load_w(hd + 3)
                if op_ is not None and hd + 2 < NH:
                    load_wo(hd + 2)
                    load_wo(hd + 3)
            for _ in outp(NH - 1):
                pass

    def s5(self, li, j):
        sy = self.sy
        NG = 64
        NP = 32
        NC = S // 8
        CT = 64
        TWO_PI = 2.0 * math.pi
        with self.scope() as es:
            U2 = self.sb(es, "U2", [128, NG, NC], BF16)
            U2B = [[Buf(f"U2_{gb}_{tt}") for tt in range(NTT)] for gb in range(8)]
            U2all = [U2B[gb][tt] for gb in range(8) for tt in range(NTT)]
            with self.scope() as es1:
                h = self.sb(es1, "hs", [128, NFT, S], BF16)
                hB = [[Buf(f"hs{ft}_{tt}") for tt in range(NTT)] for ft in range(NFT)]
                win = self.sb(es1, "win", [128, NFT, D], BF16)
                winB = Buf("win")
                sy.dma("pool", [lambda e, kt=kt: e.dma_start(out=win[:, kt, :], in_=self.s5w[j, 0, kt * 128:(kt + 1) * 128, :]) for kt in range(NFT)], writes=[winB])
                self.rmsnorm(es1, li, h, hB, inline=True)
                utm = [self.sb(es1, "utm", [128, 8, D], BF16) for _ in range(2)]
                utmB = [[Buf(f"utm{k}_{s_}") for s_ in range(8)] for k in range(2)]
                cnt = 0
                ev = 0
                for cb in range(2):
                    k = cb % 2
                    for s_ in range(8):
                        for fh in range(2):
                            pb = cnt % 4
                            cnt += 1
                            t0 = cb * 1024 + s_
                            sy.group("pe", [lambda e, kt=kt, pb=pb, fh=fh, t0=t0: e.matmul(
                                self.ps[pb][:], lhsT=h[:, kt, t0:t0 + 1017:8], rhs=win[:, kt, fh * 512:(fh + 1) * 512],
                                start=(kt == 0), stop=(kt == NFT - 1)) for kt in range(NFT)],
                                reads=[winB] + [hB[kt][2 * cb] for kt in range(NFT)] + [hB[kt][2 * cb + 1] for kt in range(NFT)], writes=[self.psB[pb]])
                            sy.op("act", lambda e, pb=pb, k=k, s_=s_, fh=fh: e.activation(
                                out=utm[k][:].rearrange("p s (g h) -> p (s g h)", h=16).rearrange("p (g s h) -> p g s h", s=8, h=16)[:, fh * 32:(fh + 1) * 32, s_, :],
                                in_=self.ps[pb][:].rearrange("p (g h) -> p g h", h=16), func=AF.Identity),
                                  reads=[self.psB[pb]], writes=[utmB[k][s_]])
                for cb in range(2):
                    k = cb % 2
                    for gq in range(16):
                        pb = 4 + gq % 2
                        psb = self.ps[pb][:].bitcast(BF16)
                        sy.group("pe", [lambda e, q=q, psb=psb, gq=gq, k=k: e.transpose(
                            psb[:, q * 128:(q + 1) * 128], utm[k][:].rearrange("p s f -> p (s f)")[:, (4 * gq + q) * 128:(4 * gq + q + 1) * 128], self.idb[:]) for q in range(4)],
                            reads=utmB[k] + [self.idbB], writes=[self.psB[pb]])
                        eng = "act" if ev % 2 == 0 else "dve"
                        ev += 1
                        dst = U2[:, 4 * gq:4 * gq + 4, cb * 128:(cb + 1) * 128]
                        src = psb[:, 0:512].rearrange("p (g c) -> p g c", g=4)
                        if eng == "act":
                            sy.op("act", lambda e, dst=dst, src=src: e.activation(out=dst, in_=src, func=AF.Identity),
                                  reads=[self.psB[pb]], writes=[U2B[gq // 2][2 * cb], U2B[gq // 2][2 * cb + 1]])
                        else:
                            sy.op("dve", lambda e, dst=dst, src=src: e.tensor_copy(out=dst, in_=src),
                                  reads=[self.psB[pb]], writes=[U2B[gq // 2][2 * cb], U2B[gq // 2][2 * cb + 1]])
            import os
            dbg = int(os.environ.get("S5DBG", "9"))
            if dbg < 2:
                return
            with self.scope() as es2:
                Tm = self.sb(es2, "Tm", [128, NG, 128], BF16)
                TmB = [Buf(f"Tm{i}") for i in range(16)]
                Btm = self.sb(es2, "Btm", [128, NP, 2, 128], BF16)
                BtB = [Buf(f"Btm{i}") for i in range(16)]
                CtR = self.sb(es2, "CtR", [128, NP, 128], BF16)
                CtI = self.sb(es2, "CtI", [128, NP, 128], BF16)
                CtB = [Buf(f"Ct{i}") for i in range(8)]
                PP1 = self.sb(es2, "PP1", [128, 8, 2, NP])
                PP2 = self.sb(es2, "PP2", [128, 8, 2, NP])
                AB = Buf("A12")
                A1, A2 = PP1[:, 0, :, :], PP2[:, 0, :, :]
                sc = self.s5c[j]
                sy.dma("sp", [lambda e, h_=h_: e.dma_start(out=Btm[:, h_ * 8:(h_ + 1) * 8, :, :].rearrange("p j r c -> p (j r c)"), in_=sc["Bt"][:, h_ * 2048:(h_ + 1) * 2048]) for h_ in range(4)], writes=BtB)
                sy.dma("sp", [lambda e: e.dma_start(out=PP1[:].rearrange("p l a b -> p (l a b)"), in_=sc["A"][:, 0:512]),
                              lambda e: e.dma_start(out=PP2[:].rearrange("p l a b -> p (l a b)"), in_=sc["A"][:, 512:1024])], writes=[AB])

                sy.dma("sp", [lambda e: e.dma_start(out=CtR[:].rearrange("p j c -> p (j c)"), in_=sc["CtR"]),
                              lambda e: e.dma_start(out=CtI[:].rearrange("p j c -> p (j c)"), in_=sc["CtI"])], writes=CtB)
                sy.dma("sp", [lambda e, h_=h_: e.dma_start(out=Tm[:, h_ * 16:(h_ + 1) * 16, :].rearrange("p g c -> p (g c)"), in_=sc["Tm"][:, h_ * 2048:(h_ + 1) * 2048]) for h_ in range(4)], writes=TmB)
                if dbg < 3:
                    return
                St = [self.sb(es2, "St", [128, CT + 1, 2, NP]) for _ in range(2)]
                StB = [Buf("St0"), Buf("St1")]
                Sb = [self.sb(es2, "Sb", [128, CT, 2, NP], BF16) for _ in range(2)]
                SbB = [Buf("Sb0"), Buf("Sb1")]
                tA = self.sb(es2, "tA", [128, 8, 2, NP])
                tB_ = self.sb(es2, "tB", [128, 8, 2, NP])
                tAB, tBB = Buf("tA"), Buf("tB")
                tC = [tA[:, 0:7, :, :], tA[:, 0:7, :, :]]
                tD = [tB_[:, 0:7, :, :], tB_[:, 0:7, :, :]]
                tCB, tDB = [tAB, tAB], [tBB, tBB]
                XlB = [[Buf(f"Xl{k_}_{l_}") for l_ in range(8)] for k_ in range(2)]
                cnt = {"b": 0, "y": 0, "g": 0}

                def stage_a(tt):
                    c0 = tt * CT
                    k = tt % 2
                    for pq in range(NP // 4):
                        pb = cnt["b"] % 2
                        cnt["b"] += 1
                        fns = []
                        for jl in range(4):
                            jp = pq * 4 + jl
                            for e_ in range(2):
                                for ri in range(2):
                                    col = (jl * 2 + ri) * CT
                                    fns.append(lambda e, jp=jp, e_=e_, ri=ri, col=col, pb=pb: e.matmul(
                                        self.ps[pb][64 * e_:64 * e_ + 64, col:col + CT], lhsT=Btm[:, jp, ri, 64 * e_:64 * e_ + 64],
                                        rhs=U2[:, 2 * jp + e_, c0:c0 + CT], start=True, stop=True))
                        sy.group("pe", fns, reads=[BtB[pq * 2], BtB[pq * 2 + 1], U2B[pq][tt]], writes=[self.psB[pb]])
                        sy.op("act", lambda e, pb=pb, pq=pq: e.activation(
                            out=St[k][:, 1:CT + 1, :, pq * 4:pq * 4 + 4].rearrange("p c r j -> p j r c"),
                            in_=self.ps[pb][:].rearrange("p (j r c) -> p j r c", j=4, r=2), func=AF.Identity),
                            reads=[self.psB[pb]], writes=XlB[k])

                def stage_b(tt):
                    k = tt % 2
                    Sk = St[k]
                    Sv = Sk[:, 1:CT + 1, :, :].rearrange("p (b l) r j -> p b l r j", l=8)

                    def swp(ap_slot1_r1, nb):
                        t_ = ap_slot1_r1
                        if nb == 1:
                            return bass.AP(tensor=t_.tensor, offset=t_.offset, ap=[list(t_.ap[0]), [-NP, 2], [1, NP]])
                        return bass.AP(tensor=t_.tensor, offset=t_.offset, ap=[list(t_.ap[0]), [8 * 2 * NP, nb], [-NP, 2], [1, NP]])

                    def step(prev, prev_sw, cur, c1, c2, ta_, tb_, rB, wB):
                        sy.op("dve", lambda e: e.tensor_tensor(out=tb_, in0=prev_sw, in1=c2, op=ALU.mult), reads=rB + [AB], writes=[tBB])
                        sy.op("dve", lambda e: e.tensor_tensor(out=ta_, in0=prev, in1=c1, op=ALU.mult), reads=rB + [AB], writes=[tAB])
                        sy.op("dve", lambda e: e.tensor_tensor(out=tb_, in0=tb_, in1=cur, op=ALU.add), reads=[tBB] + wB, writes=[tBB])
                        sy.op("dve", lambda e: e.tensor_tensor(out=cur, in0=ta_, in1=tb_, op=ALU.add), reads=[tAB, tBB], writes=wB)

                    c0B = StB[k]
                    if tt == 0:
                        sy.op("dve", lambda e: e.memset(Sk[:, 0, :, :], 0.0), writes=[c0B])
                    else:
                        sy.op("dve", lambda e: e.tensor_copy(out=Sk[:, 0, :, :], in_=St[1 - k][:, CT, :, :]), reads=[XlB[1 - k][7]], writes=[c0B])
                    step(Sk[:, 0, :, :], swp(Sk[:, 0, 1, :], 1), Sk[:, 1, :, :], A1, A2, tA[:, 0, :, :], tB_[:, 0, :, :], [c0B], [XlB[k][0]])
                    A1b = A1.unsqueeze(1).broadcast_to([128, 8, 2, NP])
                    A2b = A2.unsqueeze(1).broadcast_to([128, 8, 2, NP])
                    for l in range(1, 8):
                        step(Sv[:, :, l - 1, :, :], swp(Sv[:, 0, l - 1, 1, :], 8), Sv[:, :, l, :, :], A1b, A2b, tA[:], tB_[:], [XlB[k][l - 1]], [XlB[k][l]])
                    for blk in range(1, 8):
                        step(Sv[:, blk - 1, 7, :, :], swp(Sv[:, blk - 1, 7, 1, :], 1), Sv[:, blk, 7, :, :], PP1[:, 7, :, :], PP2[:, 7, :, :],
                             tA[:, 0, :, :], tB_[:, 0, :, :], [XlB[k][7]], [XlB[k][7]])
                    Cv = Sv[:, 0:7, 7, :, :]
                    Cs = swp(Sv[:, 0, 7, 1, :], 7)
                    for l in range(7):
                        q = l % 2
                        p1 = PP1[:, l, :, :].unsqueeze(1).broadcast_to([128, 7, 2, NP])
                        p2 = PP2[:, l, :, :].unsqueeze(1).broadcast_to([128, 7, 2, NP])
                        cur = Sv[:, 1:8, l, :, :]
                        sy.op("dve", lambda e, q=q, p2=p2: e.tensor_tensor(out=tD[q], in0=Cs, in1=p2, op=ALU.mult), reads=[XlB[k][7], AB], writes=[tDB[q]])
                        sy.op("dve", lambda e, q=q, p1=p1: e.tensor_tensor(out=tC[q], in0=Cv, in1=p1, op=ALU.mult), reads=[XlB[k][7], AB], writes=[tCB[q]])
                        sy.op("dve", lambda e, q=q, cur=cur: e.tensor_tensor(out=tD[q], in0=tD[q], in1=cur, op=ALU.add), reads=[tDB[q], XlB[k][l]], writes=[tDB[q]])
                        sy.op("dve", lambda e, q=q, cur=cur: e.tensor_tensor(out=cur, in0=tC[q], in1=tD[q], op=ALU.add), reads=[tCB[q], tDB[q]], writes=[XlB[k][l]])

                def stage_c(tt):
                    c0 = tt * CT
                    k = tt % 2
                    sy.op("act", lambda e: e.activation(out=Sb[k][:], in_=St[k][:, 0:CT, :, :], func=AF.Identity), reads=[StB[k]] + XlB[k], writes=[SbB[k]])
                    for gb in range(8):
                        pb = 2 + cnt["y"] % 2
                        cnt["y"] += 1
                        fns = []
                        for gl in range(8):
                            g = gb * 8 + gl
                            jp, e_ = g // 2, g % 2
                            out = self.ps[pb][:, gl * CT:(gl + 1) * CT]
                            fns.append(lambda e, g=g, out=out: e.matmul(out, lhsT=Tm[:, g, :], rhs=U2[:, g, c0:c0 + CT], start=True, stop=False))
                            fns.append(lambda e, jp=jp, e_=e_, out=out: e.matmul(out, lhsT=CtR[64 * e_:64 * e_ + 64, jp, :], rhs=Sb[k][64 * e_:64 * e_ + 64, :, 0, jp],
                                                                              start=False, stop=False))
                            fns.append(lambda e, jp=jp, e_=e_, out=out: e.matmul(out, lhsT=CtI[64 * e_:64 * e_ + 64, jp, :], rhs=Sb[k][64 * e_:64 * e_ + 64, :, 1, jp],
                                                                              start=False, stop=True))
                        sy.group("pe", fns, reads=[TmB[gb * 2], TmB[gb * 2 + 1], CtB[gb], U2B[gb][tt], SbB[k]], writes=[self.psB[pb]])
                        ps = self.ps[pb]
                        sy.op("act", lambda e, ps=ps, gb=gb: e.activation(
                            out=U2[:, gb * 8:(gb + 1) * 8, c0:c0 + CT], in_=ps[:].rearrange("p (g c) -> p g c", g=8), func=AF.Gelu_apprx_tanh),
                            reads=[self.psB[pb]], writes=[U2B[gb][tt]])

                stage_a(0)
                stage_b(0)
                for tt in range(NTT):
                    if tt + 1 < NTT:
                        stage_a(tt + 1)
                        stage_b(tt + 1)
                    stage_c(tt)
            if dbg < 5:
                return
            with self.scope() as es3:
                zfm = self.sb(es3, "zfm", [128, NFT, 2, 8, 128], BF16)
                zfB = [[Buf(f"zfm{fo}_{cb}") for cb in range(2)] for fo in range(NFT)]
                ztm = [self.sb(es3, "ztm", [128, D], BF16) for _ in range(2)]
                ztmB = [Buf("ztm0"), Buf("ztm1")]
                czc = {"n": 0}

                def tr(cb, fo):
                    cz = czc["n"]
                    czc["n"] += 1
                    k = cz % 2
                    pa = 4 + cz % 2
                    pbk = 6 + cz % 2
                    psa = self.ps[pa][:].bitcast(BF16)
                    psb = self.ps[pbk][:].bitcast(BF16)
                    sy.group("pe", [lambda e, g8=g8: e.transpose(
                        psa[:, g8 * 128:(g8 + 1) * 128], U2[:, fo * 8 + g8, cb * 128:(cb + 1) * 128], self.idb[:]) for g8 in range(8)],
                        reads=[U2B[fo][2 * cb], U2B[fo][2 * cb + 1], self.idbB], writes=[self.psB[pa]])
                    sy.op("act", lambda e: e.activation(out=ztm[k][:].rearrange("p (t g h) -> p g t h", g=8, t=8),
                                                       in_=psa.rearrange("p (g t h) -> p g t h", g=8, t=8), func=AF.Identity),
                          reads=[self.psB[pa]], writes=[ztmB[k]])
                    sy.group("pe", [lambda e, t=t: e.transpose(psb[:, t * 128:(t + 1) * 128], ztm[k][:, t * 128:(t + 1) * 128], self.idb[:]) for t in range(8)],
                             reads=[ztmB[k], self.idbB], writes=[self.psB[pbk]])
                    sy.op("dve", lambda e: e.tensor_copy(out=zfm[:, fo, cb, :, :].rearrange("p t c -> p (t c)"), in_=psb),
                          reads=[self.psB[pbk]], writes=[zfB[fo][cb]])

                for fo in range(NFT):
                    tr(0, fo)
                wg = self.sb(es3, "wg", [128, NFT, D], BF16)
                wo_ = self.sb(es3, "wo5", [128, NFT, D], BF16)
                wgB, woB = Buf("wg"), Buf("wo5")
                sy.dma("pool", [lambda e, kt=kt: e.dma_start(out=wg[:, kt, :], in_=self.s5w[j, 1, kt * 128:(kt + 1) * 128, :]) for kt in range(NFT)], writes=[wgB])
                sy.dma("pool", [lambda e, kt=kt: e.dma_start(out=wo_[:, kt, :], in_=self.s5w[j, 2, kt * 128:(kt + 1) * 128, :]) for kt in range(NFT)], writes=[woB])
                z2 = [self.sb(es3, "z2", [128, NFT, TT], BF16) for _ in range(2)]
                z2B = [[Buf(f"z2_{b}_{fo}") for fo in range(NFT)] for b in range(2)]
                sg = [self.sb(es3, "sg", [128, TT]) for _ in range(2)]
                sgB = [Buf("sg0"), Buf("sg1")]
                cnt = {"a": 0, "b": 0, "s": 0}

                tiles = [(0, 0), (0, 1), (1, 0), (1, 1)]

                def zcols(kt, ti):
                    cb, th = tiles[ti]
                    return zfm[:, kt, cb, 4 * th:4 * th + 4, :].rearrange("p t c -> p (t c)")

                def gate(ti):
                    b = ti % 2
                    cb, th = tiles[ti]
                    for fo in range(NFT):
                        pb = cnt["a"] % 2
                        cnt["a"] += 1
                        k = cnt["s"] % 2
                        cnt["s"] += 1
                        sy.group("pe", [lambda e, kt=kt, pb=pb, fo=fo: e.matmul(
                            self.ps[pb][:], lhsT=wg[:, kt, fo * 128:(fo + 1) * 128], rhs=zcols(kt, ti),
                            start=(kt == 0), stop=(kt == NFT - 1)) for kt in range(NFT)],
                            reads=[wgB] + [zfB[kt][cb] for kt in range(NFT)], writes=[self.psB[pb]])
                        sy.op("act", lambda e, pb=pb, k=k: e.activation(out=sg[k][:], in_=self.ps[pb][:], func=AF.Sigmoid), reads=[self.psB[pb]], writes=[sgB[k]])
                        sy.op("pool", lambda e, k=k, fo=fo: e.tensor_tensor(out=z2[b][:, fo, :], in0=zcols(fo, ti), in1=sg[k][:], op=ALU.mult),
                              reads=[sgB[k], zfB[fo][cb]], writes=[z2B[b][fo]])

                def outp(ti):
                    b = ti % 2
                    cb, th = tiles[ti]
                    xv = self.x[:].rearrange("p f (c s) -> p f s c", s=8)
                    for fo in range(NFT):
                        pb = 2 + cnt["b"] % 2
                        cnt["b"] += 1
                        sy.group("pe", [lambda e, kt=kt, pb=pb, fo=fo: e.matmul(
                            self.ps[pb][:], lhsT=wo_[:, kt, fo * 128:(fo + 1) * 128], rhs=z2[b][:, kt, :], start=(kt == 0), stop=(kt == NFT - 1))
                            for kt in range(NFT)], reads=[woB] + z2B[b], writes=[self.psB[pb]])
                        xs = xv[:, fo, 4 * th:4 * th + 4, cb * 128:(cb + 1) * 128]
                        sy.op("dve", lambda e, pb=pb, xs=xs: e.tensor_tensor(out=xs, in0=xs, in1=self.ps[pb][:].rearrange("p (t c) -> p t c", t=4), op=ALU.add),
                              reads=[self.psB[pb]] + self.xB[fo], writes=self.xB[fo])

                gate(0)
                for fo in range(0, 4):
                    tr(1, fo)
                gate(1)
                outp(0)
                for fo in range(4, 8):
                    tr(1, fo)
                gate(2)
                outp(1)
                gate(3)
                outp(2)
                outp(3)

    def s5_precompute(self, j):
        nc, sy = self.nc, self.sy
        NG, NP = 64, 32
        sc = {
            "Tm": nc.dram_tensor(f"s5c_Tm{j}", [128, NG * 128], BF16, kind="Internal").ap(),
            "Bt": nc.dram_tensor(f"s5c_Bt{j}", [128, NP * 2 * 128], BF16, kind="Internal").ap(),
            "CtR": nc.dram_tensor(f"s5c_CtR{j}", [128, NP * 128], BF16, kind="Internal").ap(),
            "CtI": nc.dram_tensor(f"s5c_CtI{j}", [128, NP * 128], BF16, kind="Internal").ap(),
            "A": nc.dram_tensor(f"s5c_A{j}", [128, 1024], F32, kind="Internal").ap(),
        }
        self.s5c[j] = sc
        with self.scope() as es2:
            Tm = self.sb(es2, "Tm", [128, NG, 128], BF16)
            TmB = [Buf(f"Tm{i}") for i in range(16)]
            Btm = self.sb(es2, "Btm", [128, NP, 2, 128], BF16)
            BtB = [Buf(f"Btm{i}") for i in range(16)]
            CtR = self.sb(es2, "CtR", [128, NP, 128], BF16)
            CtI = self.sb(es2, "CtI", [128, NP, 128], BF16)
            CtB = [Buf(f"Ct{i}") for i in range(8)]
            A1 = self.sb(es2, "A1", [128, 8, 2, NP])
            A2 = self.sb(es2, "A2", [128, 8, 2, NP])
            AB = Buf("A12")
            self.s5_prep(es2, j, Tm, TmB, Btm, BtB, CtR, CtI, CtB, A1, A2, AB)
            sy.dma("sp", [lambda e, h_=h_: e.dma_start(out=sc["Tm"][:, h_ * 2048:(h_ + 1) * 2048], in_=Tm[:, h_ * 16:(h_ + 1) * 16, :].rearrange("p g c -> p (g c)")) for h_ in range(4)], reads=TmB)
            sy.dma("sp", [lambda e, h_=h_: e.dma_start(out=sc["Bt"][:, h_ * 2048:(h_ + 1) * 2048], in_=Btm[:, h_ * 8:(h_ + 1) * 8, :, :].rearrange("p j r c -> p (j r c)")) for h_ in range(4)], reads=BtB)
            sy.dma("sp", [lambda e: e.dma_start(out=sc["CtR"], in_=CtR[:].rearrange("p j c -> p (j c)")),
                          lambda e: e.dma_start(out=sc["CtI"], in_=CtI[:].rearrange("p j c -> p (j c)"))], reads=CtB)
            sy.dma("sp", [lambda e: e.dma_start(out=sc["A"][:, 0:512], in_=A1[:].rearrange("p l a b -> p (l a b)")),
                          lambda e: e.dma_start(out=sc["A"][:, 512:1024], in_=A2[:].rearrange("p l a b -> p (l a b)"))], reads=[AB])

    def s5_prep(self, es_out, j, Tm, TmB, Btm, BtB, CtR, CtI, CtB, A1, A2, AB):
        sy = self.sy
        NP = 32
        TWO_PI = 2.0 * math.pi
        with self.scope() as es:
            prm = self.sb(es, "prm", [128, S5NC])
            prmB = Buf("prm")
            sy.dma("sp", [lambda e: e.dma_start(out=prm[:], in_=self.s5p[j])], writes=[prmB])
            lre, lim, lst = prm[:, 0:32], prm[:, 32:64], prm[:, 64:96]
            bre = prm[:, 96:608].rearrange("p (j h) -> p j h", h=16)
            bim = prm[:, 608:1120].rearrange("p (j h) -> p j h", h=16)
            cre = prm[:, 1120:1632].rearrange("p (j h) -> p j h", h=16)
            cim = prm[:, 1632:2144].rearrange("p (j h) -> p j h", h=16)
            dvec = prm[:, 2144:2208]
            names = ["lr", "dt", "lrdt", "lidt", "nr", "den", "fr", "fi", "w1", "w2"]
            sm = {n: self.sb(es, "s5" + n, [128, NP]) for n in names}
            smB = {n: Buf("s5" + n) for n in names}

            def tt_(eng, out, oB, a, aB, b, bB, op):
                sy.op(eng, lambda e: e.tensor_tensor(out=out, in0=a, in1=b, op=op), reads=[aB, bB], writes=[oB])

            def V(n):
                return sm[n][:]
            sy.op("dve", lambda e: e.tensor_scalar(out=V("lr"), in0=lre, scalar1=-1e-4, scalar2=None, op0=ALU.min), reads=[prmB], writes=[smB["lr"]])
            sy.op("act", lambda e: e.activation(out=V("dt"), in_=lst, func=AF.Exp), reads=[prmB], writes=[smB["dt"]])
            tt_("dve", V("lrdt"), smB["lrdt"], V("lr"), smB["lr"], V("dt"), smB["dt"], ALU.mult)
            tt_("dve", V("lidt"), smB["lidt"], lim, prmB, V("dt"), smB["dt"], ALU.mult)
            NK = 24
            kv = self.sb(es, "kv", [128, NK])
            kvB = Buf("kv")
            sy.op("pool", lambda e: e.iota(kv[:, 0:16], pattern=[[1, 16]], base=-7, channel_multiplier=0, allow_small_or_imprecise_dtypes=True), writes=[kvB])
            sy.op("pool", lambda e: e.iota(kv[:, 16:NK], pattern=[[8, NK - 16]], base=16, channel_multiplier=0, allow_small_or_imprecise_dtypes=True), writes=[kvB])
            big = {n: self.sb(es, "s5" + n, [128, NK, NP]) for n in ["mag", "ang", "nf", "AR", "AI"]}
            bigB = {n: Buf("s5" + n) for n in big}
            ni = self.sb(es, "s5ni", [128, NK, NP], I32)
            niB = Buf("s5ni")
            kvb = kv[:].unsqueeze(2).broadcast_to([128, NK, NP])
            sy.op("dve", lambda e: e.tensor_tensor(out=big["mag"][:], in0=kvb, in1=V("lrdt").unsqueeze(1).broadcast_to([128, NK, NP]), op=ALU.mult),
                  reads=[kvB, smB["lrdt"]], writes=[bigB["mag"]])
            sy.op("act", lambda e: e.activation(out=big["mag"][:], in_=big["mag"][:], func=AF.Exp), reads=[bigB["mag"]], writes=[bigB["mag"]])
            sy.op("dve", lambda e: e.tensor_tensor(out=big["ang"][:], in0=kvb, in1=V("lidt").unsqueeze(1).broadcast_to([128, NK, NP]), op=ALU.mult),
                  reads=[kvB, smB["lidt"]], writes=[bigB["ang"]])

            def sin_of(dst, dstB, shift):
                nf, nfB, ang, angB = big["nf"], bigB["nf"], big["ang"], bigB["ang"]
                sy.op("dve", lambda e: e.tensor_scalar(out=nf[:], in0=ang[:], scalar1=shift, scalar2=1.0 / TWO_PI, op0=ALU.add, op1=ALU.mult), reads=[angB], writes=[nfB])
                sy.op("dve", lambda e: e.tensor_copy(out=ni[:], in_=nf[:]), reads=[nfB], writes=[niB])
                sy.op("dve", lambda e: e.tensor_copy(out=nf[:], in_=ni[:]), reads=[niB], writes=[nfB])
                sy.op("dve", lambda e: e.scalar_tensor_tensor(out=nf[:], in0=nf[:], scalar=-TWO_PI, in1=ang[:], op0=ALU.mult, op1=ALU.add), reads=[nfB, angB], writes=[nfB])
                sy.op("dve", lambda e: e.tensor_scalar(out=nf[:], in0=nf[:], scalar1=shift, scalar2=math.pi, op0=ALU.add, op1=ALU.min), reads=[nfB], writes=[nfB])
                sy.op("dve", lambda e: e.tensor_scalar(out=nf[:], in0=nf[:], scalar1=-math.pi, scalar2=None, op0=ALU.max), reads=[nfB], writes=[nfB])
                sy.op("act", lambda e: e.activation(out=dst[:], in_=nf[:], func=AF.Sin), reads=[nfB], writes=[dstB])
            sin_of(big["AI"], bigB["AI"], 0.0)
            sin_of(big["AR"], bigB["AR"], math.pi / 2)
            for n in ("AI", "AR"):
                sy.op("dve", lambda e, n=n: e.tensor_tensor(out=big[n][:], in0=big[n][:], in1=big["mag"][:], op=ALU.mult), reads=[bigB[n], bigB["mag"]], writes=[bigB[n]])
            AR, AI, ARB, AIB = big["AR"], big["AI"], bigB["AR"], bigB["AI"]
            import os
            pd_ = int(os.environ.get("PREPDBG", "9"))
            if pd_ < 2:
                return
            for r_ in range(2):
                sy.op("dve", lambda e, r_=r_: e.tensor_copy(out=A1[:, :, r_, :], in_=AR[:, 15:23, :]), reads=[ARB], writes=[AB])
            sy.op("dve", lambda e: e.tensor_copy(out=A2[:, :, 1, :], in_=AI[:, 15:23, :]), reads=[AIB], writes=[AB])
            sy.op("dve", lambda e: e.tensor_scalar(out=A2[:, :, 0, :], in0=AI[:, 15:23, :], scalar1=-1.0, scalar2=None, op0=ALU.mult), reads=[AIB], writes=[AB])
            a1r, a1i = AR[:, 8, :], AI[:, 8, :]
            sy.op("dve", lambda e: e.tensor_scalar(out=V("nr"), in0=a1r, scalar1=-1.0, scalar2=None, op0=ALU.add), reads=[ARB], writes=[smB["nr"]])
            tt_("dve", V("den"), smB["den"], V("lr"), smB["lr"], V("lr"), smB["lr"], ALU.mult)
            tt_("dve", V("w1"), smB["w1"], lim, prmB, lim, prmB, ALU.mult)
            tt_("dve", V("den"), smB["den"], V("den"), smB["den"], V("w1"), smB["w1"], ALU.add)
            sy.op("dve", lambda e: e.reciprocal(out=V("den"), in_=V("den")), reads=[smB["den"]], writes=[smB["den"]])
            tt_("dve", V("w1"), smB["w1"], V("nr"), smB["nr"], V("lr"), smB["lr"], ALU.mult)
            tt_("dve", V("w2"), smB["w2"], a1i, AIB, lim, prmB, ALU.mult)
            tt_("dve", V("fr"), smB["fr"], V("w1"), smB["w1"], V("w2"), smB["w2"], ALU.add)
            tt_("dve", V("fr"), smB["fr"], V("fr"), smB["fr"], V("den"), smB["den"], ALU.mult)
            tt_("dve", V("w1"), smB["w1"], a1i, AIB, V("lr"), smB["lr"], ALU.mult)
            tt_("dve", V("w2"), smB["w2"], V("nr"), smB["nr"], lim, prmB, ALU.mult)
            tt_("dve", V("fi"), smB["fi"], V("w1"), smB["w1"], V("w2"), smB["w2"], ALU.subtract)
            tt_("dve", V("fi"), smB["fi"], V("fi"), smB["fi"], V("den"), smB["den"], ALU.mult)
            BBr = self.sb(es, "BBr", [128, NP, 16])
            BBi = self.sb(es, "BBi", [128, NP, 16])
            BBrB, BBiB = Buf("BBr"), Buf("BBi")
            IZ = self.sb(es, "IZ", [128, 2, 256])
            IZB = Buf("IZ")
            w3a_t = IZ[:].rearrange("p a b -> p (a b)").rearrange("p (j h) -> p j h", h=16)

            class _V:
                def __init__(self, ap): self.ap_ = ap
                def __getitem__(self, k): return self.ap_
            w3a = _V(w3a_t)
            w3b = self.sb(es, "w3b", [128, NP, 16])
            w3aB, w3bB = IZB, Buf("w3b")
            frb = V("fr").unsqueeze(2).broadcast_to([128, NP, 16])
            fib = V("fi").unsqueeze(2).broadcast_to([128, NP, 16])
            sy.op("dve", lambda e: e.tensor_tensor(out=w3a[:], in0=bre, in1=frb, op=ALU.mult), reads=[prmB, smB["fr"]], writes=[w3aB])
            sy.op("dve", lambda e: e.tensor_tensor(out=w3b[:], in0=bim, in1=fib, op=ALU.mult), reads=[prmB, smB["fi"]], writes=[w3bB])
            tt_("dve", BBr[:], BBrB, w3a[:], w3aB, w3b[:], w3bB, ALU.subtract)
            sy.op("dve", lambda e: e.tensor_tensor(out=w3a[:], in0=bim, in1=frb, op=ALU.mult), reads=[prmB, smB["fr"]], writes=[w3aB])
            sy.op("dve", lambda e: e.tensor_tensor(out=w3b[:], in0=bre, in1=fib, op=ALU.mult), reads=[prmB, smB["fi"]], writes=[w3bB])
            tt_("dve", BBi[:], BBiB, w3a[:], w3aB, w3b[:], w3bB, ALU.add)
            mask = self.sb(es, "msk", [128, 4, 8, 16])
            maskB = Buf("msk")
            sy.op("pool", lambda e: e.memset(mask[:], 1.0), writes=[maskB])
            sy.op("pool", lambda e: e.affine_select(out=mask[:], in_=mask[:], pattern=[[0, 4], [16, 8], [0, 16]], compare_op=ALU.is_ge, fill=0.0,
                                                   base=15, channel_multiplier=-1), reads=[maskB], writes=[maskB])
            ident = self.sb(es, "idn", [128, 128])
            identB = Buf("idn")
            sy.op("pool", lambda e: e.memset(ident[:], 0.0), writes=[identB])
            sy.op("pool", lambda e: e.affine_select(out=ident[:], in_=ident[:], pattern=[[-1, 128]], compare_op=ALU.not_equal, fill=1.0, base=0, channel_multiplier=1),
                  reads=[identB], writes=[identB])
            if pd_ < 3:
                return
            PBN = 4
            shp = [128, PBN, 8, 16]
            shp9 = [128, PBN, 9, 16]
            tmp_sets = [{n: self.sb(es, "s5t" + n, shp9 if n in ("CpR", "CpI") else shp) for n in ["BnR", "BnI", "BtR", "BtI", "CpR", "CpI"]}
                        for _ in range(2)]
            tmpB_sets = [{n: Buf("s5t" + n) for n in tmp_sets[0]} for _ in range(2)]
            scr = {n: self.sb(es, "s5t" + n, shp9 if n in ("x1", "x2") else shp) for n in ["x1", "x2", "y1", "y2"]}
            scrB = {n: Buf("s5t" + n) for n in scr}
            for d_, dB_ in zip(tmp_sets, tmpB_sets):
                d_.update(scr)
                dB_.update(scrB)
            tmp, tmpB = dict(tmp_sets[0]), dict(tmpB_sets[0])
            CpZ_sets = []
            for _ in range(2):
                zr = self.sb(es, "CpZR", [128, PBN, 2, 128])
                zi = self.sb(es, "CpZI", [128, PBN, 2, 128])
                zb = Buf("CpZ")
                sy.op("pool", lambda e, zr=zr: e.memset(zr[:], 0.0), writes=[zb])
                sy.op("pool", lambda e, zi=zi: e.memset(zi[:], 0.0), writes=[zb])
                CpZ_sets.append((zr, zi, zb))
            cnt = 0

            def cprod(eng, x1, x2, k0, Zr, Zi, ZB, p0, outR, outI, negI, outRB, outIB, npow=8):
                kk0, sgn = k0
                def pw(A):
                    t = A[:, kk0, p0:p0 + PBN]
                    return bass.AP(tensor=t.tensor, offset=t.offset, ap=[list(t.ap[0]), [1, PBN], [sgn * NP, npow], [0, 16]])
                def zz(Z):
                    t = Z[:, p0:p0 + PBN, :]
                    return bass.AP(tensor=t.tensor, offset=t.offset, ap=[list(t.ap[0]), [16, PBN], [0, npow], [1, 16]])
                X1, X2 = tmp[x1][:, :, 0:npow, :], tmp[x2][:, :, 0:npow, :]
                sy.op(eng, lambda e: e.tensor_tensor(out=X1[:], in0=pw(AR), in1=zz(Zr), op=ALU.mult), reads=[ARB] + ZB, writes=[tmpB[x1]])
                sy.op(eng, lambda e: e.tensor_tensor(out=X2[:], in0=pw(AI), in1=zz(Zi), op=ALU.mult), reads=[AIB] + ZB, writes=[tmpB[x2]])
                sy.op(eng, lambda e: e.tensor_tensor(out=outR, in0=X1[:], in1=X2[:], op=ALU.subtract), reads=[tmpB[x1], tmpB[x2]], writes=[outRB])
                sy.op(eng, lambda e: e.tensor_tensor(out=X1[:], in0=pw(AR), in1=zz(Zi), op=ALU.mult), reads=[ARB] + ZB, writes=[tmpB[x1]])
                sy.op(eng, lambda e: e.tensor_tensor(out=X2[:], in0=pw(AI), in1=zz(Zr), op=ALU.mult), reads=[AIB] + ZB, writes=[tmpB[x2]])
                if negI:
                    sy.op(eng, lambda e: e.tensor_scalar(out=X1[:], in0=X1[:], scalar1=-1.0, scalar2=None, op0=ALU.mult), reads=[tmpB[x1]], writes=[tmpB[x1]])
                    sy.op(eng, lambda e: e.tensor_tensor(out=outI, in0=X1[:], in1=X2[:], op=ALU.subtract), reads=[tmpB[x1], tmpB[x2]], writes=[outIB])
                else:
                    sy.op(eng, lambda e: e.tensor_tensor(out=outI, in0=X1[:], in1=X2[:], op=ALU.add), reads=[tmpB[x1], tmpB[x2]], writes=[outIB])

            CB = [prmB]
            sy.op("pool", lambda e: e.memset(IZ[:], 0.0), writes=[IZB])
            sy.op("pool", lambda e: e.tensor_copy(out=IZ[:, 0, 0:128], in_=ident[:]), reads=[identB], writes=[IZB])
            sy.op("pool", lambda e: e.tensor_copy(out=IZ[:, 1, 128:256], in_=ident[:]), reads=[identB], writes=[IZB])
            Lb = [self.sb(es, "Lb", [128, 128]) for _ in range(4)]
            LbB = [Buf(f"Lb{i}") for i in range(4)]
            prev_evacs = []
            for bi in range(NP // PBN):
                p0 = bi * PBN
                tmp.update(tmp_sets[bi % 2])
                tmpB.update(tmpB_sets[bi % 2])
                CpZR, CpZI, CpZB = CpZ_sets[bi % 2]
                POOL_ = "pool" if pd_ != 5 else "dve"
                cprod("dve", "x1", "x2", (7, -1), BBr, BBi, [BBrB, BBiB], p0, tmp["BnR"][:], tmp["BnI"][:], False, tmpB["BnR"], tmpB["BnI"])
                if pd_ < 4:
                    continue
                cprod(POOL_, "y1", "y2", (14, -1), BBr, BBi, [BBrB, BBiB], p0, tmp["BtR"][:], tmp["BtI"][:], False, tmpB["BtR"], tmpB["BtI"])
                cprod("dve", "x1", "x2", (7, 1), cre, cim, CB, p0, tmp["CpR"][:], tmp["CpI"][:], True, tmpB["CpR"], tmpB["CpI"], npow=9)
                ctb = CtB[0]
                for nm, dst in (("CpR", CtR), ("CpI", CtI)):
                    sy.op("act", lambda e, nm=nm, dst=dst: e.activation(out=dst[:, p0:p0 + PBN, :].rearrange("p j (t h) -> p j t h", h=16),
                                                                      in_=tmp[nm][:, :, 1:9, :], func=AF.Identity),
                          reads=[tmpB[nm]], writes=[ctb])
                if pd_ < 6:
                    continue
                for nm, zn in (("CpR", CpZR), ("CpI", CpZI)):
                    for e_ in range(2):
                        r0, r1 = 64 * e_, 64 * e_ + 64
                        sy.op("act", lambda e, nm=nm, zn=zn, e_=e_, r0=r0, r1=r1: e.activation(
                            out=zn[r0:r1, :, e_, :].rearrange("p j (s h) -> p j s h", h=16), in_=tmp[nm][r0:r1, :, 0:8, :], func=AF.Identity),
                            reads=[tmpB[nm]], writes=[CpZB])
                evacs = []
                for half in range(PBN // 2):
                    pb = 2 * (bi % 2) + half
                    fns = []
                    gs = []
                    for q in range(2):
                        jl = half * 2 + q
                        gs += [2 * (p0 + jl), 2 * (p0 + jl) + 1]
                        out = self.ps[pb][:, q * 256:(q + 1) * 256]
                        fns.append(lambda e, jl=jl, out=out: e.matmul(
                            out, lhsT=tmp["BnR"][:, jl, :, :].rearrange("p s h -> p (s h)"), rhs=CpZR[:, jl, :, :].rearrange("p e c -> p (e c)"),
                            start=True, stop=False))
                        fns.append(lambda e, jl=jl, out=out: e.matmul(
                            out, lhsT=tmp["BnI"][:, jl, :, :].rearrange("p s h -> p (s h)"), rhs=CpZI[:, jl, :, :].rearrange("p e c -> p (e c)"),
                            start=False, stop=False))
                        for e_ in range(2):
                            g = 2 * (p0 + jl) + e_
                            li_ = q * 2 + e_
                            sy.op("act", lambda e, g=g, li_=li_: e.activation(out=Lb[li_][:], in_=ident[:], func=AF.Identity, scale=dvec[:, g:g + 1]),
                                  reads=[identB, prmB], writes=[LbB[li_]])
                            fns.append(lambda e, out=out, li_=li_, e_=e_: e.matmul(out, lhsT=Lb[li_][:], rhs=IZ[:, e_, :], start=False, stop=(e_ == 1)))
                    sy.group("pe", fns, reads=[tmpB["BnR"], tmpB["BnI"], CpZB, IZB] + LbB, writes=[self.psB[pb]])

                    def ev_t(pb=pb, gs=gs):
                        sy.op("dve", lambda e: e.tensor_tensor(out=Tm[:, gs[0]:gs[0] + 4, :].rearrange("p g c -> p (g c)"), in0=self.ps[pb][:],
                                                               in1=mask[:].rearrange("p a t h -> p (a t h)"), op=ALU.mult),
                              reads=[self.psB[pb], maskB], writes=[TmB[gs[0] // 4]])
                    evacs.append(ev_t)
                for hp in range(PBN // 2):
                    pb = 4 + 2 * (bi % 2) + hp
                    fns = []
                    for q in range(2):
                        jl = hp * 2 + q
                        for ri, nm in enumerate(("BtR", "BtI")):
                            col = (q * 2 + ri) * 128
                            fns.append(lambda e, jl=jl, nm=nm, col=col, pb=pb: e.transpose(
                                self.ps[pb][:, col:col + 128], tmp[nm][:, jl, :, :].rearrange("p s h -> p (s h)"), ident[:]))
                    sy.group("pe", fns, reads=[tmpB["BtR"], tmpB["BtI"], identB], writes=[self.psB[pb]])
                    jp0 = p0 + hp * 2

                    def ev_b(pb=pb, jp0=jp0):
                        sy.op("act", lambda e: e.activation(out=Btm[:, jp0:jp0 + 2, :, :].rearrange("p j r c -> p (j r c)"), in_=self.ps[pb][:], func=AF.Identity),
                              reads=[self.psB[pb]], writes=[BtB[jp0 // 2]])
                    evacs.append(ev_b)
                for f_ in prev_evacs:
                    f_()
                prev_evacs = evacs
            for f_ in prev_evacs:
                f_()

    def final(self, s, next_s=None):
        sy = self.sy
        with self.scope() as es:
            sq = [self.sb(es, "fsq", [128, NFT, TT], BF16) for _ in range(2)]
            sqB = [Buf("fsq0"), Buf("fsq1")]
            rs = [self.sb(es, "frs", [128, TT]) for _ in range(2)]
            rsB = [Buf("frs0"), Buf("frs1")]
            ob = [self.sb(es, "fo", [128, NFT, TT]) for _ in range(2)]
            obB = [Buf("fo0"), Buf("fo1")]

            def square(tt):
                b = tt % 2
                xin = self.x[:, :, tt * TT:(tt + 1) * TT]
                sy.op("act", lambda e: e.activation(out=sq[b][:], in_=xin, func=AF.Square),
                      reads=[self.xB[ft][tt] for ft in range(NFT)], writes=[sqB[b]])

            square(0)
            for tt in range(NTT):
                b = tt % 2
                if tt + 1 < NTT:
                    square(tt + 1)
                pb = 6 + b
                sy.group("pe", [lambda e, ft=ft, pb=pb, b=b: e.matmul(self.ps[pb][:], lhsT=self.ones_bf[:], rhs=sq[b][:, ft, :],
                                                                      start=(ft == 0), stop=(ft == NFT - 1)) for ft in range(NFT)],
                         reads=[sqB[b], self.onesB], writes=[self.psB[pb]])
                sy.op("act", lambda e, pb=pb, b=b: e.activation(out=rs[b][:], in_=self.ps[pb][:], func=AF.Ln, scale=1.0 / D, bias=self.epsT[:]),
                      reads=[self.psB[pb], self.epsB], writes=[rsB[b]])
                sy.op("act", lambda e, b=b: e.activation(out=rs[b][:], in_=rs[b][:], func=AF.Exp, scale=-0.5), reads=[rsB[b]], writes=[rsB[b]])
                for ft in range(NFT):
                    sy.op("dve", lambda e, b=b, ft=ft, tt=tt: e.scalar_tensor_tensor(
                        out=ob[b][:, ft, :], in0=self.xt(ft, tt), scalar=self.gcol(8, ft), in1=rs[b][:],
                        op0=ALU.mult, op1=ALU.mult),
                        reads=[self.xB[ft][tt], rsB[b], self.gB], writes=[obB[b]])
                sy.dma("sp", [lambda e, b=b, tt=tt: e.dma_start(
                    out=self.yT[s, :, tt * TT:(tt + 1) * TT].rearrange("(ft p) t -> p ft t", p=128), in_=ob[b][:])],
                    reads=[obB[b]])
                if next_s is not None:
                    for ft in range(NFT):
                        sy.dma("sp", [lambda e, ft=ft, tt=tt: e.dma_start(out=self.xt(ft, tt), in_=self.xT[next_s, ft * 128:(ft + 1) * 128, tt * TT:(tt + 1) * TT])],
                               writes=[self.xB[ft][tt]])


def _vecs(inp):
    allg = np.concatenate([inp["norm_mix"], inp["norm_mlp"], np.asarray(inp["norm_final"])[None]], 0)
    v = np.zeros((128, NV), np.float32)
    v[:, 0:72] = allg.reshape(9, NFT, 128).transpose(2, 0, 1).reshape(128, 72)
    v[:, V_PSCALE:V_PSCALE + 8] = np.asarray(inp["pool_scale"])[0].reshape(NFT, 128).T
    v[:, V_SUBLN] = np.asarray(inp["da_subln"])[0]
    lam = np.concatenate([np.asarray(inp[k])[0] for k in ("da_lam_q1", "da_lam_k1", "da_lam_q2", "da_lam_k2")])
    v[:, V_LAM:V_LAM + 256] = lam[None, :]
    return v


def _s5p(inp):
    out = np.zeros((2, 128, S5NC), np.float32)
    for j in range(2):
        def pl(a):
            a = np.asarray(a)
            sh = a.shape[2:]
            a = a.reshape((32, 2, 64) + sh)
            a = np.moveaxis(a, 0, 2)
            return a.reshape((128, 32) + sh)
        lre = pl(inp["s5_lam_re"][j])
        lim = pl(inp["s5_lam_im"][j])
        lst = pl(np.repeat(np.asarray(inp["s5_log_step"][j])[:, None], 64, 1))
        bre = pl(inp["s5_b_re"][j])
        bim = pl(inp["s5_b_im"][j])
        cre = pl(np.asarray(inp["s5_c_re"][j]).transpose(0, 2, 1))
        cim = pl(np.asarray(inp["s5_c_im"][j]).transpose(0, 2, 1))
        dv = np.tile(np.asarray(inp["s5_d"][j]).reshape(64, 16).T, (8, 1))
        o = out[j]
        o[:, 0:32] = lre
        o[:, 32:64] = lim
        o[:, 64:96] = lst
        o[:, 96:608] = bre.reshape(128, 512)
        o[:, 608:1120] = bim.reshape(128, 512)
        o[:, 1120:1632] = cre.reshape(128, 512)
        o[:, 1632:2144] = cim.reshape(128, 512)
        o[:, 2144:2208] = dv
    return out


def host_shared(inp):
    f = lambda a: np.ascontiguousarray(np.asarray(a, np.float32))
    return {
        "vecs": _vecs(inp),
        "w1": f(inp["mlp_w1"]),
        "w2": f(inp["mlp_w2"]),
        "poolw": f(inp["pool_w"][0]),
        "wqkv": f(inp["da_w_qkv"][0]),
        "wo": f(inp["da_w_o"][0]),
        "s5w": f(np.stack([inp["s5_w_in"], inp["s5_w_gate"], inp["s5_w_out"]], 1)),
        "s5p": _s5p(inp),
    }


def kernel(**inputs):
    x = np.asarray(inputs["x"], np.float32)
    prog = Prog()
    nc = prog.build()
    shared = host_shared(inputs)
    in_maps = []
    for c in range(N_CORES):
        xs = x[c * SEQ_PER_CORE:(c + 1) * SEQ_PER_CORE]
        m = dict(shared)
        m["xT"] = np.ascontiguousarray(xs.transpose(0, 2, 1))
        in_maps.append(m)
    res = run_bass_kernel_spmd(nc, in_maps, core_ids=list(range(N_CORES)))
    out = np.empty_like(x)
    for c in range(N_CORES):
        out[c * SEQ_PER_CORE:(c + 1) * SEQ_PER_CORE] = res.results[c]["yT"].transpose(0, 2, 1)
    return out
```

```python
import contextlib
import math
import numpy as np
import concourse.bass as bass
import concourse.mybir as mybir
from concourse.bass_utils import run_bass_kernel_spmd

F32 = mybir.dt.float32
BF16 = mybir.dt.bfloat16
I32 = mybir.dt.int32
AF = mybir.ActivationFunctionType
ALU = mybir.AluOpType

D = 1024
S = 2048
NFT = 8
TT = 512
NTT = S // TT
DFF = 4096
EPS = 1e-6
DEPTH = 4
N_CORES = 8
SEQ_PER_CORE = 2
V_PSCALE = 72
V_SUBLN = 80
V_LAM = 81
NV = 81 + 256
S5NC = 96 + 2048 + 64
LAM_INIT = 0.8 - 0.6 * math.exp(-0.3 * 1)
POOL_W = (2, 4, 8, 16)


class Buf:
    __slots__ = ("name", "w", "r")

    def __init__(self, name):
        self.name = name
        self.w = None
        self.r = {}


class Eng:
    def __init__(self, name, e, sem):
        self.name = name
        self.e = e
        self.sem = sem
        self.count = 0
        self.waited = {}


class Sync:
    def __init__(self, nc, n_dma_sems=12):
        self.nc = nc
        self.engs = {}
        for name, e in (("pe", nc.tensor), ("act", nc.scalar), ("dve", nc.vector),
                        ("pool", nc.gpsimd), ("sp", nc.sync)):
            self.engs[name] = Eng(name, e, nc.alloc_semaphore(name="s_" + name))
        self.dma_sems = {q: [[nc.alloc_semaphore(name=f"d{q}{i}"), 0] for i in range(n_dma_sems)] for q in ("sp", "pool")}
        self.dma_rr = {"sp": 0, "pool": 0}
        self.ninst = 0

    def _wait(self, eng, ticket):
        sem, val, src = ticket
        if src == "pe" and eng.name == "pe":
            return
        k = id(sem)
        if eng.waited.get(k, 0) >= val:
            return
        eng.e.wait_ge(sem, val)
        eng.waited[k] = val

    def _deps(self, eng, reads, writes):
        need = {}

        def add(t):
            if t is None:
                return
            k = id(t[0])
            if k not in need or need[k][1] < t[1]:
                need[k] = t
        for b in reads:
            add(b.w)
        for b in writes:
            add(b.w)
            for t in b.r.values():
                add(t)
        for t in need.values():
            self._wait(eng, t)

    @staticmethod
    def _mark(ticket, key, reads, writes):
        for b in reads:
            b.r[key] = ticket
        for b in writes:
            b.w = ticket
            b.r = {}

    def op(self, engname, fn, reads=(), writes=()):
        eng = self.engs[engname]
        self._deps(eng, reads, writes)
        inst = fn(eng.e)
        eng.count += 1
        inst.then_inc(eng.sem, 1)
        t = (eng.sem, eng.count, eng.name)
        self._mark(t, eng.name, reads, writes)
        self.ninst += 1
        return t

    def group(self, engname, fns, reads=(), writes=()):
        eng = self.engs[engname]
        self._deps(eng, reads, writes)
        inst = None
        for fn in fns:
            inst = fn(eng.e)
            self.ninst += 1
        eng.count += 1
        inst.then_inc(eng.sem, 1)
        t = (eng.sem, eng.count, eng.name)
        self._mark(t, eng.name, reads, writes)
        return t

    def dma(self, engname, fns, reads=(), writes=()):
        eng = self.engs[engname]
        self._deps(eng, reads, writes)
        pool_ = self.dma_sems[engname]
        slot = pool_[self.dma_rr[engname]]
        self.dma_rr[engname] = (self.dma_rr[engname] + 1) % len(pool_)
        sem, total = slot
        if total > 0:
            self._wait(eng, (sem, total, None))
        for fn in fns:
            fn(eng.e).then_inc(sem, 16)
            total += 16
            self.ninst += 1
        slot[1] = total
        t = (sem, total, None)
        self._mark(t, "dma%d" % id(sem), reads, writes)
        return t

    def barrier(self, names=("pe", "act", "dve", "pool", "sp")):
        for n in names:
            eng = self.engs[n]
            for m in names:
                if m != n:
                    o = self.engs[m]
                    if o.count > 0:
                        self._wait(eng, (o.sem, o.count, o.name))
            for q in self.dma_sems:
                for sem, total in self.dma_sems[q]:
                    if total > 0:
                        self._wait(eng, (sem, total, None))


class Prog:
    def __init__(self, n_seq=SEQ_PER_CORE, layers=None):
        if layers is None:
            layers = [(i, i % 3, True) for i in range(DEPTH)]
        self.layers = layers
        self.n_seq = n_seq
        self.nc = bass.Bass("TRN2", target_bir_lowering=False)
        self.es = contextlib.ExitStack()
        self._uid = 0

    def dram_in(self, name, shape):
        return self.nc.dram_tensor(name, list(shape), F32, kind="ExternalInput").ap()

    def sb(self, es, name, shape, dt=F32):
        self._uid += 1
        return es.enter_context(self.nc.sbuf_tensor(f"{name}_{self._uid}", list(shape), dt))

    @contextlib.contextmanager
    def scope(self):
        with contextlib.ExitStack() as es:
            yield es
            self.sy.barrier()

    def build(self):
        nc = self.nc
        ns = self.n_seq
        self.xT = self.dram_in("xT", [ns, D, S])
        self.gall = self.dram_in("vecs", [128, NV])
        self.w1 = self.dram_in("w1", [DEPTH, D, DFF])
        self.w2 = self.dram_in("w2", [DEPTH, DFF, D])
        self.poolw = self.dram_in("poolw", [4, 256, 256])
        self.wqkv = self.dram_in("wqkv", [D, 3 * D])
        self.wo = self.dram_in("wo", [D, D])
        self.s5w = self.dram_in("s5w", [2, 3, D, D])
        self.s5p = self.dram_in("s5p", [2, 128, S5NC])
        self.yT = nc.dram_tensor("yT", [ns, D, S], F32, kind="ExternalOutput").ap()
        with self.es as es:
            self.sy = Sync(nc)
            sy = self.sy
            self.ps = [es.enter_context(nc.psum_tensor(f"psb{i}", [128, 512], F32)) for i in range(8)]
            self.psB = [Buf(f"ps{i}") for i in range(8)]
            self.ones_bf = self.sb(es, "ones", [128, 128], BF16)
            self.onesB = Buf("ones")
            sy.op("dve", lambda e: e.memset(self.ones_bf[:], 1.0), writes=[self.onesB])
            self.idb = self.sb(es, "idb", [128, 128], BF16)
            self.idbB = Buf("idb")
            with self.scope() as es0:
                idf = self.sb(es0, "idf0", [128, 128])
                idfB = Buf("idf0")
                sy.op("pool", lambda e: e.memset(idf[:], 0.0), writes=[idfB])
                sy.op("pool", lambda e: e.affine_select(out=idf[:], in_=idf[:], pattern=[[-1, 128]], compare_op=ALU.not_equal, fill=1.0, base=0, channel_multiplier=1),
                      reads=[idfB], writes=[idfB])
                sy.op("dve", lambda e: e.tensor_copy(out=self.idb[:], in_=idf[:]), reads=[idfB], writes=[self.idbB])
            self.epsT = self.sb(es, "eps", [128, 1])
            self.epsB = Buf("eps")
            sy.op("dve", lambda e: e.memset(self.epsT[:], EPS), writes=[self.epsB])
            self.g = self.sb(es, "gall", [128, NV])
            self.gB = Buf("g")
            sy.dma("sp", [lambda e: e.dma_start(out=self.g[:], in_=self.gall)], writes=[self.gB])
            self.x = self.sb(es, "x", [128, NFT, S])
            self.xB = [[Buf(f"x{ft}_{tt}") for tt in range(NTT)] for ft in range(NFT)]
            self.load_x(0)
            self.s5c = {}
            self.rope_c = None
            for (li_, kind_, _m) in self.layers:
                if kind_ == 0 and (li_ // 3) not in self.s5c:
                    self.s5_precompute(li_ // 3)
            for s in range(ns):
                self.seq(s, s + 1 if s + 1 < ns else None)
            sy.barrier()
        return nc

    def gcol(self, idx, ft):
        c = idx * NFT + ft
        return self.g[:, c:c + 1]

    def xt(self, ft, tt):
        return self.x[:, ft, tt * TT:(tt + 1) * TT]

    def load_x(self, s):
        sy = self.sy
        for tt in range(NTT):
            for ft in range(NFT):
                sy.dma("sp", [lambda e, ft=ft, tt=tt: e.dma_start(out=self.xt(ft, tt), in_=self.xT[s, ft * 128:(ft + 1) * 128, tt * TT:(tt + 1) * TT])],
                       writes=[self.xB[ft][tt]])

    def seq(self, s, next_s=None):
        sy = self.sy
        for (li, kind, do_mlp) in self.layers:
            if kind == 0:
                self.s5(li, li // 3)
            elif kind == 1:
                self.attn(li)
            elif kind == 2:
                self.pool(li)
            if do_mlp:
                self.mlp(li)
        self.final(s, next_s)

    def rmsnorm(self, es, gidx, h, hB, inline=False):
        sy = self.sy
        if inline:
            self._rmsnorm(es, gidx, h, hB)
            return
        with self.scope() as es2:
            self._rmsnorm(es2, gidx, h, hB)

    def _rmsnorm(self, es, gidx, h, hB):
        sy = self.sy
        sq = [self.sb(es, "sq", [128, NFT, TT], BF16) for _ in range(2)]
        sqB = [Buf("sq0"), Buf("sq1")]
        rs = [self.sb(es, "rs", [128, TT]) for _ in range(2)]
        rsB = [Buf("rs0"), Buf("rs1")]
        pbank = [6, 7]

        def square(tt):
            b = tt % 2
            xin = self.x[:, :, tt * TT:(tt + 1) * TT]
            sy.op("act", lambda e: e.activation(out=sq[b][:], in_=xin, func=AF.Square),
                  reads=[self.xB[ft][tt] for ft in range(NFT)], writes=[sqB[b]])

        square(0)
        for tt in range(NTT):
            b = tt % 2
            if tt + 1 < NTT:
                square(tt + 1)
            pb = pbank[b]
            sy.group("pe", [lambda e, b=b, ft=ft, pb=pb: e.matmul(self.ps[pb][:], lhsT=self.ones_bf[:], rhs=sq[b][:, ft, :],
                                                                start=(ft == 0), stop=(ft == NFT - 1)) for ft in range(NFT)],
                     reads=[sqB[b], self.onesB], writes=[self.psB[pb]])
            sy.op("act", lambda e, b=b, pb=pb: e.activation(out=rs[b][:], in_=self.ps[pb][:], func=AF.Ln, scale=1.0 / D, bias=self.epsT[:]),
                  reads=[self.psB[pb], self.epsB], writes=[rsB[b]])
            sy.op("act", lambda e, b=b: e.activation(out=rs[b][:], in_=rs[b][:], func=AF.Exp, scale=-0.5), reads=[rsB[b]], writes=[rsB[b]])
            for ft in range(NFT):
                sy.op("dve", lambda e, b=b, ft=ft, tt=tt: e.scalar_tensor_tensor(
                    out=h[:, ft, tt * TT:(tt + 1) * TT], in0=self.xt(ft, tt), scalar=self.gcol(gidx, ft), in1=rs[b][:],
                    op0=ALU.mult, op1=ALU.mult),
                    reads=[self.xB[ft][tt], rsB[b], self.gB], writes=[hB[ft][tt]])

    def mlp(self, li):
        sy = self.sy
        HC = 4
        NHC = DFF // (HC * 128)
        with self.scope() as es:
            h = self.sb(es, "h", [128, NFT, S], BF16)
            hB = [[Buf(f"h{ft}_{tt}") for tt in range(NTT)] for ft in range(NFT)]
            hid = [self.sb(es, "hid", [128, HC, S], BF16) for _ in range(2)]
            hidB = [[[Buf(f"hid{b}_{hi}_{tt}") for tt in range(NTT)] for hi in range(HC)] for b in range(2)]
            w1c = [self.sb(es, "w1c", [128, NFT, HC * 128], BF16) for _ in range(2)]
            w1B = [Buf("w1c0"), Buf("w1c1")]
            w2c = [self.sb(es, "w2c", [128, HC, D], BF16) for _ in range(2)]
            w2B = [Buf("w2c0"), Buf("w2c1")]
            rt = [self.sb(es, "rt", [128, TT]) for _ in range(3)]
            rtB = [Buf(f"rt{i}") for i in range(3)]
            cnt = {"f": 0, "s": 0, "r": 0}

            def load(hc):
                b = hc % 2
                c0 = hc * HC * 128
                sy.dma("pool", [lambda e: e.dma_start(out=w1c[b][:], in_=self.w1[li, :, c0:c0 + HC * 128].rearrange("(ft p) n -> p ft n", p=128))],
                       writes=[w1B[b]])
                sy.dma("pool", [lambda e: e.dma_start(out=w2c[b][:], in_=self.w2[li, c0:c0 + HC * 128, :].rearrange("(hi p) n -> p hi n", p=128))],
                       writes=[w2B[b]])

            def first(hc):
                b = hc % 2
                for tt in range(NTT):
                    for h2 in range(HC // 2):
                        pbs = [2 * (cnt["f"] % 2), 2 * (cnt["f"] % 2) + 1]
                        cnt["f"] += 1
                        fns = []
                        for q, pb in enumerate(pbs):
                            hi = 2 * h2 + q
                            for ft in range(NFT):
                                fns.append(lambda e, ft=ft, pb=pb, hi=hi: e.matmul(
                                    self.ps[pb][:], lhsT=w1c[b][:, ft, hi * 128:(hi + 1) * 128], rhs=h[:, ft, tt * TT:(tt + 1) * TT],
                                    start=(ft == 0), stop=(ft == NFT - 1)))
                        sy.group("pe", fns, reads=[w1B[b]] + [hB[ft][tt] for ft in range(NFT)], writes=[self.psB[pb] for pb in pbs])
                        for q, pb in enumerate(pbs):
                            hi = 2 * h2 + q
                            r = cnt["r"] % 3
                            cnt["r"] += 1
                            sy.op("act", lambda e, pb=pb, r=r: e.activation(out=rt[r][:], in_=self.ps[pb][:], func=AF.Relu),
                                  reads=[self.psB[pb]], writes=[rtB[r]])
                            sy.op("pool", lambda e, r=r, hi=hi: e.tensor_tensor(
                                out=hid[b][:, hi, tt * TT:(tt + 1) * TT], in0=rt[r][:], in1=rt[r][:], op=ALU.mult),
                                reads=[rtB[r]], writes=[hidB[b][hi][tt]])

            def second(hc):
                b = hc % 2
                for tt in range(NTT):
                    for f2 in range(NFT // 2):
                        pbs = [4 + 2 * (cnt["s"] % 2), 5 + 2 * (cnt["s"] % 2)]
                        cnt["s"] += 1
                        fns = []
                        for q, pb in enumerate(pbs):
                            fo = 2 * f2 + q
                            for hi in range(HC):
                                fns.append(lambda e, hi=hi, pb=pb, fo=fo: e.matmul(
                                    self.ps[pb][:], lhsT=w2c[b][:, hi, fo * 128:(fo + 1) * 128], rhs=hid[b][:, hi, tt * TT:(tt + 1) * TT],
                                    start=(hi == 0), stop=(hi == HC - 1)))
                        sy.group("pe", fns, reads=[w2B[b]] + [hidB[b][hi][tt] for hi in range(HC)], writes=[self.psB[pb] for pb in pbs])
                        for q, pb in enumerate(pbs):
                            fo = 2 * f2 + q
                            sy.op("dve", lambda e, pb=pb, fo=fo: e.tensor_tensor(
                                out=self.xt(fo, tt), in0=self.xt(fo, tt), in1=self.ps[pb][:], op=ALU.add),
                                reads=[self.psB[pb], self.xB[fo][tt]], writes=[self.xB[fo][tt]])

            import os
            dbg = int(os.environ.get("MLPDBG", "9"))
            load(0)
            load(1)
            self.rmsnorm(es, 4 + li, h, hB, inline=True)
            if dbg >= 2:
                first(0)
            for hc in range(NHC):
                if hc + 1 < NHC and dbg >= 2:
                    first(hc + 1)
                if dbg >= 3:
                    second(hc)
                if hc + 2 < NHC:
                    load(hc + 2)

    def pool(self, li):
        sy = self.sy
        with self.scope() as es:
            h = self.sb(es, "hp", [128, NFT, S])
            hB = [[Buf(f"hp{ft}_{tt}") for tt in range(NTT)] for ft in range(NFT)]
            self.rmsnorm(es, li, h, hB)
            p = self.sb(es, "pp", [128, NFT, S], BF16)
            pB = [Buf(f"pp{ft}") for ft in range(NFT)]
            wp = self.sb(es, "wp", [128, 4, 2, 256], BF16)
            wpB = Buf("wp")
            sy.dma("pool", [lambda e: e.dma_start(out=wp[:], in_=self.poolw.rearrange("g (kt p) d -> p g kt d", p=128))], writes=[wpB])
            rc = self.sb(es, "rc", [128, 16])
            rcB = Buf("rc")
            sy.op("pool", lambda e: e.iota(rc[:], pattern=[[1, 16]], base=1, channel_multiplier=0, allow_small_or_imprecise_dtypes=True), writes=[rcB])
            sy.op("dve", lambda e: e.reciprocal(out=rc[:], in_=rc[:]), reads=[rcB], writes=[rcB])
            ab = [[self.sb(es, "pa", [128, S]) for _ in range(2)] for _ in range(2)]
            abB = [[Buf("pa"), Buf("pb")] for _ in range(2)]
            tmpc = [self.sb(es, "ptc", [128, 16]) for _ in range(2)]
            tmpB = [Buf("ptc0"), Buf("ptc1")]
            for ft in range(NFT):
                w = POOL_W[ft // 2]
                eng = "dve" if ft % 2 == 0 else "pool"
                k = ft % 2
                hall = [hB[ft][tt] for tt in range(NTT)]
                src, srcB = h[:, ft, :], hall
                sh = 1
                i = 0
                while sh < w:
                    dst, dstB = ab[k][i % 2], [abB[k][i % 2]]
                    sy.op(eng, lambda e, dst=dst, src=src, sh=sh: e.tensor_tensor(out=dst[:, sh:], in0=src[:, sh:], in1=src[:, :S - sh], op=ALU.add),
                          reads=srcB, writes=dstB)
                    sy.op(eng, lambda e, dst=dst, src=src, sh=sh: e.tensor_copy(out=dst[:, 0:sh], in_=src[:, 0:sh]), reads=srcB, writes=dstB)
                    src, srcB = dst[:], dstB
                    sh *= 2
                    i += 1
                sy.op("dve", lambda e, src=src, ft=ft, w=w: e.scalar_tensor_tensor(out=p[:, ft, :], in0=src, scalar=1.0 / w, in1=h[:, ft, :],
                                                                                 op0=ALU.mult, op1=ALU.subtract),
                      reads=srcB + hall, writes=[pB[ft]])
                sy.op("dve", lambda e, src=src, k=k, w=w: e.tensor_tensor(out=tmpc[k][:, 0:w - 1], in0=src[:, 0:w - 1], in1=rc[:, 0:w - 1], op=ALU.mult),
                      reads=srcB + [rcB], writes=[tmpB[k]])
                sy.op("dve", lambda e, k=k, ft=ft, w=w: e.tensor_tensor(out=p[:, ft, 0:w - 1], in0=tmpc[k][:, 0:w - 1], in1=h[:, ft, 0:w - 1], op=ALU.subtract),
                      reads=[tmpB[k]] + hall, writes=[pB[ft]])
            cnt = 0
            for tt in range(NTT):
                for g in range(4):
                    for oc in range(2):
                        pb = cnt % 4
                        cnt += 1
                        fo = 2 * g + oc
                        sy.group("pe", [lambda e, kt=kt, pb=pb, g=g, oc=oc, tt=tt: e.matmul(
                            self.ps[pb][:], lhsT=wp[:, g, kt, oc * 128:(oc + 1) * 128], rhs=p[:, 2 * g + kt, tt * TT:(tt + 1) * TT],
                            start=(kt == 0), stop=(kt == 1)) for kt in range(2)],
                            reads=[wpB, pB[2 * g], pB[2 * g + 1]], writes=[self.psB[pb]])
                        sy.op("dve", lambda e, pb=pb, fo=fo, tt=tt: e.scalar_tensor_tensor(
                            out=self.xt(fo, tt), in0=self.ps[pb][:], scalar=self.g[:, V_PSCALE + fo:V_PSCALE + fo + 1], in1=self.xt(fo, tt),
                            op0=ALU.mult, op1=ALU.add),
                            reads=[self.psB[pb], self.xB[fo][tt], self.gB], writes=[self.xB[fo][tt]])

    def attn(self, li):
        sy = self.sy
        NH = 8
        TWO_PI = 2.0 * math.pi
        with self.scope() as es:
            h = self.sb(es, "ha", [128, NFT, S], BF16)
            hB = [[Buf(f"ha{ft}_{tt}") for tt in range(NTT)] for ft in range(NFT)]
            hall = [hB[ft][tt] for ft in range(NFT) for tt in range(NTT)]
            cosT = self.sb(es, "cosT", [128, S])
            sinS = self.sb(es, "sinS", [128, S])
            cosB, sinB = Buf("cosT"), Buf("sinS")
            perm = self.sb(es, "perm", [128, 128])
            permB = Buf("perm")
            nlam = self.sb(es, "nlam", [128, 1])
            nlamB = Buf("nlam")
            subs = self.sb(es, "subs", [128, 1])
            subsB = Buf("subs")
            with self.scope() as es2:
                if self.rope_c is None:
                    jf = self.sb(es2, "jf", [128, 1])
                    jB = Buf("jf")
                    pi_ = self.sb(es2, "pi", [128, 1])
                    piB = Buf("pi")
                    sy.op("pool", lambda e: e.iota(pi_[:], pattern=[[0, 1]], base=0, channel_multiplier=1, allow_small_or_imprecise_dtypes=True), writes=[piB])
                    qi = self.sb(es2, "qi", [128, 1], I32)
                    qB = Buf("qi")
                    sy.op("dve", lambda e: e.tensor_scalar(out=jf[:], in0=pi_[:], scalar1=-15.5, scalar2=1.0 / 32, op0=ALU.add, op1=ALU.mult), reads=[piB], writes=[jB])
                    sy.op("dve", lambda e: e.tensor_copy(out=qi[:], in_=jf[:]), reads=[jB], writes=[qB])
                    sy.op("dve", lambda e: e.tensor_copy(out=jf[:], in_=qi[:]), reads=[qB], writes=[jB])
                    sy.op("dve", lambda e: e.scalar_tensor_tensor(out=jf[:], in0=jf[:], scalar=-32.0, in1=pi_[:], op0=ALU.mult, op1=ALU.add), reads=[jB, piB], writes=[jB])
                    invf = self.sb(es2, "invf", [128, 1])
                    ifB = Buf("invf")
                    sy.op("act", lambda e: e.activation(out=invf[:], in_=jf[:], func=AF.Exp, scale=-math.log(10000.0) * 2.0 / 64.0), reads=[jB], writes=[ifB])
                    sgn = self.sb(es2, "sgn", [128, 1])
                    sgB = Buf("sgn")
                    sy.op("pool", lambda e: e.memset(sgn[:], 1.0), writes=[sgB])
                    sy.op("pool", lambda e: e.memset(sgn[0:32, :], -1.0), reads=[], writes=[sgB])
                    sy.op("pool", lambda e: e.memset(sgn[64:96, :], -1.0), reads=[], writes=[sgB])
                    ang = self.sb(es2, "ang", [128, S])
                    angB = Buf("ang")
                    ni = self.sb(es2, "ni", [128, S], I32)
                    niB = Buf("ni")
                    nf = self.sb(es2, "nf", [128, S])
                    nfB = Buf("nf")
                    sy.op("pool", lambda e: e.iota(ang[:], pattern=[[1, S]], base=0, channel_multiplier=0, allow_small_or_imprecise_dtypes=True), writes=[angB])
                    sy.op("dve", lambda e: e.tensor_scalar(out=ang[:], in0=ang[:], scalar1=invf[:], scalar2=None, op0=ALU.mult), reads=[angB, ifB], writes=[angB])

                    def sin_of(dst, dstB, shift, signed):
                        sy.op("dve", lambda e: e.tensor_scalar(out=nf[:], in0=ang[:], scalar1=shift, scalar2=1.0 / TWO_PI, op0=ALU.add, op1=ALU.mult), reads=[angB], writes=[nfB])
                        sy.op("dve", lambda e: e.tensor_copy(out=ni[:], in_=nf[:]), reads=[nfB], writes=[niB])
                        sy.op("dve", lambda e: e.tensor_copy(out=nf[:], in_=ni[:]), reads=[niB], writes=[nfB])
                        sy.op("dve", lambda e: e.scalar_tensor_tensor(out=nf[:], in0=nf[:], scalar=-TWO_PI, in1=ang[:], op0=ALU.mult, op1=ALU.add), reads=[nfB, angB], writes=[nfB])
                        sy.op("dve", lambda e: e.tensor_scalar(out=nf[:], in0=nf[:], scalar1=shift, scalar2=math.pi, op0=ALU.add, op1=ALU.min), reads=[nfB], writes=[nfB])
                        sy.op("dve", lambda e: e.tensor_scalar(out=nf[:], in0=nf[:], scalar1=-math.pi, scalar2=None, op0=ALU.max), reads=[nfB], writes=[nfB])
                        if signed:
                            sy.op("act", lambda e: e.activation(out=dst[:], in_=nf[:], func=AF.Sin), reads=[nfB], writes=[dstB])
                            sy.op("dve", lambda e: e.tensor_scalar(out=dst[:], in0=dst[:], scalar1=sgn[:], scalar2=None, op0=ALU.mult), reads=[dstB, sgB], writes=[dstB])
                        else:
                            sy.op("act", lambda e: e.activation(out=dst[:], in_=nf[:], func=AF.Sin), reads=[nfB], writes=[dstB])
                    sin_of(sinS, sinB, 0.0, True)
                    sin_of(cosT, cosB, math.pi / 2, False)
                    self.rope_c = self.nc.dram_tensor("rope_c", [128, 2 * S], F32, kind="Internal").ap()
                    sy.dma("sp", [lambda e: e.dma_start(out=self.rope_c[:, 0:S], in_=cosT[:]),
                                  lambda e: e.dma_start(out=self.rope_c[:, S:2 * S], in_=sinS[:])], reads=[cosB, sinB])
                else:
                    sy.dma("sp", [lambda e: e.dma_start(out=cosT[:], in_=self.rope_c[:, 0:S]),
                                  lambda e: e.dma_start(out=sinS[:], in_=self.rope_c[:, S:2 * S])], writes=[cosB, sinB])
                sy.op("pool", lambda e: e.memset(perm[:], 0.0), writes=[permB])
                for (c0, off) in ((0, 32), (32, -32), (64, 32), (96, -32)):
                    sy.op("pool", lambda e, c0=c0, off=off: e.affine_select(
                        out=perm[:, c0:c0 + 32], in_=perm[:, c0:c0 + 32], pattern=[[-1, 32]], compare_op=ALU.not_equal, fill=1.0,
                        base=-(c0 + off), channel_multiplier=1), reads=[permB], writes=[permB])
                lt = self.sb(es2, "lt", [128, 2, 64])
                ltB = Buf("lt")
                ls = self.sb(es2, "ls", [128, 2])
                lsB = Buf("ls")
                lv = self.g[:, V_LAM:V_LAM + 256].rearrange("p (a b c) -> p a b c", a=2, b=2)
                sy.op("dve", lambda e: e.tensor_tensor(out=lt[:], in0=lv[:, :, 0, :], in1=lv[:, :, 1, :], op=ALU.mult), reads=[self.gB], writes=[ltB])
                sy.op("dve", lambda e: e.reduce_sum(out=ls[:], in_=lt[:], axis=mybir.AxisListType.X), reads=[ltB], writes=[lsB])
                sy.op("act", lambda e: e.activation(out=ls[:], in_=ls[:], func=AF.Exp), reads=[lsB], writes=[lsB])
                sy.op("dve", lambda e: e.tensor_tensor(out=nlam[:], in0=ls[:, 1:2], in1=ls[:, 0:1], op=ALU.subtract), reads=[lsB], writes=[nlamB])
                sy.op("dve", lambda e: e.tensor_scalar(out=nlam[:], in0=nlam[:], scalar1=-LAM_INIT, scalar2=None, op0=ALU.add), reads=[nlamB], writes=[nlamB])
                sy.op("dve", lambda e: e.tensor_scalar(out=subs[:], in0=self.g[:, V_SUBLN:V_SUBLN + 1], scalar1=1.0 - LAM_INIT, scalar2=None, op0=ALU.mult),
                      reads=[self.gB], writes=[subsB])
            wq = [self.sb(es, "wq", [128, NFT, 3, 128], BF16) for _ in range(2)]
            wqB = [Buf("wq0"), Buf("wq1")]
            woh = [self.sb(es, "woh", [128, D], BF16) for _ in range(4)]
            woB = [Buf(f"wo{i}") for i in range(4)]

            def load_w(hd):
                b = hd % 2
                fns = []
                for j in range(3):
                    c0 = j * D + hd * 128
                    fns.append(lambda e, j=j, c0=c0: e.dma_start(out=wq[b][:, :, j, :], in_=self.wqkv[:, c0:c0 + 128].rearrange("(kt p) n -> p kt n", p=128)))
                sy.dma("pool", fns, writes=[wqB[b]])

            def load_wo(hd):
                b = hd % 4
                sy.dma("pool", [lambda e: e.dma_start(out=woh[b][:], in_=self.wo[hd * 128:(hd + 1) * 128, :])], writes=[woB[b]])

            load_w(0)
            load_w(1)
            for h_ in range(4):
                load_wo(h_)
            self.rmsnorm(es, li, h, hB)
            qh = [self.sb(es, "qh", [128, S], BF16) for _ in range(2)]
            kh = [self.sb(es, "kh", [128, S], BF16) for _ in range(2)]
            vh = [self.sb(es, "vh", [128, 16, 128], BF16) for _ in range(2)]
            oth = [self.sb(es, "oth", [128, S], BF16) for _ in range(4)]
            qB = [[Buf(f"qh{b}_{tt}") for tt in range(NTT)] for b in range(2)]
            kB = [[Buf(f"kh{b}_{tt}") for tt in range(NTT)] for b in range(2)]
            vB = [[Buf(f"vh{b}_{j}") for j in range(4)] for b in range(2)]
            oB = [[Buf(f"oth{b}_{tt}") for tt in range(NTT)] for b in range(4)]
            qf = [self.sb(es, "qf", [128, TT]) for _ in range(2)]
            qfB = [Buf("qf0"), Buf("qf1")]
            ta = [self.sb(es, "ta", [128, TT]) for _ in range(2)]
            taB = [Buf("ta0"), Buf("ta1")]
            tb = [self.sb(es, "tb", [128, TT]) for _ in range(2)]
            tbB = [Buf("tb0"), Buf("tb1")]
            pT = [self.sb(es, "pT", [128, TT], BF16) for _ in range(4)]
            pTB = [Buf(f"pT{i}") for i in range(4)]
            t1 = self.sb(es, "t1", [128, TT])
            t1B = Buf("t1")
            rr = [self.sb(es, "rr", [128, TT]) for _ in range(2)]
            rrB = [Buf("rr0"), Buf("rr1")]
            rs = self.sb(es, "rs_a", [128, TT])
            rsB = Buf("rs_a")
            pcp = [self.sb(es, "pcp", [128, TT]) for _ in range(2)]
            pcpB = [Buf("pcp0"), Buf("pcp1")]
            of = self.sb(es, "of", [128, TT])
            ofB = Buf("of")
            osq = self.sb(es, "osq", [128, TT], BF16)
            osqB = Buf("osq")
            cnt = {"m": 0, "s": 0, "o": 0, "p": 0, "q": 0}

            def misc_bank():
                cnt["m"] += 1
                return cnt["m"] % 2

            def proj(hd):
                b = hd % 2
                for j, (dst, dB, sc) in enumerate(((qh[b], qB[b], 0.125), (kh[b], kB[b], 1.0))):
                    for tt in range(NTT):
                        pb = misc_bank()
                        i2 = cnt["q"] % 2
                        cnt["q"] += 1
                        sy.group("pe", [lambda e, kt=kt, pb=pb, j=j, tt=tt: e.matmul(
                            self.ps[pb][:], lhsT=wq[b][:, kt, j, :], rhs=h[:, kt, tt * TT:(tt + 1) * TT], start=(kt == 0), stop=(kt == NFT - 1))
                            for kt in range(NFT)], reads=[wqB[b]] + [hB[kt][tt] for kt in range(NFT)], writes=[self.psB[pb]])
                        sy.op("dve", lambda e, pb=pb, i2=i2, sc=sc: e.tensor_scalar(out=qf[i2][:], in0=self.ps[pb][:], scalar1=sc, scalar2=None, op0=ALU.mult),
                              reads=[self.psB[pb]], writes=[qfB[i2]])
                        yield
                        pb2 = misc_bank()
                        sy.op("pe", lambda e, pb2=pb2, i2=i2: e.matmul(self.ps[pb2][:], lhsT=perm[:], rhs=qf[i2][:], start=True, stop=True),
                              reads=[permB, qfB[i2]], writes=[self.psB[pb2]])
                        sy.op("dve", lambda e, i2=i2, tt=tt: e.tensor_tensor(out=ta[i2][:], in0=qf[i2][:], in1=cosT[:, tt * TT:(tt + 1) * TT], op=ALU.mult),
                              reads=[qfB[i2], cosB], writes=[taB[i2]])
                        sy.op("dve", lambda e, i2=i2, tt=tt, pb2=pb2: e.tensor_tensor(out=tb[i2][:], in0=self.ps[pb2][:], in1=sinS[:, tt * TT:(tt + 1) * TT], op=ALU.mult),
                              reads=[self.psB[pb2], sinB], writes=[tbB[i2]])
                        sy.op("pool", lambda e, i2=i2, tt=tt, dst=dst: e.tensor_tensor(out=dst[:, tt * TT:(tt + 1) * TT], in0=ta[i2][:], in1=tb[i2][:], op=ALU.add),
                              reads=[taB[i2], tbB[i2]], writes=[dB[tt]])
                        yield
                for jq in range(4):
                    pb = misc_bank()
                    fns = []
                    for jj in range(4):
                        t0 = (jq * 4 + jj) * 128
                        for kt in range(NFT):
                            fns.append(lambda e, kt=kt, jj=jj, t0=t0, pb=pb: e.matmul(
                                self.ps[pb][:, jj * 128:(jj + 1) * 128], lhsT=h[:, kt, t0:t0 + 128], rhs=wq[b][:, kt, 2, :],
                                start=(kt == 0), stop=(kt == NFT - 1)))
                    sy.group("pe", fns, reads=[wqB[b]] + [hB[kt][jq] for kt in range(NFT)], writes=[self.psB[pb]])
                    sy.op("dve", lambda e, pb=pb, jq=jq: e.tensor_copy(out=vh[b][:, jq * 4:(jq + 1) * 4, :].rearrange("p a b -> p (a b)"), in_=self.ps[pb][:]),
                          reads=[self.psB[pb]], writes=[vB[b][jq]])
                    yield

            G = 2

            def core(hd, extra=None):
                b = hd % 2
                ob = hd % 4
                batches = []
                for qt in range(NTT):
                    nkt = 4 * qt + 4
                    for m in range(2):
                        for k0 in range(0, nkt, G):
                            batches.append((qt, m, k0, nkt))
                n = len(batches)
                po, pd = 6, 7
                deferred = []

                def qk(i):
                    qt, m, k0, nkt = batches[i]
                    r0, r1 = 64 * m, 64 * m + 64
                    sset = cnt["s"] % 2
                    cnt["s"] += 1
                    fns, tl = [], []
                    for g_ in range(G):
                        kt = k0 + g_
                        r = kt - 4 * qt
                        c0 = 128 * r if r > 0 else 0
                        pst = 2 + 2 * sset + g_
                        ip = cnt["p"] % 4
                        cnt["p"] += 1
                        tl.append((kt, r, c0, pst, ip))
                        fns.append(lambda e, kt=kt, c0=c0, pst=pst: e.matmul(
                            self.ps[pst][:, c0:TT], lhsT=kh[b][r0:r1, kt * 128:(kt + 1) * 128], rhs=qh[b][r0:r1, qt * TT + c0:(qt + 1) * TT],
                            start=True, stop=True))
                    sy.group("pe", fns, reads=[kB[b][k0 // 4], qB[b][qt]], writes=[self.psB[t[3]] for t in tl])
                    for (kt, r, c0, pst, ip) in tl:
                        sy.op("act", lambda e, c0=c0, pst=pst, ip=ip: e.activation(out=pT[ip][:, c0:TT], in_=self.ps[pst][:, c0:TT], func=AF.Exp),
                              reads=[self.psB[pst]], writes=[pTB[ip]])
                        if r >= 0:
                            sy.op("pool", lambda e, ip=ip, c0=c0: e.memset(pT[ip][64:128, c0:c0 + 64], 0.0), reads=[], writes=[pTB[ip]])
                    return tl

                def av(i, tl):
                    qt, m, k0, nkt = batches[i]
                    fns = []
                    for (kt, r, c0, pst, ip) in tl:
                        fns.append(lambda e, kt=kt, c0=c0, ip=ip: e.matmul(self.ps[po][:, c0:TT], lhsT=vh[b][:, kt, :], rhs=pT[ip][:, c0:TT],
                                                                         start=(kt == 0), stop=(kt == nkt - 1)))
                        fns.append(lambda e, kt=kt, c0=c0, ip=ip: e.matmul(self.ps[pd][:, c0:TT], lhsT=self.ones_bf[:], rhs=pT[ip][:, c0:TT],
                                                                         start=(kt == 0), stop=(kt == nkt - 1)))
                    sy.group("pe", fns, reads=[vB[b][k0 // 4], self.onesB] + [pTB[t[4]] for t in tl], writes=[self.psB[po], self.psB[pd]])
                    if k0 + G >= nkt:
                        epi(qt, m, po, pd)

                def epi(qt, m, po, pd):
                    k = m
                    sy.op("dve", lambda e: e.tensor_copy(out=pcp[k][:], in_=self.ps[po][:]), reads=[self.psB[po]], writes=[pcpB[k]])
                    sy.op("act", lambda e: e.activation(out=rr[k][:], in_=self.ps[pd][:], func=AF.Ln), reads=[self.psB[pd]], writes=[rrB[k]])
                    sy.op("act", lambda e: e.activation(out=rr[k][:], in_=rr[k][:], func=AF.Exp, scale=-1.0), reads=[rrB[k]], writes=[rrB[k]])
                    if m == 0:
                        sy.op("dve", lambda e: e.tensor_tensor(out=t1[:], in0=pcp[k][:], in1=rr[k][:], op=ALU.mult),
                              reads=[pcpB[k], rrB[k]], writes=[t1B])
                        return
                    sy.op("dve", lambda e: e.tensor_tensor(out=rr[k][:], in0=pcp[k][:], in1=rr[k][:], op=ALU.mult),
                          reads=[pcpB[k], rrB[k]], writes=[rrB[k]])
                    sy.op("dve", lambda e: e.scalar_tensor_tensor(out=of[:], in0=rr[k][:], scalar=nlam[:], in1=t1[:], op0=ALU.mult, op1=ALU.add),
                          reads=[rrB[k], t1B, nlamB], writes=[ofB])
                    deferred.append([2, lambda: subln_sq(qt)])
                    deferred.append([4, lambda: subln(qt)])

                def subln_sq(qt):
                    sy.op("dve", lambda e: e.tensor_tensor(out=osq[:], in0=of[:], in1=of[:], op=ALU.mult), reads=[ofB], writes=[osqB])

                def subln(qt):
                    pb = misc_bank()
                    sy.op("pe", lambda e: e.matmul(self.ps[pb][:], lhsT=self.ones_bf[:], rhs=osq[:], start=True, stop=True),
                          reads=[osqB, self.onesB], writes=[self.psB[pb]])
                    sy.op("act", lambda e: e.activation(out=rs[:], in_=self.ps[pb][:], func=AF.Ln, scale=1.0 / 128, bias=self.epsT[:]),
                          reads=[self.psB[pb], self.epsB], writes=[rsB])
                    sy.op("act", lambda e: e.activation(out=rs[:], in_=rs[:], func=AF.Exp, scale=-0.5), reads=[rsB], writes=[rsB])
                    sy.op("dve", lambda e: e.scalar_tensor_tensor(out=oth[ob][:, qt * TT:(qt + 1) * TT], in0=of[:], scalar=subs[:], in1=rs[:],
                                                                 op0=ALU.mult, op1=ALU.mult),
                          reads=[ofB, rsB, subsB], writes=[oB[ob][qt]])

                pend = {}
                for i in range(n + 1):
                    if i < n:
                        pend[i] = qk(i)
                    if i >= 1:
                        for d_ in list(deferred):
                            d_[0] -= 1
                            if d_[0] <= 0:
                                deferred.remove(d_)
                                d_[1]()
                        av(i - 1, pend.pop(i - 1))
                        if extra is not None:
                            next(extra, None)
                for d_ in deferred:
                    d_[1]()
                if extra is not None:
                    for _ in extra:
                        pass

            def outp(hd):
                hs = [hd - 1, hd]
                for tt in range(NTT):
                    for f2 in range(NFT // 2):
                        pbs = [0, 1]
                        fns = []
                        for q, pb in enumerate(pbs):
                            fo = 2 * f2 + q
                            for i_, h_ in enumerate(hs):
                                bb = h_ % 4
                                fns.append(lambda e, pb=pb, fo=fo, bb=bb, i_=i_: e.matmul(
                                    self.ps[pb][:], lhsT=woh[bb][:, fo * 128:(fo + 1) * 128], rhs=oth[bb][:, tt * TT:(tt + 1) * TT],
                                    start=(i_ == 0), stop=(i_ == 1)))
                        sy.group("pe", fns, reads=[woB[h_ % 4] for h_ in hs] + [oB[h_ % 4][tt] for h_ in hs], writes=[self.psB[0], self.psB[1]])
                        for q, pb in enumerate(pbs):
                            fo = 2 * f2 + q
                            sy.op("dve", lambda e, pb=pb, fo=fo: e.tensor_tensor(out=self.xt(fo, tt), in0=self.xt(fo, tt), in1=self.ps[pb][:], op=ALU.add),
                                  reads=[self.psB[pb], self.xB[fo][tt]], writes=[self.xB[fo][tt]])
                        yield

            def chain(*gens):
                for g_ in gens:
                    if g_ is not None:
                        yield from g_

            for _ in proj(0):
                pass
            load_w(2)
            for hd in range(NH):
                pj = proj(hd + 1) if hd + 1 < NH else None
                op_ = outp(hd - 1) if (hd % 2 == 0 and hd >= 2) else None
                core(hd, chain(pj, op_))
                if hd + 3 < NH:
                    load_w(hd + 3)
                if op_ is not None and hd + 2 < NH:
                    load_wo(hd + 2)
                    load_wo(hd + 3)
            for _ in outp(NH - 1):
                pass

    def s5(self, li, j):
        sy = self.sy
        NG = 64
        NP = 32
        NC = S // 8
        CT = 64
        TWO_PI = 2.0 * math.pi
        with self.scope() as es:
            U2 = self.sb(es, "U2", [128, NG, NC], BF16)
            U2B = [[Buf(f"U2_{gb}_{tt}") for tt in range(NTT)] for gb in range(8)]
            U2all = [U2B[gb][tt] for gb in range(8) for tt in range(NTT)]
            with self.scope() as es1:
                h = self.sb(es1, "hs", [128, NFT, S], BF16)
                hB = [[Buf(f"hs{ft}_{tt}") for tt in range(NTT)] for ft in range(NFT)]
                win = self.sb(es1, "win", [128, NFT, D], BF16)
                winB = Buf("win")
                sy.dma("pool", [lambda e, kt=kt: e.dma_start(out=win[:, kt, :], in_=self.s5w[j, 0, kt * 128:(kt + 1) * 128, :]) for kt in range(NFT)], writes=[winB])
                self.rmsnorm(es1, li, h, hB, inline=True)
                utm = [self.sb(es1, "utm", [128, 8, D], BF16) for _ in range(2)]
                utmB = [[Buf(f"utm{k}_{s_}") for s_ in range(8)] for k in range(2)]
                cnt = 0
                ev = 0
                for cb in range(2):
                    k = cb % 2
                    for s_ in range(8):
                        for fh in range(2):
                            pb = cnt % 4
                            cnt += 1
                            t0 = cb * 1024 + s_
                            sy.group("pe", [lambda e, kt=kt, pb=pb, fh=fh, t0=t0: e.matmul(
                                self.ps[pb][:], lhsT=h[:, kt, t0:t0 + 1017:8], rhs=win[:, kt, fh * 512:(fh + 1) * 512],
                                start=(kt == 0), stop=(kt == NFT - 1)) for kt in range(NFT)],
                                reads=[winB] + [hB[kt][2 * cb] for kt in range(NFT)] + [hB[kt][2 * cb + 1] for kt in range(NFT)], writes=[self.psB[pb]])
                            sy.op("act", lambda e, pb=pb, k=k, s_=s_, fh=fh: e.activation(
                                out=utm[k][:].rearrange("p s (g h) -> p (s g h)", h=16).rearrange("p (g s h) -> p g s h", s=8, h=16)[:, fh * 32:(fh + 1) * 32, s_, :],
                                in_=self.ps[pb][:].rearrange("p (g h) -> p g h", h=16), func=AF.Identity),
                                  reads=[self.psB[pb]], writes=[utmB[k][s_]])
                for cb in range(2):
                    k = cb % 2
                    for gq in range(16):
                        pb = 4 + gq % 2
                        psb = self.ps[pb][:].bitcast(BF16)
                        sy.group("pe", [lambda e, q=q, psb=psb, gq=gq, k=k: e.transpose(
                            psb[:, q * 128:(q + 1) * 128], utm[k][:].rearrange("p s f -> p (s f)")[:, (4 * gq + q) * 128:(4 * gq + q + 1) * 128], self.idb[:]) for q in range(4)],
                            reads=utmB[k] + [self.idbB], writes=[self.psB[pb]])
                        eng = "act" if ev % 2 == 0 else "dve"
                        ev += 1
                        dst = U2[:, 4 * gq:4 * gq + 4, cb * 128:(cb + 1) * 128]
                        src = psb[:, 0:512].rearrange("p (g c) -> p g c", g=4)
                        if eng == "act":
                            sy.op("act", lambda e, dst=dst, src=src: e.activation(out=dst, in_=src, func=AF.Identity),
                                  reads=[self.psB[pb]], writes=[U2B[gq // 2][2 * cb], U2B[gq // 2][2 * cb + 1]])
                        else:
                            sy.op("dve", lambda e, dst=dst, src=src: e.tensor_copy(out=dst, in_=src),
                                  reads=[self.psB[pb]], writes=[U2B[gq // 2][2 * cb], U2B[gq // 2][2 * cb + 1]])
            import os
            dbg = int(os.environ.get("S5DBG", "9"))
            if dbg < 2:
                return
            with self.scope() as es2:
                Tm = self.sb(es2, "Tm", [128, NG, 128], BF16)
                TmB = [Buf(f"Tm{i}") for i in range(16)]
                Btm = self.sb(es2, "Btm", [128, NP, 2, 128], BF16)
                BtB = [Buf(f"Btm{i}") for i in range(16)]
                CtR = self.sb(es2, "CtR", [128, NP, 128], BF16)
                CtI = self.sb(es2, "CtI", [128, NP, 128], BF16)
                CtB = [Buf(f"Ct{i}") for i in range(8)]
                PP1 = self.sb(es2, "PP1", [128, 8, 2, NP])
                PP2 = self.sb(es2, "PP2", [128, 8, 2, NP])
                AB = Buf("A12")
                A1, A2 = PP1[:, 0, :, :], PP2[:, 0, :, :]
                sc = self.s5c[j]
                for h_ in range(4):
                    sy.dma("sp", [lambda e, h_=h_: e.dma_start(out=Btm[:, h_ * 8:(h_ + 1) * 8, :, :].rearrange("p j r c -> p (j r c)"), in_=sc["Bt"][:, h_ * 2048:(h_ + 1) * 2048])],
                           writes=BtB[4 * h_:4 * h_ + 4])
                sy.dma("sp", [lambda e: e.dma_start(out=PP1[:].rearrange("p l a b -> p (l a b)"), in_=sc["A"][:, 0:512]),
                              lambda e: e.dma_start(out=PP2[:].rearrange("p l a b -> p (l a b)"), in_=sc["A"][:, 512:1024])], writes=[AB])

                sy.dma("sp", [lambda e: e.dma_start(out=CtR[:].rearrange("p j c -> p (j c)"), in_=sc["CtR"]),
                              lambda e: e.dma_start(out=CtI[:].rearrange("p j c -> p (j c)"), in_=sc["CtI"])], writes=CtB)
                sy.dma("sp", [lambda e, h_=h_: e.dma_start(out=Tm[:, h_ * 16:(h_ + 1) * 16, :].rearrange("p g c -> p (g c)"), in_=sc["Tm"][:, h_ * 2048:(h_ + 1) * 2048]) for h_ in range(4)], writes=TmB)
                if dbg < 3:
                    return
                St = [self.sb(es2, "St", [128, CT + 1, 2, NP]) for _ in range(2)]
                StB = [Buf("St0"), Buf("St1")]
                Sb = [self.sb(es2, "Sb", [128, CT, 2, NP], BF16) for _ in range(2)]
                SbB = [Buf("Sb0"), Buf("Sb1")]
                tA = self.sb(es2, "tA", [128, 8, 2, NP])
                tB_ = self.sb(es2, "tB", [128, 8, 2, NP])
                tAB, tBB = Buf("tA"), Buf("tB")
                tC = [tA[:, 0:7, :, :], tA[:, 0:7, :, :]]
                tD = [tB_[:, 0:7, :, :], tB_[:, 0:7, :, :]]
                tCB, tDB = [tAB, tAB], [tBB, tBB]
                XlB = [[Buf(f"Xl{k_}_{l_}") for l_ in range(8)] for k_ in range(2)]
                cnt = {"b": 0, "y": 0, "g": 0}

                def stage_a(tt):
                    c0 = tt * CT
                    k = tt % 2
                    for pq in range(NP // 4):
                        pb = cnt["b"] % 2
                        cnt["b"] += 1
                        fns = []
                        for jl in range(4):
                            jp = pq * 4 + jl
                            for e_ in range(2):
                                for ri in range(2):
                                    col = (jl * 2 + ri) * CT
                                    fns.append(lambda e, jp=jp, e_=e_, ri=ri, col=col, pb=pb: e.matmul(
                                        self.ps[pb][64 * e_:64 * e_ + 64, col:col + CT], lhsT=Btm[:, jp, ri, 64 * e_:64 * e_ + 64],
                                        rhs=U2[:, 2 * jp + e_, c0:c0 + CT], start=True, stop=True))
                        sy.group("pe", fns, reads=[BtB[pq * 2], BtB[pq * 2 + 1], U2B[pq][tt]], writes=[self.psB[pb]])
                        sy.op("act", lambda e, pb=pb, pq=pq: e.activation(
                            out=St[k][:, 1:CT + 1, :, pq * 4:pq * 4 + 4].rearrange("p c r j -> p j r c"),
                            in_=self.ps[pb][:].rearrange("p (j r c) -> p j r c", j=4, r=2), func=AF.Identity),
                            reads=[self.psB[pb]], writes=XlB[k])

                def stage_b(tt):
                    k = tt % 2
                    Sk = St[k]
                    Sv = Sk[:, 1:CT + 1, :, :].rearrange("p (b l) r j -> p b l r j", l=8)

                    def swp(ap_slot1_r1, nb):
                        t_ = ap_slot1_r1
                        if nb == 1:
                            return bass.AP(tensor=t_.tensor, offset=t_.offset, ap=[list(t_.ap[0]), [-NP, 2], [1, NP]])
                        return bass.AP(tensor=t_.tensor, offset=t_.offset, ap=[list(t_.ap[0]), [8 * 2 * NP, nb], [-NP, 2], [1, NP]])

                    def step(prev, prev_sw, cur, c1, c2, ta_, tb_, rB, wB):
                        sy.op("dve", lambda e: e.tensor_tensor(out=tb_, in0=prev_sw, in1=c2, op=ALU.mult), reads=rB + [AB], writes=[tBB])
                        sy.op("dve", lambda e: e.tensor_tensor(out=ta_, in0=prev, in1=c1, op=ALU.mult), reads=rB + [AB], writes=[tAB])
                        sy.op("dve", lambda e: e.tensor_tensor(out=tb_, in0=tb_, in1=cur, op=ALU.add), reads=[tBB] + wB, writes=[tBB])
                        sy.op("dve", lambda e: e.tensor_tensor(out=cur, in0=ta_, in1=tb_, op=ALU.add), reads=[tAB, tBB], writes=wB)

                    c0B = StB[k]
                    if tt == 0:
                        sy.op("dve", lambda e: e.memset(Sk[:, 0, :, :], 0.0), writes=[c0B])
                    else:
                        sy.op("dve", lambda e: e.tensor_copy(out=Sk[:, 0, :, :], in_=St[1 - k][:, CT, :, :]), reads=[XlB[1 - k][7]], writes=[c0B])
                    step(Sk[:, 0, :, :], swp(Sk[:, 0, 1, :], 1), Sk[:, 1, :, :], A1, A2, tA[:, 0, :, :], tB_[:, 0, :, :], [c0B], [XlB[k][0]])
                    A1b = A1.unsqueeze(1).broadcast_to([128, 8, 2, NP])
                    A2b = A2.unsqueeze(1).broadcast_to([128, 8, 2, NP])
                    for l in range(1, 8):
                        step(Sv[:, :, l - 1, :, :], swp(Sv[:, 0, l - 1, 1, :], 8), Sv[:, :, l, :, :], A1b, A2b, tA[:], tB_[:], [XlB[k][l - 1]], [XlB[k][l]])
                    for blk in range(1, 8):
                        step(Sv[:, blk - 1, 7, :, :], swp(Sv[:, blk - 1, 7, 1, :], 1), Sv[:, blk, 7, :, :], PP1[:, 7, :, :], PP2[:, 7, :, :],
                             tA[:, 0, :, :], tB_[:, 0, :, :], [XlB[k][7]], [XlB[k][7]])
                    Cv = Sv[:, 0:7, 7, :, :]
                    Cs = swp(Sv[:, 0, 7, 1, :], 7)
                    for l in range(7):
                        q = l % 2
                        p1 = PP1[:, l, :, :].unsqueeze(1).broadcast_to([128, 7, 2, NP])
                        p2 = PP2[:, l, :, :].unsqueeze(1).broadcast_to([128, 7, 2, NP])
                        cur = Sv[:, 1:8, l, :, :]
                        sy.op("dve", lambda e, q=q, p2=p2: e.tensor_tensor(out=tD[q], in0=Cs, in1=p2, op=ALU.mult), reads=[XlB[k][7], AB], writes=[tDB[q]])
                        sy.op("dve", lambda e, q=q, p1=p1: e.tensor_tensor(out=tC[q], in0=Cv, in1=p1, op=ALU.mult), reads=[XlB[k][7], AB], writes=[tCB[q]])
                        sy.op("dve", lambda e, q=q, cur=cur: e.tensor_tensor(out=tD[q], in0=tD[q], in1=cur, op=ALU.add), reads=[tDB[q], XlB[k][l]], writes=[tDB[q]])
                        sy.op("dve", lambda e, q=q, cur=cur: e.tensor_tensor(out=cur, in0=tC[q], in1=tD[q], op=ALU.add), reads=[tCB[q], tDB[q]], writes=[XlB[k][l]])

                def stage_c(tt):
                    c0 = tt * CT
                    k = tt % 2
                    sy.op("act", lambda e: e.activation(out=Sb[k][:], in_=St[k][:, 0:CT, :, :], func=AF.Identity), reads=[StB[k]] + XlB[k], writes=[SbB[k]])
                    for gb in range(8):
                        pb = 2 + cnt["y"] % 2
                        cnt["y"] += 1
                        fns = []
                        for gl in range(8):
                            g = gb * 8 + gl
                            jp, e_ = g // 2, g % 2
                            out = self.ps[pb][:, gl * CT:(gl + 1) * CT]
                            fns.append(lambda e, g=g, out=out: e.matmul(out, lhsT=Tm[:, g, :], rhs=U2[:, g, c0:c0 + CT], start=True, stop=False))
                            fns.append(lambda e, jp=jp, e_=e_, out=out: e.matmul(out, lhsT=CtR[64 * e_:64 * e_ + 64, jp, :], rhs=Sb[k][64 * e_:64 * e_ + 64, :, 0, jp],
                                                                              start=False, stop=False))
                            fns.append(lambda e, jp=jp, e_=e_, out=out: e.matmul(out, lhsT=CtI[64 * e_:64 * e_ + 64, jp, :], rhs=Sb[k][64 * e_:64 * e_ + 64, :, 1, jp],
                                                                              start=False, stop=True))
                        sy.group("pe", fns, reads=[TmB[gb * 2], TmB[gb * 2 + 1], CtB[gb], U2B[gb][tt], SbB[k]], writes=[self.psB[pb]])
                        ps = self.ps[pb]
                        sy.op("act", lambda e, ps=ps, gb=gb: e.activation(
                            out=U2[:, gb * 8:(gb + 1) * 8, c0:c0 + CT], in_=ps[:].rearrange("p (g c) -> p g c", g=8), func=AF.Gelu_apprx_tanh),
                            reads=[self.psB[pb]], writes=[U2B[gb][tt]])

                stage_a(0)
                stage_b(0)
                for tt in range(NTT):
                    if tt + 1 < NTT:
                        stage_a(tt + 1)
                        stage_b(tt + 1)
                    stage_c(tt)
            if dbg < 5:
                return
            with self.scope() as es3:
                zfm = self.sb(es3, "zfm", [128, NFT, 2, 8, 128], BF16)
                zfB = [[Buf(f"zfm{fo}_{cb}") for cb in range(2)] for fo in range(NFT)]
                ztm = [self.sb(es3, "ztm", [128, D], BF16) for _ in range(2)]
                ztmB = [Buf("ztm0"), Buf("ztm1")]
                czc = {"n": 0}

                def tr(cb, fo):
                    cz = czc["n"]
                    czc["n"] += 1
                    k = cz % 2
                    pa = 4 + cz % 2
                    pbk = 6 + cz % 2
                    psa = self.ps[pa][:].bitcast(BF16)
                    psb = self.ps[pbk][:].bitcast(BF16)
                    sy.group("pe", [lambda e, g8=g8: e.transpose(
                        psa[:, g8 * 128:(g8 + 1) * 128], U2[:, fo * 8 + g8, cb * 128:(cb + 1) * 128], self.idb[:]) for g8 in range(8)],
                        reads=[U2B[fo][2 * cb], U2B[fo][2 * cb + 1], self.idbB], writes=[self.psB[pa]])
                    sy.op("act", lambda e: e.activation(out=ztm[k][:].rearrange("p (t g h) -> p g t h", g=8, t=8),
                                                       in_=psa.rearrange("p (g t h) -> p g t h", g=8, t=8), func=AF.Identity),
                          reads=[self.psB[pa]], writes=[ztmB[k]])
                    sy.group("pe", [lambda e, t=t: e.transpose(psb[:, t * 128:(t + 1) * 128], ztm[k][:, t * 128:(t + 1) * 128], self.idb[:]) for t in range(8)],
                             reads=[ztmB[k], self.idbB], writes=[self.psB[pbk]])
                    sy.op("dve", lambda e: e.tensor_copy(out=zfm[:, fo, cb, :, :].rearrange("p t c -> p (t c)"), in_=psb),
                          reads=[self.psB[pbk]], writes=[zfB[fo][cb]])

                for fo in range(NFT):
                    tr(0, fo)
                wg = self.sb(es3, "wg", [128, NFT, D], BF16)
                wo_ = self.sb(es3, "wo5", [128, NFT, D], BF16)
                wgB, woB = Buf("wg"), Buf("wo5")
                sy.dma("pool", [lambda e, kt=kt: e.dma_start(out=wg[:, kt, :], in_=self.s5w[j, 1, kt * 128:(kt + 1) * 128, :]) for kt in range(NFT)], writes=[wgB])
                sy.dma("pool", [lambda e, kt=kt: e.dma_start(out=wo_[:, kt, :], in_=self.s5w[j, 2, kt * 128:(kt + 1) * 128, :]) for kt in range(NFT)], writes=[woB])
                z2 = [self.sb(es3, "z2", [128, NFT, TT], BF16) for _ in range(2)]
                z2B = [[Buf(f"z2_{b}_{fo}") for fo in range(NFT)] for b in range(2)]
                sg = [self.sb(es3, "sg", [128, TT]) for _ in range(2)]
                sgB = [Buf("sg0"), Buf("sg1")]
                cnt = {"a": 0, "b": 0, "s": 0}

                tiles = [(0, 0), (0, 1), (1, 0), (1, 1)]

                def zcols(kt, ti):
                    cb, th = tiles[ti]
                    return zfm[:, kt, cb, 4 * th:4 * th + 4, :].rearrange("p t c -> p (t c)")

                def gate(ti):
                    b = ti % 2
                    cb, th = tiles[ti]
                    for fo in range(NFT):
                        pb = cnt["a"] % 2
                        cnt["a"] += 1
                        k = cnt["s"] % 2
                        cnt["s"] += 1
                        sy.group("pe", [lambda e, kt=kt, pb=pb, fo=fo: e.matmul(
                            self.ps[pb][:], lhsT=wg[:, kt, fo * 128:(fo + 1) * 128], rhs=zcols(kt, ti),
                            start=(kt == 0), stop=(kt == NFT - 1)) for kt in range(NFT)],
                            reads=[wgB] + [zfB[kt][cb] for kt in range(NFT)], writes=[self.psB[pb]])
                        sy.op("act", lambda e, pb=pb, k=k: e.activation(out=sg[k][:], in_=self.ps[pb][:], func=AF.Sigmoid), reads=[self.psB[pb]], writes=[sgB[k]])
                        sy.op("pool", lambda e, k=k, fo=fo: e.tensor_tensor(out=z2[b][:, fo, :], in0=zcols(fo, ti), in1=sg[k][:], op=ALU.mult),
                              reads=[sgB[k], zfB[fo][cb]], writes=[z2B[b][fo]])

                def outp(ti):
                    b = ti % 2
                    cb, th = tiles[ti]
                    xv = self.x[:].rearrange("p f (c s) -> p f s c", s=8)
                    for fo in range(NFT):
                        pb = 2 + cnt["b"] % 2
                        cnt["b"] += 1
                        sy.group("pe", [lambda e, kt=kt, pb=pb, fo=fo: e.matmul(
                            self.ps[pb][:], lhsT=wo_[:, kt, fo * 128:(fo + 1) * 128], rhs=z2[b][:, kt, :], start=(kt == 0), stop=(kt == NFT - 1))
                            for kt in range(NFT)], reads=[woB] + z2B[b], writes=[self.psB[pb]])
                        xs = xv[:, fo, 4 * th:4 * th + 4, cb * 128:(cb + 1) * 128]
                        sy.op("dve", lambda e, pb=pb, xs=xs: e.tensor_tensor(out=xs, in0=xs, in1=self.ps[pb][:].rearrange("p (t c) -> p t c", t=4), op=ALU.add),
                              reads=[self.psB[pb]] + self.xB[fo], writes=self.xB[fo])

                gate(0)
                for fo in range(0, 4):
                    tr(1, fo)
                gate(1)
                outp(0)
                for fo in range(4, 8):
                    tr(1, fo)
                gate(2)
                outp(1)
                gate(3)
                outp(2)
                outp(3)

    def s5_precompute(self, j):
        nc, sy = self.nc, self.sy
        NG, NP = 64, 32
        sc = {
            "Tm": nc.dram_tensor(f"s5c_Tm{j}", [128, NG * 128], BF16, kind="Internal").ap(),
            "Bt": nc.dram_tensor(f"s5c_Bt{j}", [128, NP * 2 * 128], BF16, kind="Internal").ap(),
            "CtR": nc.dram_tensor(f"s5c_CtR{j}", [128, NP * 128], BF16, kind="Internal").ap(),
            "CtI": nc.dram_tensor(f"s5c_CtI{j}", [128, NP * 128], BF16, kind="Internal").ap(),
            "A": nc.dram_tensor(f"s5c_A{j}", [128, 1024], F32, kind="Internal").ap(),
        }
        self.s5c[j] = sc
        with self.scope() as es2:
            Tm = self.sb(es2, "Tm", [128, NG, 128], BF16)
            TmB = [Buf(f"Tm{i}") for i in range(16)]
            Btm = self.sb(es2, "Btm", [128, NP, 2, 128], BF16)
            BtB = [Buf(f"Btm{i}") for i in range(16)]
            CtR = self.sb(es2, "CtR", [128, NP, 128], BF16)
            CtI = self.sb(es2, "CtI", [128, NP, 128], BF16)
            CtB = [Buf(f"Ct{i}") for i in range(8)]
            A1 = self.sb(es2, "A1", [128, 8, 2, NP])
            A2 = self.sb(es2, "A2", [128, 8, 2, NP])
            AB = Buf("A12")
            self.s5_prep(es2, j, Tm, TmB, Btm, BtB, CtR, CtI, CtB, A1, A2, AB)
            sy.dma("sp", [lambda e, h_=h_: e.dma_start(out=sc["Tm"][:, h_ * 2048:(h_ + 1) * 2048], in_=Tm[:, h_ * 16:(h_ + 1) * 16, :].rearrange("p g c -> p (g c)")) for h_ in range(4)], reads=TmB)
            sy.dma("sp", [lambda e, h_=h_: e.dma_start(out=sc["Bt"][:, h_ * 2048:(h_ + 1) * 2048], in_=Btm[:, h_ * 8:(h_ + 1) * 8, :, :].rearrange("p j r c -> p (j r c)")) for h_ in range(4)], reads=BtB)
            sy.dma("sp", [lambda e: e.dma_start(out=sc["CtR"], in_=CtR[:].rearrange("p j c -> p (j c)")),
                          lambda e: e.dma_start(out=sc["CtI"], in_=CtI[:].rearrange("p j c -> p (j c)"))], reads=CtB)
            sy.dma("sp", [lambda e: e.dma_start(out=sc["A"][:, 0:512], in_=A1[:].rearrange("p l a b -> p (l a b)")),
                          lambda e: e.dma_start(out=sc["A"][:, 512:1024], in_=A2[:].rearrange("p l a b -> p (l a b)"))], reads=[AB])

    def s5_prep(self, es_out, j, Tm, TmB, Btm, BtB, CtR, CtI, CtB, A1, A2, AB):
        sy = self.sy
        NP = 32
        TWO_PI = 2.0 * math.pi
        with self.scope() as es:
            prm = self.sb(es, "prm", [128, S5NC])
            prmB = Buf("prm")
            sy.dma("sp", [lambda e: e.dma_start(out=prm[:], in_=self.s5p[j])], writes=[prmB])
            lre, lim, lst = prm[:, 0:32], prm[:, 32:64], prm[:, 64:96]
            bre = prm[:, 96:608].rearrange("p (j h) -> p j h", h=16)
            bim = prm[:, 608:1120].rearrange("p (j h) -> p j h", h=16)
            cre = prm[:, 1120:1632].rearrange("p (j h) -> p j h", h=16)
            cim = prm[:, 1632:2144].rearrange("p (j h) -> p j h", h=16)
            dvec = prm[:, 2144:2208]
            names = ["lr", "dt", "lrdt", "lidt", "nr", "den", "fr", "fi", "w1", "w2"]
            sm = {n: self.sb(es, "s5" + n, [128, NP]) for n in names}
            smB = {n: Buf("s5" + n) for n in names}

            def tt_(eng, out, oB, a, aB, b, bB, op):
                sy.op(eng, lambda e: e.tensor_tensor(out=out, in0=a, in1=b, op=op), reads=[aB, bB], writes=[oB])

            def V(n):
                return sm[n][:]
            sy.op("dve", lambda e: e.tensor_scalar(out=V("lr"), in0=lre, scalar1=-1e-4, scalar2=None, op0=ALU.min), reads=[prmB], writes=[smB["lr"]])
            sy.op("act", lambda e: e.activation(out=V("dt"), in_=lst, func=AF.Exp), reads=[prmB], writes=[smB["dt"]])
            tt_("dve", V("lrdt"), smB["lrdt"], V("lr"), smB["lr"], V("dt"), smB["dt"], ALU.mult)
            tt_("dve", V("lidt"), smB["lidt"], lim, prmB, V("dt"), smB["dt"], ALU.mult)
            NK = 24
            kv = self.sb(es, "kv", [128, NK])
            kvB = Buf("kv")
            sy.op("pool", lambda e: e.iota(kv[:, 0:16], pattern=[[1, 16]], base=-7, channel_multiplier=0, allow_small_or_imprecise_dtypes=True), writes=[kvB])
            sy.op("pool", lambda e: e.iota(kv[:, 16:NK], pattern=[[8, NK - 16]], base=16, channel_multiplier=0, allow_small_or_imprecise_dtypes=True), writes=[kvB])
            big = {n: self.sb(es, "s5" + n, [128, NK, NP]) for n in ["mag", "ang", "nf", "AR", "AI"]}
            bigB = {n: Buf("s5" + n) for n in big}
            ni = self.sb(es, "s5ni", [128, NK, NP], I32)
            niB = Buf("s5ni")
            kvb = kv[:].unsqueeze(2).broadcast_to([128, NK, NP])
            sy.op("dve", lambda e: e.tensor_tensor(out=big["mag"][:], in0=kvb, in1=V("lrdt").unsqueeze(1).broadcast_to([128, NK, NP]), op=ALU.mult),
                  reads=[kvB, smB["lrdt"]], writes=[bigB["mag"]])
            sy.op("act", lambda e: e.activation(out=big["mag"][:], in_=big["mag"][:], func=AF.Exp), reads=[bigB["mag"]], writes=[bigB["mag"]])
            sy.op("dve", lambda e: e.tensor_tensor(out=big["ang"][:], in0=kvb, in1=V("lidt").unsqueeze(1).broadcast_to([128, NK, NP]), op=ALU.mult),
                  reads=[kvB, smB["lidt"]], writes=[bigB["ang"]])

            def sin_of(dst, dstB, shift):
                nf, nfB, ang, angB = big["nf"], bigB["nf"], big["ang"], bigB["ang"]
                sy.op("dve", lambda e: e.tensor_scalar(out=nf[:], in0=ang[:], scalar1=shift, scalar2=1.0 / TWO_PI, op0=ALU.add, op1=ALU.mult), reads=[angB], writes=[nfB])
                sy.op("dve", lambda e: e.tensor_copy(out=ni[:], in_=nf[:]), reads=[nfB], writes=[niB])
                sy.op("dve", lambda e: e.tensor_copy(out=nf[:], in_=ni[:]), reads=[niB], writes=[nfB])
                sy.op("dve", lambda e: e.scalar_tensor_tensor(out=nf[:], in0=nf[:], scalar=-TWO_PI, in1=ang[:], op0=ALU.mult, op1=ALU.add), reads=[nfB, angB], writes=[nfB])
                sy.op("dve", lambda e: e.tensor_scalar(out=nf[:], in0=nf[:], scalar1=shift, scalar2=math.pi, op0=ALU.add, op1=ALU.min), reads=[nfB], writes=[nfB])
                sy.op("dve", lambda e: e.tensor_scalar(out=nf[:], in0=nf[:], scalar1=-math.pi, scalar2=None, op0=ALU.max), reads=[nfB], writes=[nfB])
                sy.op("act", lambda e: e.activation(out=dst[:], in_=nf[:], func=AF.Sin), reads=[nfB], writes=[dstB])
            sin_of(big["AI"], bigB["AI"], 0.0)
            sin_of(big["AR"], bigB["AR"], math.pi / 2)
            for n in ("AI", "AR"):
                sy.op("dve", lambda e, n=n: e.tensor_tensor(out=big[n][:], in0=big[n][:], in1=big["mag"][:], op=ALU.mult), reads=[bigB[n], bigB["mag"]], writes=[bigB[n]])
            AR, AI, ARB, AIB = big["AR"], big["AI"], bigB["AR"], bigB["AI"]
            import os
            pd_ = int(os.environ.get("PREPDBG", "9"))
            if pd_ < 2:
                return
            for r_ in range(2):
                sy.op("dve", lambda e, r_=r_: e.tensor_copy(out=A1[:, :, r_, :], in_=AR[:, 15:23, :]), reads=[ARB], writes=[AB])
            sy.op("dve", lambda e: e.tensor_copy(out=A2[:, :, 1, :], in_=AI[:, 15:23, :]), reads=[AIB], writes=[AB])
            sy.op("dve", lambda e: e.tensor_scalar(out=A2[:, :, 0, :], in0=AI[:, 15:23, :], scalar1=-1.0, scalar2=None, op0=ALU.mult), reads=[AIB], writes=[AB])
            a1r, a1i = AR[:, 8, :], AI[:, 8, :]
            sy.op("dve", lambda e: e.tensor_scalar(out=V("nr"), in0=a1r, scalar1=-1.0, scalar2=None, op0=ALU.add), reads=[ARB], writes=[smB["nr"]])
            tt_("dve", V("den"), smB["den"], V("lr"), smB["lr"], V("lr"), smB["lr"], ALU.mult)
            tt_("dve", V("w1"), smB["w1"], lim, prmB, lim, prmB, ALU.mult)
            tt_("dve", V("den"), smB["den"], V("den"), smB["den"], V("w1"), smB["w1"], ALU.add)
            sy.op("dve", lambda e: e.reciprocal(out=V("den"), in_=V("den")), reads=[smB["den"]], writes=[smB["den"]])
            tt_("dve", V("w1"), smB["w1"], V("nr"), smB["nr"], V("lr"), smB["lr"], ALU.mult)
            tt_("dve", V("w2"), smB["w2"], a1i, AIB, lim, prmB, ALU.mult)
            tt_("dve", V("fr"), smB["fr"], V("w1"), smB["w1"], V("w2"), smB["w2"], ALU.add)
            tt_("dve", V("fr"), smB["fr"], V("fr"), smB["fr"], V("den"), smB["den"], ALU.mult)
            tt_("dve", V("w1"), smB["w1"], a1i, AIB, V("lr"), smB["lr"], ALU.mult)
            tt_("dve", V("w2"), smB["w2"], V("nr"), smB["nr"], lim, prmB, ALU.mult)
            tt_("dve", V("fi"), smB["fi"], V("w1"), smB["w1"], V("w2"), smB["w2"], ALU.subtract)
            tt_("dve", V("fi"), smB["fi"], V("fi"), smB["fi"], V("den"), smB["den"], ALU.mult)
            BBr = self.sb(es, "BBr", [128, NP, 16])
            BBi = self.sb(es, "BBi", [128, NP, 16])
            BBrB, BBiB = Buf("BBr"), Buf("BBi")
            IZ = self.sb(es, "IZ", [128, 2, 256])
            IZB = Buf("IZ")
            w3a_t = IZ[:].rearrange("p a b -> p (a b)").rearrange("p (j h) -> p j h", h=16)

            class _V:
                def __init__(self, ap): self.ap_ = ap
                def __getitem__(self, k): return self.ap_
            w3a = _V(w3a_t)
            w3b = self.sb(es, "w3b", [128, NP, 16])
            w3aB, w3bB = IZB, Buf("w3b")
            frb = V("fr").unsqueeze(2).broadcast_to([128, NP, 16])
            fib = V("fi").unsqueeze(2).broadcast_to([128, NP, 16])
            sy.op("dve", lambda e: e.tensor_tensor(out=w3a[:], in0=bre, in1=frb, op=ALU.mult), reads=[prmB, smB["fr"]], writes=[w3aB])
            sy.op("dve", lambda e: e.tensor_tensor(out=w3b[:], in0=bim, in1=fib, op=ALU.mult), reads=[prmB, smB["fi"]], writes=[w3bB])
            tt_("dve", BBr[:], BBrB, w3a[:], w3aB, w3b[:], w3bB, ALU.subtract)
            sy.op("dve", lambda e: e.tensor_tensor(out=w3a[:], in0=bim, in1=frb, op=ALU.mult), reads=[prmB, smB["fr"]], writes=[w3aB])
            sy.op("dve", lambda e: e.tensor_tensor(out=w3b[:], in0=bre, in1=fib, op=ALU.mult), reads=[prmB, smB["fi"]], writes=[w3bB])
            tt_("dve", BBi[:], BBiB, w3a[:], w3aB, w3b[:], w3bB, ALU.add)
            mask = self.sb(es, "msk", [128, 4, 8, 16])
            maskB = Buf("msk")
            sy.op("pool", lambda e: e.memset(mask[:], 1.0), writes=[maskB])
            sy.op("pool", lambda e: e.affine_select(out=mask[:], in_=mask[:], pattern=[[0, 4], [16, 8], [0, 16]], compare_op=ALU.is_ge, fill=0.0,
                                                   base=15, channel_multiplier=-1), reads=[maskB], writes=[maskB])
            ident = self.sb(es, "idn", [128, 128])
            identB = Buf("idn")
            sy.op("pool", lambda e: e.memset(ident[:], 0.0), writes=[identB])
            sy.op("pool", lambda e: e.affine_select(out=ident[:], in_=ident[:], pattern=[[-1, 128]], compare_op=ALU.not_equal, fill=1.0, base=0, channel_multiplier=1),
                  reads=[identB], writes=[identB])
            if pd_ < 3:
                return
            PBN = 4
            shp = [128, PBN, 8, 16]
            shp9 = [128, PBN, 9, 16]
            tmp_sets = [{n: self.sb(es, "s5t" + n, shp9 if n in ("CpR", "CpI") else shp) for n in ["BnR", "BnI", "BtR", "BtI", "CpR", "CpI"]}
                        for _ in range(2)]
            tmpB_sets = [{n: Buf("s5t" + n) for n in tmp_sets[0]} for _ in range(2)]
            scr = {n: self.sb(es, "s5t" + n, shp9 if n in ("x1", "x2") else shp) for n in ["x1", "x2", "y1", "y2"]}
            scrB = {n: Buf("s5t" + n) for n in scr}
            for d_, dB_ in zip(tmp_sets, tmpB_sets):
                d_.update(scr)
                dB_.update(scrB)
            tmp, tmpB = dict(tmp_sets[0]), dict(tmpB_sets[0])
            CpZ_sets = []
            for _ in range(2):
                zr = self.sb(es, "CpZR", [128, PBN, 2, 128])
                zi = self.sb(es, "CpZI", [128, PBN, 2, 128])
                zb = Buf("CpZ")
                sy.op("pool", lambda e, zr=zr: e.memset(zr[:], 0.0), writes=[zb])
                sy.op("pool", lambda e, zi=zi: e.memset(zi[:], 0.0), writes=[zb])
                CpZ_sets.append((zr, zi, zb))
            cnt = 0

            def cprod(eng, x1, x2, k0, Zr, Zi, ZB, p0, outR, outI, negI, outRB, outIB, npow=8):
                kk0, sgn = k0
                def pw(A):
                    t = A[:, kk0, p0:p0 + PBN]
                    return bass.AP(tensor=t.tensor, offset=t.offset, ap=[list(t.ap[0]), [1, PBN], [sgn * NP, npow], [0, 16]])
                def zz(Z):
                    t = Z[:, p0:p0 + PBN, :]
                    return bass.AP(tensor=t.tensor, offset=t.offset, ap=[list(t.ap[0]), [16, PBN], [0, npow], [1, 16]])
                X1, X2 = tmp[x1][:, :, 0:npow, :], tmp[x2][:, :, 0:npow, :]
                sy.op(eng, lambda e: e.tensor_tensor(out=X1[:], in0=pw(AR), in1=zz(Zr), op=ALU.mult), reads=[ARB] + ZB, writes=[tmpB[x1]])
                sy.op(eng, lambda e: e.tensor_tensor(out=X2[:], in0=pw(AI), in1=zz(Zi), op=ALU.mult), reads=[AIB] + ZB, writes=[tmpB[x2]])
                sy.op(eng, lambda e: e.tensor_tensor(out=outR, in0=X1[:], in1=X2[:], op=ALU.subtract), reads=[tmpB[x1], tmpB[x2]], writes=[outRB])
                sy.op(eng, lambda e: e.tensor_tensor(out=X1[:], in0=pw(AR), in1=zz(Zi), op=ALU.mult), reads=[ARB] + ZB, writes=[tmpB[x1]])
                sy.op(eng, lambda e: e.tensor_tensor(out=X2[:], in0=pw(AI), in1=zz(Zr), op=ALU.mult), reads=[AIB] + ZB, writes=[tmpB[x2]])
                if negI:
                    sy.op(eng, lambda e: e.tensor_scalar(out=X1[:], in0=X1[:], scalar1=-1.0, scalar2=None, op0=ALU.mult), reads=[tmpB[x1]], writes=[tmpB[x1]])
                    sy.op(eng, lambda e: e.tensor_tensor(out=outI, in0=X1[:], in1=X2[:], op=ALU.subtract), reads=[tmpB[x1], tmpB[x2]], writes=[outIB])
                else:
                    sy.op(eng, lambda e: e.tensor_tensor(out=outI, in0=X1[:], in1=X2[:], op=ALU.add), reads=[tmpB[x1], tmpB[x2]], writes=[outIB])

            CB = [prmB]
            sy.op("pool", lambda e: e.memset(IZ[:], 0.0), writes=[IZB])
            sy.op("pool", lambda e: e.tensor_copy(out=IZ[:, 0, 0:128], in_=ident[:]), reads=[identB], writes=[IZB])
            sy.op("pool", lambda e: e.tensor_copy(out=IZ[:, 1, 128:256], in_=ident[:]), reads=[identB], writes=[IZB])
            Lb = [self.sb(es, "Lb", [128, 128]) for _ in range(4)]
            LbB = [Buf(f"Lb{i}") for i in range(4)]
            prev_evacs = []
            for bi in range(NP // PBN):
                p0 = bi * PBN
                tmp.update(tmp_sets[bi % 2])
                tmpB.update(tmpB_sets[bi % 2])
                CpZR, CpZI, CpZB = CpZ_sets[bi % 2]
                POOL_ = "pool" if pd_ != 5 else "dve"
                cprod("dve", "x1", "x2", (7, -1), BBr, BBi, [BBrB, BBiB], p0, tmp["BnR"][:], tmp["BnI"][:], False, tmpB["BnR"], tmpB["BnI"])
                if pd_ < 4:
                    continue
                cprod(POOL_, "y1", "y2", (14, -1), BBr, BBi, [BBrB, BBiB], p0, tmp["BtR"][:], tmp["BtI"][:], False, tmpB["BtR"], tmpB["BtI"])
                cprod("dve", "x1", "x2", (7, 1), cre, cim, CB, p0, tmp["CpR"][:], tmp["CpI"][:], True, tmpB["CpR"], tmpB["CpI"], npow=9)
                ctb = CtB[0]
                for nm, dst in (("CpR", CtR), ("CpI", CtI)):
                    sy.op("act", lambda e, nm=nm, dst=dst: e.activation(out=dst[:, p0:p0 + PBN, :].rearrange("p j (t h) -> p j t h", h=16),
                                                                      in_=tmp[nm][:, :, 1:9, :], func=AF.Identity),
                          reads=[tmpB[nm]], writes=[ctb])
                if pd_ < 6:
                    continue
                for nm, zn in (("CpR", CpZR), ("CpI", CpZI)):
                    for e_ in range(2):
                        r0, r1 = 64 * e_, 64 * e_ + 64
                        sy.op("act", lambda e, nm=nm, zn=zn, e_=e_, r0=r0, r1=r1: e.activation(
                            out=zn[r0:r1, :, e_, :].rearrange("p j (s h) -> p j s h", h=16), in_=tmp[nm][r0:r1, :, 0:8, :], func=AF.Identity),
                            reads=[tmpB[nm]], writes=[CpZB])
                evacs = []
                for half in range(PBN // 2):
                    pb = 2 * (bi % 2) + half
                    fns = []
                    gs = []
                    for q in range(2):
                        jl = half * 2 + q
                        gs += [2 * (p0 + jl), 2 * (p0 + jl) + 1]
                        out = self.ps[pb][:, q * 256:(q + 1) * 256]
                        fns.append(lambda e, jl=jl, out=out: e.matmul(
                            out, lhsT=tmp["BnR"][:, jl, :, :].rearrange("p s h -> p (s h)"), rhs=CpZR[:, jl, :, :].rearrange("p e c -> p (e c)"),
                            start=True, stop=False))
                        fns.append(lambda e, jl=jl, out=out: e.matmul(
                            out, lhsT=tmp["BnI"][:, jl, :, :].rearrange("p s h -> p (s h)"), rhs=CpZI[:, jl, :, :].rearrange("p e c -> p (e c)"),
                            start=False, stop=False))
                        for e_ in range(2):
                            g = 2 * (p0 + jl) + e_
                            li_ = q * 2 + e_
                            sy.op("act", lambda e, g=g, li_=li_: e.activation(out=Lb[li_][:], in_=ident[:], func=AF.Identity, scale=dvec[:, g:g + 1]),
                                  reads=[identB, prmB], writes=[LbB[li_]])
                            fns.append(lambda e, out=out, li_=li_, e_=e_: e.matmul(out, lhsT=Lb[li_][:], rhs=IZ[:, e_, :], start=False, stop=(e_ == 1)))
                    sy.group("pe", fns, reads=[tmpB["BnR"], tmpB["BnI"], CpZB, IZB] + LbB, writes=[self.psB[pb]])

                    def ev_t(pb=pb, gs=gs):
                        sy.op("dve", lambda e: e.tensor_tensor(out=Tm[:, gs[0]:gs[0] + 4, :].rearrange("p g c -> p (g c)"), in0=self.ps[pb][:],
                                                               in1=mask[:].rearrange("p a t h -> p (a t h)"), op=ALU.mult),
                              reads=[self.psB[pb], maskB], writes=[TmB[gs[0] // 4]])
                    evacs.append(ev_t)
                for hp in range(PBN // 2):
                    pb = 4 + 2 * (bi % 2) + hp
                    fns = []
                    for q in range(2):
                        jl = hp * 2 + q
                        for ri, nm in enumerate(("BtR", "BtI")):
                            col = (q * 2 + ri) * 128
                            fns.append(lambda e, jl=jl, nm=nm, col=col, pb=pb: e.transpose(
                                self.ps[pb][:, col:col + 128], tmp[nm][:, jl, :, :].rearrange("p s h -> p (s h)"), ident[:]))
                    sy.group("pe", fns, reads=[tmpB["BtR"], tmpB["BtI"], identB], writes=[self.psB[pb]])
                    jp0 = p0 + hp * 2

                    def ev_b(pb=pb, jp0=jp0):
                        sy.op("act", lambda e: e.activation(out=Btm[:, jp0:jp0 + 2, :, :].rearrange("p j r c -> p (j r c)"), in_=self.ps[pb][:], func=AF.Identity),
                              reads=[self.psB[pb]], writes=[BtB[jp0 // 2]])
                    evacs.append(ev_b)
                for f_ in prev_evacs:
                    f_()
                prev_evacs = evacs
            for f_ in prev_evacs:
                f_()

    def final(self, s, next_s=None):
        sy = self.sy
        with self.scope() as es:
            sq = [self.sb(es, "fsq", [128, NFT, TT], BF16) for _ in range(2)]
            sqB = [Buf("fsq0"), Buf("fsq1")]
            rs = [self.sb(es, "frs", [128, TT]) for _ in range(2)]
            rsB = [Buf("frs0"), Buf("frs1")]
            ob = [self.sb(es, "fo", [128, NFT, TT]) for _ in range(2)]
            obB = [Buf("fo0"), Buf("fo1")]

            def square(tt):
                b = tt % 2
                xin = self.x[:, :, tt * TT:(tt + 1) * TT]
                sy.op("act", lambda e: e.activation(out=sq[b][:], in_=xin, func=AF.Square),
                      reads=[self.xB[ft][tt] for ft in range(NFT)], writes=[sqB[b]])

            square(0)
            for tt in range(NTT):
                b = tt % 2
                if tt + 1 < NTT:
                    square(tt + 1)
                pb = 6 + b
                sy.group("pe", [lambda e, ft=ft, pb=pb, b=b: e.matmul(self.ps[pb][:], lhsT=self.ones_bf[:], rhs=sq[b][:, ft, :],
                                                                      start=(ft == 0), stop=(ft == NFT - 1)) for ft in range(NFT)],
                         reads=[sqB[b], self.onesB], writes=[self.psB[pb]])
                sy.op("act", lambda e, pb=pb, b=b: e.activation(out=rs[b][:], in_=self.ps[pb][:], func=AF.Ln, scale=1.0 / D, bias=self.epsT[:]),
                      reads=[self.psB[pb], self.epsB], writes=[rsB[b]])
                sy.op("act", lambda e, b=b: e.activation(out=rs[b][:], in_=rs[b][:], func=AF.Exp, scale=-0.5), reads=[rsB[b]], writes=[rsB[b]])
                for ft in range(NFT):
                    sy.op("dve", lambda e, b=b, ft=ft, tt=tt: e.scalar_tensor_tensor(
                        out=ob[b][:, ft, :], in0=self.xt(ft, tt), scalar=self.gcol(8, ft), in1=rs[b][:],
                        op0=ALU.mult, op1=ALU.mult),
                        reads=[self.xB[ft][tt], rsB[b], self.gB], writes=[obB[b]])
                sy.dma("sp", [lambda e, b=b, tt=tt: e.dma_start(
                    out=self.yT[s, :, tt * TT:(tt + 1) * TT].rearrange("(ft p) t -> p ft t", p=128), in_=ob[b][:])],
                    reads=[obB[b]])
                if next_s is not None:
                    for ft in range(NFT):
                        sy.dma("sp", [lambda e, ft=ft, tt=tt: e.dma_start(out=self.xt(ft, tt), in_=self.xT[next_s, ft * 128:(ft + 1) * 128, tt * TT:(tt + 1) * TT])],
                               writes=[self.xB[ft][tt]])


def _vecs(inp):
    allg = np.concatenate([inp["norm_mix"], inp["norm_mlp"], np.asarray(inp["norm_final"])[None]], 0)
    v = np.zeros((128, NV), np.float32)
    v[:, 0:72] = allg.reshape(9, NFT, 128).transpose(2, 0, 1).reshape(128, 72)
    v[:, V_PSCALE:V_PSCALE + 8] = np.asarray(inp["pool_scale"])[0].reshape(NFT, 128).T
    v[:, V_SUBLN] = np.asarray(inp["da_subln"])[0]
    lam = np.concatenate([np.asarray(inp[k])[0] for k in ("da_lam_q1", "da_lam_k1", "da_lam_q2", "da_lam_k2")])
    v[:, V_LAM:V_LAM + 256] = lam[None, :]
    return v


def _s5p(inp):
    out = np.zeros((2, 128, S5NC), np.float32)
    for j in range(2):
        def pl(a):
            a = np.asarray(a)
            sh = a.shape[2:]
            a = a.reshape((32, 2, 64) + sh)
            a = np.moveaxis(a, 0, 2)
            return a.reshape((128, 32) + sh)
        lre = pl(inp["s5_lam_re"][j])
        lim = pl(inp["s5_lam_im"][j])
        lst = pl(np.repeat(np.asarray(inp["s5_log_step"][j])[:, None], 64, 1))
        bre = pl(inp["s5_b_re"][j])
        bim = pl(inp["s5_b_im"][j])
        cre = pl(np.asarray(inp["s5_c_re"][j]).transpose(0, 2, 1))
        cim = pl(np.asarray(inp["s5_c_im"][j]).transpose(0, 2, 1))
        dv = np.tile(np.asarray(inp["s5_d"][j]).reshape(64, 16).T, (8, 1))
        o = out[j]
        o[:, 0:32] = lre
        o[:, 32:64] = lim
        o[:, 64:96] = lst
        o[:, 96:608] = bre.reshape(128, 512)
        o[:, 608:1120] = bim.reshape(128, 512)
        o[:, 1120:1632] = cre.reshape(128, 512)
        o[:, 1632:2144] = cim.reshape(128, 512)
        o[:, 2144:2208] = dv
    return out


def host_shared(inp):
    f = lambda a: np.ascontiguousarray(np.asarray(a, np.float32))
    return {
        "vecs": _vecs(inp),
        "w1": f(inp["mlp_w1"]),
        "w2": f(inp["mlp_w2"]),
        "poolw": f(inp["pool_w"][0]),
        "wqkv": f(inp["da_w_qkv"][0]),
        "wo": f(inp["da_w_o"][0]),
        "s5w": f(np.stack([inp["s5_w_in"], inp["s5_w_gate"], inp["s5_w_out"]], 1)),
        "s5p": _s5p(inp),
    }


def kernel(**inputs):
    x = np.asarray(inputs["x"], np.float32)
    prog = Prog()
    nc = prog.build()
    shared = host_shared(inputs)
    in_maps = []
    for c in range(N_CORES):
        xs = x[c * SEQ_PER_CORE:(c + 1) * SEQ_PER_CORE]
        m = dict(shared)
        m["xT"] = np.ascontiguousarray(xs.transpose(0, 2, 1))
        in_maps.append(m)
    res = run_bass_kernel_spmd(nc, in_maps, core_ids=list(range(N_CORES)))
    out = np.empty_like(x)
    for c in range(N_CORES):
        out[c * SEQ_PER_CORE:(c + 1) * SEQ_PER_CORE] = res.results[c]["yT"].transpose(0, 2, 1)
    return out
```

```python
import contextlib
import math
import numpy as np
import concourse.bass as bass
import concourse.mybir as mybir
from concourse.bass_utils import run_bass_kernel_spmd

F32 = mybir.dt.float32
BF16 = mybir.dt.bfloat16
I32 = mybir.dt.int32
AF = mybir.ActivationFunctionType
ALU = mybir.AluOpType

D = 1024
S = 2048
NFT = 8
TT = 512
NTT = S // TT
DFF = 4096
EPS = 1e-6
DEPTH = 4
N_CORES = 8
SEQ_PER_CORE = 2
V_PSCALE = 72
V_SUBLN = 80
V_LAM = 81
NV = 81 + 256
S5NC = 96 + 2048 + 64
LAM_INIT = 0.8 - 0.6 * math.exp(-0.3 * 1)
POOL_W = (2, 4, 8, 16)


class Buf:
    __slots__ = ("name", "w", "r")

    def __init__(self, name):
        self.name = name
        self.w = None
        self.r = {}


class Eng:
    def __init__(self, name, e, sem):
        self.name = name
        self.e = e
        self.sem = sem
        self.count = 0
        self.waited = {}


class Sync:
    def __init__(self, nc, n_dma_sems=12):
        self.nc = nc
        self.engs = {}
        for name, e in (("pe", nc.tensor), ("act", nc.scalar), ("dve", nc.vector),
                        ("pool", nc.gpsimd), ("sp", nc.sync)):
            self.engs[name] = Eng(name, e, nc.alloc_semaphore(name="s_" + name))
        self.dma_sems = {q: [[nc.alloc_semaphore(name=f"d{q}{i}"), 0] for i in range(n_dma_sems)] for q in ("sp", "pool")}
        self.dma_rr = {"sp": 0, "pool": 0}
        self.ninst = 0

    def _wait(self, eng, ticket):
        sem, val, src = ticket
        if src == "pe" and eng.name == "pe":
            return
        k = id(sem)
        if eng.waited.get(k, 0) >= val:
            return
        eng.e.wait_ge(sem, val)
        eng.waited[k] = val

    def _deps(self, eng, reads, writes):
        need = {}

        def add(t):
            if t is None:
                return
            k = id(t[0])
            if k not in need or need[k][1] < t[1]:
                need[k] = t
        for b in reads:
            add(b.w)
        for b in writes:
            add(b.w)
            for t in b.r.values():
                add(t)
        for t in need.values():
            self._wait(eng, t)

    @staticmethod
    def _mark(ticket, key, reads, writes):
        for b in reads:
            b.r[key] = ticket
        for b in writes:
            b.w = ticket
            b.r = {}

    def op(self, engname, fn, reads=(), writes=()):
        eng = self.engs[engname]
        self._deps(eng, reads, writes)
        inst = fn(eng.e)
        eng.count += 1
        inst.then_inc(eng.sem, 1)
        t = (eng.sem, eng.count, eng.name)
        self._mark(t, eng.name, reads, writes)
        self.ninst += 1
        return t

    def group(self, engname, fns, reads=(), writes=()):
        eng = self.engs[engname]
        self._deps(eng, reads, writes)
        inst = None
        for fn in fns:
            inst = fn(eng.e)
            self.ninst += 1
        eng.count += 1
        inst.then_inc(eng.sem, 1)
        t = (eng.sem, eng.count, eng.name)
        self._mark(t, eng.name, reads, writes)
        return t

    def dma(self, engname, fns, reads=(), writes=()):
        eng = self.engs[engname]
        self._deps(eng, reads, writes)
        pool_ = self.dma_sems[engname]
        slot = pool_[self.dma_rr[engname]]
        self.dma_rr[engname] = (self.dma_rr[engname] + 1) % len(pool_)
        sem, total = slot
        if total > 0:
            self._wait(eng, (sem, total, None))
        for fn in fns:
            fn(eng.e).then_inc(sem, 16)
            total += 16
            self.ninst += 1
        slot[1] = total
        t = (sem, total, None)
        self._mark(t, "dma%d" % id(sem), reads, writes)
        return t

    def barrier(self, names=("pe", "act", "dve", "pool", "sp")):
        for n in names:
            eng = self.engs[n]
            for m in names:
                if m != n:
                    o = self.engs[m]
                    if o.count > 0:
                        self._wait(eng, (o.sem, o.count, o.name))
            for q in self.dma_sems:
                for sem, total in self.dma_sems[q]:
                    if total > 0:
                        self._wait(eng, (sem, total, None))


class Prog:
    def __init__(self, n_seq=SEQ_PER_CORE, layers=None):
        if layers is None:
            layers = [(i, i % 3, True) for i in range(DEPTH)]
        self.layers = layers
        self.n_seq = n_seq
        self.nc = bass.Bass("TRN2", target_bir_lowering=False)
        self.es = contextlib.ExitStack()
        self._uid = 0

    def dram_in(self, name, shape):
        return self.nc.dram_tensor(name, list(shape), F32, kind="ExternalInput").ap()

    def sb(self, es, name, shape, dt=F32):
        self._uid += 1
        return es.enter_context(self.nc.sbuf_tensor(f"{name}_{self._uid}", list(shape), dt))

    @contextlib.contextmanager
    def scope(self):
        with contextlib.ExitStack() as es:
            yield es
            self.sy.barrier()

    def build(self):
        nc = self.nc
        ns = self.n_seq
        self.xT = self.dram_in("xT", [ns, D, S])
        self.gall = self.dram_in("vecs", [128, NV])
        self.w1 = self.dram_in("w1", [DEPTH, D, DFF])
        self.w2 = self.dram_in("w2", [DEPTH, DFF, D])
        self.poolw = self.dram_in("poolw", [4, 256, 256])
        self.wqkv = self.dram_in("wqkv", [D, 3 * D])
        self.wo = self.dram_in("wo", [D, D])
        self.s5w = self.dram_in("s5w", [2, 3, D, D])
        self.s5p = self.dram_in("s5p", [2, 128, S5NC])
        self.yT = nc.dram_tensor("yT", [ns, D, S], F32, kind="ExternalOutput").ap()
        with self.es as es:
            self.sy = Sync(nc)
            sy = self.sy
            self.ps = [es.enter_context(nc.psum_tensor(f"psb{i}", [128, 512], F32)) for i in range(8)]
            self.psB = [Buf(f"ps{i}") for i in range(8)]
            self.ones_bf = self.sb(es, "ones", [128, 128], BF16)
            self.onesB = Buf("ones")
            sy.op("dve", lambda e: e.memset(self.ones_bf[:], 1.0), writes=[self.onesB])
            self.idb = self.sb(es, "idb", [128, 128], BF16)
            self.idbB = Buf("idb")
            with self.scope() as es0:
                idf = self.sb(es0, "idf0", [128, 128])
                idfB = Buf("idf0")
                sy.op("pool", lambda e: e.memset(idf[:], 0.0), writes=[idfB])
                sy.op("pool", lambda e: e.affine_select(out=idf[:], in_=idf[:], pattern=[[-1, 128]], compare_op=ALU.not_equal, fill=1.0, base=0, channel_multiplier=1),
                      reads=[idfB], writes=[idfB])
                sy.op("dve", lambda e: e.tensor_copy(out=self.idb[:], in_=idf[:]), reads=[idfB], writes=[self.idbB])
            self.epsT = self.sb(es, "eps", [128, 1])
            self.epsB = Buf("eps")
            sy.op("dve", lambda e: e.memset(self.epsT[:], EPS), writes=[self.epsB])
            self.g = self.sb(es, "gall", [128, NV])
            self.gB = Buf("g")
            sy.dma("sp", [lambda e: e.dma_start(out=self.g[:], in_=self.gall)], writes=[self.gB])
            self.x = self.sb(es, "x", [128, NFT, S])
            self.xB = [[Buf(f"x{ft}_{tt}") for tt in range(NTT)] for ft in range(NFT)]
            self.load_x(0)
            self.s5c = {}
            self.rope_c = None
            for (li_, kind_, _m) in self.layers:
                if kind_ == 0 and (li_ // 3) not in self.s5c:
                    self.s5_precompute(li_ // 3)
            for s in range(ns):
                self.seq(s, s + 1 if s + 1 < ns else None)
            sy.barrier()
        return nc

    def gcol(self, idx, ft):
        c = idx * NFT + ft
        return self.g[:, c:c + 1]

    def xt(self, ft, tt):
        return self.x[:, ft, tt * TT:(tt + 1) * TT]

    def load_x(self, s):
        sy = self.sy
        for tt in range(NTT):
            for ft in range(NFT):
                sy.dma("sp", [lambda e, ft=ft, tt=tt: e.dma_start(out=self.xt(ft, tt), in_=self.xT[s, ft * 128:(ft + 1) * 128, tt * TT:(tt + 1) * TT])],
                       writes=[self.xB[ft][tt]])

    def seq(self, s, next_s=None):
        sy = self.sy
        for (li, kind, do_mlp) in self.layers:
            if kind == 0:
                self.s5(li, li // 3)
            elif kind == 1:
                self.attn(li)
            elif kind == 2:
                self.pool(li)
            if do_mlp:
                self.mlp(li)
        self.final(s, next_s)

    def rmsnorm(self, es, gidx, h, hB, inline=False):
        sy = self.sy
        if inline:
            self._rmsnorm(es, gidx, h, hB)
            return
        with self.scope() as es2:
            self._rmsnorm(es2, gidx, h, hB)

    def _rmsnorm(self, es, gidx, h, hB):
        sy = self.sy
        sq = [self.sb(es, "sq", [128, NFT, TT], BF16) for _ in range(2)]
        sqB = [Buf("sq0"), Buf("sq1")]
        rs = [self.sb(es, "rs", [128, TT]) for _ in range(2)]
        rsB = [Buf("rs0"), Buf("rs1")]
        pbank = [6, 7]

        def square(tt):
            b = tt % 2
            xin = self.x[:, :, tt * TT:(tt + 1) * TT]
            sy.op("act", lambda e: e.activation(out=sq[b][:], in_=xin, func=AF.Square),
                  reads=[self.xB[ft][tt] for ft in range(NFT)], writes=[sqB[b]])

        square(0)
        for tt in range(NTT):
            b = tt % 2
            if tt + 1 < NTT:
                square(tt + 1)
            pb = pbank[b]
            sy.group("pe", [lambda e, b=b, ft=ft, pb=pb: e.matmul(self.ps[pb][:], lhsT=self.ones_bf[:], rhs=sq[b][:, ft, :],
                                                                start=(ft == 0), stop=(ft == NFT - 1)) for ft in range(NFT)],
                     reads=[sqB[b], self.onesB], writes=[self.psB[pb]])
            sy.op("act", lambda e, b=b, pb=pb: e.activation(out=rs[b][:], in_=self.ps[pb][:], func=AF.Ln, scale=1.0 / D, bias=self.epsT[:]),
                  reads=[self.psB[pb], self.epsB], writes=[rsB[b]])
            sy.op("act", lambda e, b=b: e.activation(out=rs[b][:], in_=rs[b][:], func=AF.Exp, scale=-0.5), reads=[rsB[b]], writes=[rsB[b]])
            for ft in range(NFT):
                sy.op("dve", lambda e, b=b, ft=ft, tt=tt: e.scalar_tensor_tensor(
                    out=h[:, ft, tt * TT:(tt + 1) * TT], in0=self.xt(ft, tt), scalar=self.gcol(gidx, ft), in1=rs[b][:],
                    op0=ALU.mult, op1=ALU.mult),
                    reads=[self.xB[ft][tt], rsB[b], self.gB], writes=[hB[ft][tt]])

    def mlp(self, li):
        sy = self.sy
        HC = 4
        NHC = DFF // (HC * 128)
        with self.scope() as es:
            h = self.sb(es, "h", [128, NFT, S], BF16)
            hB = [[Buf(f"h{ft}_{tt}") for tt in range(NTT)] for ft in range(NFT)]
            hid = [self.sb(es, "hid", [128, HC, S], BF16) for _ in range(2)]
            hidB = [[[Buf(f"hid{b}_{hi}_{tt}") for tt in range(NTT)] for hi in range(HC)] for b in range(2)]
            w1c = [self.sb(es, "w1c", [128, NFT, HC * 128], BF16) for _ in range(2)]
            w1B = [Buf("w1c0"), Buf("w1c1")]
            w2c = [self.sb(es, "w2c", [128, HC, D], BF16) for _ in range(2)]
            w2B = [Buf("w2c0"), Buf("w2c1")]
            rt = [self.sb(es, "rt", [128, TT]) for _ in range(3)]
            rtB = [Buf(f"rt{i}") for i in range(3)]
            cnt = {"f": 0, "s": 0, "r": 0}

            def load(hc):
                b = hc % 2
                c0 = hc * HC * 128
                sy.dma("pool", [lambda e: e.dma_start(out=w1c[b][:], in_=self.w1[li, :, c0:c0 + HC * 128].rearrange("(ft p) n -> p ft n", p=128))],
                       writes=[w1B[b]])
                sy.dma("pool", [lambda e: e.dma_start(out=w2c[b][:], in_=self.w2[li, c0:c0 + HC * 128, :].rearrange("(hi p) n -> p hi n", p=128))],
                       writes=[w2B[b]])

            def first(hc):
                b = hc % 2
                for tt in range(NTT):
                    for h2 in range(HC // 2):
                        pbs = [2 * (cnt["f"] % 2), 2 * (cnt["f"] % 2) + 1]
                        cnt["f"] += 1
                        fns = []
                        for q, pb in enumerate(pbs):
                            hi = 2 * h2 + q
                            for ft in range(NFT):
                                fns.append(lambda e, ft=ft, pb=pb, hi=hi: e.matmul(
                                    self.ps[pb][:], lhsT=w1c[b][:, ft, hi * 128:(hi + 1) * 128], rhs=h[:, ft, tt * TT:(tt + 1) * TT],
                                    start=(ft == 0), stop=(ft == NFT - 1)))
                        sy.group("pe", fns, reads=[w1B[b]] + [hB[ft][tt] for ft in range(NFT)], writes=[self.psB[pb] for pb in pbs])
                        for q, pb in enumerate(pbs):
                            hi = 2 * h2 + q
                            r = cnt["r"] % 3
                            cnt["r"] += 1
                            sy.op("act", lambda e, pb=pb, r=r: e.activation(out=rt[r][:], in_=self.ps[pb][:], func=AF.Relu),
                                  reads=[self.psB[pb]], writes=[rtB[r]])
                            sy.op("pool", lambda e, r=r, hi=hi: e.tensor_tensor(
                                out=hid[b][:, hi, tt * TT:(tt + 1) * TT], in0=rt[r][:], in1=rt[r][:], op=ALU.mult),
                                reads=[rtB[r]], writes=[hidB[b][hi][tt]])

            def second(hc):
                b = hc % 2
                for tt in range(NTT):
                    for f2 in range(NFT // 2):
                        pbs = [4 + 2 * (cnt["s"] % 2), 5 + 2 * (cnt["s"] % 2)]
                        cnt["s"] += 1
                        fns = []
                        for q, pb in enumerate(pbs):
                            fo = 2 * f2 + q
                            for hi in range(HC):
                                fns.append(lambda e, hi=hi, pb=pb, fo=fo: e.matmul(
                                    self.ps[pb][:], lhsT=w2c[b][:, hi, fo * 128:(fo + 1) * 128], rhs=hid[b][:, hi, tt * TT:(tt + 1) * TT],
                                    start=(hi == 0), stop=(hi == HC - 1)))
                        sy.group("pe", fns, reads=[w2B[b]] + [hidB[b][hi][tt] for hi in range(HC)], writes=[self.psB[pb] for pb in pbs])
                        for q, pb in enumerate(pbs):
                            fo = 2 * f2 + q
                            sy.op("dve", lambda e, pb=pb, fo=fo: e.tensor_tensor(
                                out=self.xt(fo, tt), in0=self.xt(fo, tt), in1=self.ps[pb][:], op=ALU.add),
                                reads=[self.psB[pb], self.xB[fo][tt]], writes=[self.xB[fo][tt]])

            import os
            dbg = int(os.environ.get("MLPDBG", "9"))
            load(0)
            load(1)
            self.rmsnorm(es, 4 + li, h, hB, inline=True)
            if dbg >= 2:
                first(0)
            for hc in range(NHC):
                if hc + 1 < NHC and dbg >= 2:
                    first(hc + 1)
                if dbg >= 3:
                    second(hc)
                if hc + 2 < NHC:
                    load(hc + 2)

    def pool(self, li):
        sy = self.sy
        with self.scope() as es:
            h = self.sb(es, "hp", [128, NFT, S])
            hB = [[Buf(f"hp{ft}_{tt}") for tt in range(NTT)] for ft in range(NFT)]
            self.rmsnorm(es, li, h, hB)
            p = self.sb(es, "pp", [128, NFT, S], BF16)
            pB = [Buf(f"pp{ft}") for ft in range(NFT)]
            wp = self.sb(es, "wp", [128, 4, 2, 256], BF16)
            wpB = Buf("wp")
            sy.dma("pool", [lambda e: e.dma_start(out=wp[:], in_=self.poolw.rearrange("g (kt p) d -> p g kt d", p=128))], writes=[wpB])
            rc = self.sb(es, "rc", [128, 16])
            rcB = Buf("rc")
            sy.op("pool", lambda e: e.iota(rc[:], pattern=[[1, 16]], base=1, channel_multiplier=0, allow_small_or_imprecise_dtypes=True), writes=[rcB])
            sy.op("dve", lambda e: e.reciprocal(out=rc[:], in_=rc[:]), reads=[rcB], writes=[rcB])
            ab = [[self.sb(es, "pa", [128, S]) for _ in range(2)] for _ in range(2)]
            abB = [[Buf("pa"), Buf("pb")] for _ in range(2)]
            tmpc = [self.sb(es, "ptc", [128, 16]) for _ in range(2)]
            tmpB = [Buf("ptc0"), Buf("ptc1")]
            for ft in range(NFT):
                w = POOL_W[ft // 2]
                eng = "dve" if ft % 2 == 0 else "pool"
                k = ft % 2
                hall = [hB[ft][tt] for tt in range(NTT)]
                src, srcB = h[:, ft, :], hall
                sh = 1
                i = 0
                while sh < w:
                    dst, dstB = ab[k][i % 2], [abB[k][i % 2]]
                    sy.op(eng, lambda e, dst=dst, src=src, sh=sh: e.tensor_tensor(out=dst[:, sh:], in0=src[:, sh:], in1=src[:, :S - sh], op=ALU.add),
                          reads=srcB, writes=dstB)
                    sy.op(eng, lambda e, dst=dst, src=src, sh=sh: e.tensor_copy(out=dst[:, 0:sh], in_=src[:, 0:sh]), reads=srcB, writes=dstB)
                    src, srcB = dst[:], dstB
                    sh *= 2
                    i += 1
                sy.op("dve", lambda e, src=src, ft=ft, w=w: e.scalar_tensor_tensor(out=p[:, ft, :], in0=src, scalar=1.0 / w, in1=h[:, ft, :],
                                                                                 op0=ALU.mult, op1=ALU.subtract),
                      reads=srcB + hall, writes=[pB[ft]])
                sy.op("dve", lambda e, src=src, k=k, w=w: e.tensor_tensor(out=tmpc[k][:, 0:w - 1], in0=src[:, 0:w - 1], in1=rc[:, 0:w - 1], op=ALU.mult),
                      reads=srcB + [rcB], writes=[tmpB[k]])
                sy.op("dve", lambda e, k=k, ft=ft, w=w: e.tensor_tensor(out=p[:, ft, 0:w - 1], in0=tmpc[k][:, 0:w - 1], in1=h[:, ft, 0:w - 1], op=ALU.subtract),
                      reads=[tmpB[k]] + hall, writes=[pB[ft]])
            cnt = 0
            for tt in range(NTT):
                for g in range(4):
                    for oc in range(2):
                        pb = cnt % 4
                        cnt += 1
                        fo = 2 * g + oc
                        sy.group("pe", [lambda e, kt=kt, pb=pb, g=g, oc=oc, tt=tt: e.matmul(
                            self.ps[pb][:], lhsT=wp[:, g, kt, oc * 128:(oc + 1) * 128], rhs=p[:, 2 * g + kt, tt * TT:(tt + 1) * TT],
                            start=(kt == 0), stop=(kt == 1)) for kt in range(2)],
                            reads=[wpB, pB[2 * g], pB[2 * g + 1]], writes=[self.psB[pb]])
                        sy.op("dve", lambda e, pb=pb, fo=fo, tt=tt: e.scalar_tensor_tensor(
                            out=self.xt(fo, tt), in0=self.ps[pb][:], scalar=self.g[:, V_PSCALE + fo:V_PSCALE + fo + 1], in1=self.xt(fo, tt),
                            op0=ALU.mult, op1=ALU.add),
                            reads=[self.psB[pb], self.xB[fo][tt], self.gB], writes=[self.xB[fo][tt]])

    def attn(self, li):
        sy = self.sy
        NH = 8
        TWO_PI = 2.0 * math.pi
        with self.scope() as es:
            h = self.sb(es, "ha", [128, NFT, S], BF16)
            hB = [[Buf(f"ha{ft}_{tt}") for tt in range(NTT)] for ft in range(NFT)]
            hall = [hB[ft][tt] for ft in range(NFT) for tt in range(NTT)]
            cosT = self.sb(es, "cosT", [128, S])
            sinS = self.sb(es, "sinS", [128, S])
            cosB, sinB = Buf("cosT"), Buf("sinS")
            perm = self.sb(es, "perm", [128, 128])
            permB = Buf("perm")
            nlam = self.sb(es, "nlam", [128, 1])
            nlamB = Buf("nlam")
            subs = self.sb(es, "subs", [128, 1])
            subsB = Buf("subs")
            with self.scope() as es2:
                if self.rope_c is None:
                    jf = self.sb(es2, "jf", [128, 1])
                    jB = Buf("jf")
                    pi_ = self.sb(es2, "pi", [128, 1])
                    piB = Buf("pi")
                    sy.op("pool", lambda e: e.iota(pi_[:], pattern=[[0, 1]], base=0, channel_multiplier=1, allow_small_or_imprecise_dtypes=True), writes=[piB])
                    qi = self.sb(es2, "qi", [128, 1], I32)
                    qB = Buf("qi")
                    sy.op("dve", lambda e: e.tensor_scalar(out=jf[:], in0=pi_[:], scalar1=-15.5, scalar2=1.0 / 32, op0=ALU.add, op1=ALU.mult), reads=[piB], writes=[jB])
                    sy.op("dve", lambda e: e.tensor_copy(out=qi[:], in_=jf[:]), reads=[jB], writes=[qB])
                    sy.op("dve", lambda e: e.tensor_copy(out=jf[:], in_=qi[:]), reads=[qB], writes=[jB])
                    sy.op("dve", lambda e: e.scalar_tensor_tensor(out=jf[:], in0=jf[:], scalar=-32.0, in1=pi_[:], op0=ALU.mult, op1=ALU.add), reads=[jB, piB], writes=[jB])
                    invf = self.sb(es2, "invf", [128, 1])
                    ifB = Buf("invf")
                    sy.op("act", lambda e: e.activation(out=invf[:], in_=jf[:], func=AF.Exp, scale=-math.log(10000.0) * 2.0 / 64.0), reads=[jB], writes=[ifB])
                    sgn = self.sb(es2, "sgn", [128, 1])
                    sgB = Buf("sgn")
                    sy.op("pool", lambda e: e.memset(sgn[:], 1.0), writes=[sgB])
                    sy.op("pool", lambda e: e.memset(sgn[0:32, :], -1.0), reads=[], writes=[sgB])
                    sy.op("pool", lambda e: e.memset(sgn[64:96, :], -1.0), reads=[], writes=[sgB])
                    ang = self.sb(es2, "ang", [128, S])
                    angB = Buf("ang")
                    ni = self.sb(es2, "ni", [128, S], I32)
                    niB = Buf("ni")
                    nf = self.sb(es2, "nf", [128, S])
                    nfB = Buf("nf")
                    sy.op("pool", lambda e: e.iota(ang[:], pattern=[[1, S]], base=0, channel_multiplier=0, allow_small_or_imprecise_dtypes=True), writes=[angB])
                    sy.op("dve", lambda e: e.tensor_scalar(out=ang[:], in0=ang[:], scalar1=invf[:], scalar2=None, op0=ALU.mult), reads=[angB, ifB], writes=[angB])

                    def sin_of(dst, dstB, shift, signed):
                        sy.op("dve", lambda e: e.tensor_scalar(out=nf[:], in0=ang[:], scalar1=shift, scalar2=1.0 / TWO_PI, op0=ALU.add, op1=ALU.mult), reads=[angB], writes=[nfB])
                        sy.op("dve", lambda e: e.tensor_copy(out=ni[:], in_=nf[:]), reads=[nfB], writes=[niB])
                        sy.op("dve", lambda e: e.tensor_copy(out=nf[:], in_=ni[:]), reads=[niB], writes=[nfB])
                        sy.op("dve", lambda e: e.scalar_tensor_tensor(out=nf[:], in0=nf[:], scalar=-TWO_PI, in1=ang[:], op0=ALU.mult, op1=ALU.add), reads=[nfB, angB], writes=[nfB])
                        sy.op("dve", lambda e: e.tensor_scalar(out=nf[:], in0=nf[:], scalar1=shift, scalar2=math.pi, op0=ALU.add, op1=ALU.min), reads=[nfB], writes=[nfB])
                        sy.op("dve", lambda e: e.tensor_scalar(out=nf[:], in0=nf[:], scalar1=-math.pi, scalar2=None, op0=ALU.max), reads=[nfB], writes=[nfB])
                        if signed:
                            sy.op("act", lambda e: e.activation(out=dst[:], in_=nf[:], func=AF.Sin), reads=[nfB], writes=[dstB])
                            sy.op("dve", lambda e: e.tensor_scalar(out=dst[:], in0=dst[:], scalar1=sgn[:], scalar2=None, op0=ALU.mult), reads=[dstB, sgB], writes=[dstB])
                        else:
                            sy.op("act", lambda e: e.activation(out=dst[:], in_=nf[:], func=AF.Sin), reads=[nfB], writes=[dstB])
                    sin_of(sinS, sinB, 0.0, True)
                    sin_of(cosT, cosB, math.pi / 2, False)
                    self.rope_c = self.nc.dram_tensor("rope_c", [128, 2 * S], F32, kind="Internal").ap()
                    sy.dma("sp", [lambda e: e.dma_start(out=self.rope_c[:, 0:S], in_=cosT[:]),
                                  lambda e: e.dma_start(out=self.rope_c[:, S:2 * S], in_=sinS[:])], reads=[cosB, sinB])
                else:
                    sy.dma("sp", [lambda e: e.dma_start(out=cosT[:], in_=self.rope_c[:, 0:S]),
                                  lambda e: e.dma_start(out=sinS[:], in_=self.rope_c[:, S:2 * S])], writes=[cosB, sinB])
                sy.op("pool", lambda e: e.memset(perm[:], 0.0), writes=[permB])
                for (c0, off) in ((0, 32), (32, -32), (64, 32), (96, -32)):
                    sy.op("pool", lambda e, c0=c0, off=off: e.affine_select(
                        out=perm[:, c0:c0 + 32], in_=perm[:, c0:c0 + 32], pattern=[[-1, 32]], compare_op=ALU.not_equal, fill=1.0,
                        base=-(c0 + off), channel_multiplier=1), reads=[permB], writes=[permB])
                lt = self.sb(es2, "lt", [128, 2, 64])
                ltB = Buf("lt")
                ls = self.sb(es2, "ls", [128, 2])
                lsB = Buf("ls")
                lv = self.g[:, V_LAM:V_LAM + 256].rearrange("p (a b c) -> p a b c", a=2, b=2)
                sy.op("dve", lambda e: e.tensor_tensor(out=lt[:], in0=lv[:, :, 0, :], in1=lv[:, :, 1, :], op=ALU.mult), reads=[self.gB], writes=[ltB])
                sy.op("dve", lambda e: e.reduce_sum(out=ls[:], in_=lt[:], axis=mybir.AxisListType.X), reads=[ltB], writes=[lsB])
                sy.op("act", lambda e: e.activation(out=ls[:], in_=ls[:], func=AF.Exp), reads=[lsB], writes=[lsB])
                sy.op("dve", lambda e: e.tensor_tensor(out=nlam[:], in0=ls[:, 1:2], in1=ls[:, 0:1], op=ALU.subtract), reads=[lsB], writes=[nlamB])
                sy.op("dve", lambda e: e.tensor_scalar(out=nlam[:], in0=nlam[:], scalar1=-LAM_INIT, scalar2=None, op0=ALU.add), reads=[nlamB], writes=[nlamB])
                sy.op("dve", lambda e: e.tensor_scalar(out=subs[:], in0=self.g[:, V_SUBLN:V_SUBLN + 1], scalar1=1.0 - LAM_INIT, scalar2=None, op0=ALU.mult),
                      reads=[self.gB], writes=[subsB])
            wq = [self.sb(es, "wq", [128, NFT, 3, 128], BF16) for _ in range(2)]
            wqB = [Buf("wq0"), Buf("wq1")]
            woh = [self.sb(es, "woh", [128, D], BF16) for _ in range(4)]
            woB = [Buf(f"wo{i}") for i in range(4)]

            def load_w(hd):
                b = hd % 2
                fns = []
                for j in range(3):
                    c0 = j * D + hd * 128
                    fns.append(lambda e, j=j, c0=c0: e.dma_start(out=wq[b][:, :, j, :], in_=self.wqkv[:, c0:c0 + 128].rearrange("(kt p) n -> p kt n", p=128)))
                sy.dma("pool", fns, writes=[wqB[b]])

            def load_wo(hd):
                b = hd % 4
                sy.dma("pool", [lambda e: e.dma_start(out=woh[b][:], in_=self.wo[hd * 128:(hd + 1) * 128, :])], writes=[woB[b]])

            load_w(0)
            load_w(1)
            for h_ in range(4):
                load_wo(h_)
            self.rmsnorm(es, li, h, hB)
            qh = [self.sb(es, "qh", [128, S], BF16) for _ in range(2)]
            kh = [self.sb(es, "kh", [128, S], BF16) for _ in range(2)]
            vh = [self.sb(es, "vh", [128, 16, 128], BF16) for _ in range(2)]
            oth = [self.sb(es, "oth", [128, S], BF16) for _ in range(4)]
            qB = [[Buf(f"qh{b}_{tt}") for tt in range(NTT)] for b in range(2)]
            kB = [[Buf(f"kh{b}_{tt}") for tt in range(NTT)] for b in range(2)]
            vB = [[Buf(f"vh{b}_{j}") for j in range(4)] for b in range(2)]
            oB = [[Buf(f"oth{b}_{tt}") for tt in range(NTT)] for b in range(4)]
            qf = [self.sb(es, "qf", [128, TT]) for _ in range(2)]
            qfB = [Buf("qf0"), Buf("qf1")]
            ta = [self.sb(es, "ta", [128, TT]) for _ in range(2)]
            taB = [Buf("ta0"), Buf("ta1")]
            tb = [self.sb(es, "tb", [128, TT]) for _ in range(2)]
            tbB = [Buf("tb0"), Buf("tb1")]
            pT = [self.sb(es, "pT", [128, TT], BF16) for _ in range(4)]
            pTB = [Buf(f"pT{i}") for i in range(4)]
            t1 = self.sb(es, "t1", [128, TT])
            t1B = Buf("t1")
            rr = [self.sb(es, "rr", [128, TT]) for _ in range(2)]
            rrB = [Buf("rr0"), Buf("rr1")]
            rs = self.sb(es, "rs_a", [128, TT])
            rsB = Buf("rs_a")
            pcp = [self.sb(es, "pcp", [128, TT]) for _ in range(2)]
            pcpB = [Buf("pcp0"), Buf("pcp1")]
            of = self.sb(es, "of", [128, TT])
            ofB = Buf("of")
            osq = self.sb(es, "osq", [128, TT], BF16)
            osqB = Buf("osq")
            cnt = {"m": 0, "s": 0, "o": 0, "p": 0, "q": 0}

            def misc_bank():
                cnt["m"] += 1
                return cnt["m"] % 2

            def proj(hd):
                b = hd % 2
                for j, (dst, dB, sc) in enumerate(((qh[b], qB[b], 0.125), (kh[b], kB[b], 1.0))):
                    for tt in range(NTT):
                        pb = misc_bank()
                        i2 = cnt["q"] % 2
                        cnt["q"] += 1
                        sy.group("pe", [lambda e, kt=kt, pb=pb, j=j, tt=tt: e.matmul(
                            self.ps[pb][:], lhsT=wq[b][:, kt, j, :], rhs=h[:, kt, tt * TT:(tt + 1) * TT], start=(kt == 0), stop=(kt == NFT - 1))
                            for kt in range(NFT)], reads=[wqB[b]] + [hB[kt][tt] for kt in range(NFT)], writes=[self.psB[pb]])
                        sy.op("dve", lambda e, pb=pb, i2=i2, sc=sc: e.tensor_scalar(out=qf[i2][:], in0=self.ps[pb][:], scalar1=sc, scalar2=None, op0=ALU.mult),
                              reads=[self.psB[pb]], writes=[qfB[i2]])
                        yield
                        pb2 = misc_bank()
                        sy.op("pe", lambda e, pb2=pb2, i2=i2: e.matmul(self.ps[pb2][:], lhsT=perm[:], rhs=qf[i2][:], start=True, stop=True),
                              reads=[permB, qfB[i2]], writes=[self.psB[pb2]])
                        sy.op("dve", lambda e, i2=i2, tt=tt: e.tensor_tensor(out=ta[i2][:], in0=qf[i2][:], in1=cosT[:, tt * TT:(tt + 1) * TT], op=ALU.mult),
                              reads=[qfB[i2], cosB], writes=[taB[i2]])
                        sy.op("dve", lambda e, i2=i2, tt=tt, pb2=pb2: e.tensor_tensor(out=tb[i2][:], in0=self.ps[pb2][:], in1=sinS[:, tt * TT:(tt + 1) * TT], op=ALU.mult),
                              reads=[self.psB[pb2], sinB], writes=[tbB[i2]])
                        sy.op("pool", lambda e, i2=i2, tt=tt, dst=dst: e.tensor_tensor(out=dst[:, tt * TT:(tt + 1) * TT], in0=ta[i2][:], in1=tb[i2][:], op=ALU.add),
                              reads=[taB[i2], tbB[i2]], writes=[dB[tt]])
                        yield
                for jq in range(4):
                    pb = misc_bank()
                    fns = []
                    for jj in range(4):
                        t0 = (jq * 4 + jj) * 128
                        for kt in range(NFT):
                            fns.append(lambda e, kt=kt, jj=jj, t0=t0, pb=pb: e.matmul(
                                self.ps[pb][:, jj * 128:(jj + 1) * 128], lhsT=h[:, kt, t0:t0 + 128], rhs=wq[b][:, kt, 2, :],
                                start=(kt == 0), stop=(kt == NFT - 1)))
                    sy.group("pe", fns, reads=[wqB[b]] + [hB[kt][jq] for kt in range(NFT)], writes=[self.psB[pb]])
                    sy.op("dve", lambda e, pb=pb, jq=jq: e.tensor_copy(out=vh[b][:, jq * 4:(jq + 1) * 4, :].rearrange("p a b -> p (a b)"), in_=self.ps[pb][:]),
                          reads=[self.psB[pb]], writes=[vB[b][jq]])
                    yield

            G = 2

            def core(hd, extra=None):
                b = hd % 2
                ob = hd % 4
                batches = []
                for qt in range(NTT):
                    nkt = 4 * qt + 4
                    for m in range(2):
                        for k0 in range(0, nkt, G):
                            batches.append((qt, m, k0, nkt))
                n = len(batches)
                po, pd = 6, 7
                deferred = []

                def qk(i):
                    qt, m, k0, nkt = batches[i]
                    r0, r1 = 64 * m, 64 * m + 64
                    sset = cnt["s"] % 2
                    cnt["s"] += 1
                    fns, tl = [], []
                    for g_ in range(G):
                        kt = k0 + g_
                        r = kt - 4 * qt
                        c0 = 128 * r if r > 0 else 0
                        pst = 2 + 2 * sset + g_
                        ip = cnt["p"] % 4
                        cnt["p"] += 1
                        tl.append((kt, r, c0, pst, ip))
                        fns.append(lambda e, kt=kt, c0=c0, pst=pst: e.matmul(
                            self.ps[pst][:, c0:TT], lhsT=kh[b][r0:r1, kt * 128:(kt + 1) * 128], rhs=qh[b][r0:r1, qt * TT + c0:(qt + 1) * TT],
                            start=True, stop=True))
                    sy.group("pe", fns, reads=[kB[b][k0 // 4], qB[b][qt]], writes=[self.psB[t[3]] for t in tl])
                    for (kt, r, c0, pst, ip) in tl:
                        sy.op("act", lambda e, c0=c0, pst=pst, ip=ip: e.activation(out=pT[ip][:, c0:TT], in_=self.ps[pst][:, c0:TT], func=AF.Exp),
                              reads=[self.psB[pst]], writes=[pTB[ip]])
                        if r >= 0:
                            sy.op("pool", lambda e, ip=ip, c0=c0: e.memset(pT[ip][64:128, c0:c0 + 64], 0.0), reads=[], writes=[pTB[ip]])
                    return tl

                def av(i, tl):
                    qt, m, k0, nkt = batches[i]
                    fns = []
                    for (kt, r, c0, pst, ip) in tl:
                        fns.append(lambda e, kt=kt, c0=c0, ip=ip: e.matmul(self.ps[po][:, c0:TT], lhsT=vh[b][:, kt, :], rhs=pT[ip][:, c0:TT],
                                                                         start=(kt == 0), stop=(kt == nkt - 1)))
                        fns.append(lambda e, kt=kt, c0=c0, ip=ip: e.matmul(self.ps[pd][:, c0:TT], lhsT=self.ones_bf[:], rhs=pT[ip][:, c0:TT],
                                                                         start=(kt == 0), stop=(kt == nkt - 1)))
                    sy.group("pe", fns, reads=[vB[b][k0 // 4], self.onesB] + [pTB[t[4]] for t in tl], writes=[self.psB[po], self.psB[pd]])
                    if k0 + G >= nkt:
                        epi(qt, m, po, pd)

                def epi(qt, m, po, pd):
                    k = m
                    sy.op("dve", lambda e: e.tensor_copy(out=pcp[k][:], in_=self.ps[po][:]), reads=[self.psB[po]], writes=[pcpB[k]])
                    sy.op("act", lambda e: e.activation(out=rr[k][:], in_=self.ps[pd][:], func=AF.Ln), reads=[self.psB[pd]], writes=[rrB[k]])
                    sy.op("act", lambda e: e.activation(out=rr[k][:], in_=rr[k][:], func=AF.Exp, scale=-1.0), reads=[rrB[k]], writes=[rrB[k]])
                    if m == 0:
                        sy.op("dve", lambda e: e.tensor_tensor(out=t1[:], in0=pcp[k][:], in1=rr[k][:], op=ALU.mult),
                              reads=[pcpB[k], rrB[k]], writes=[t1B])
                        return
                    sy.op("dve", lambda e: e.tensor_tensor(out=rr[k][:], in0=pcp[k][:], in1=rr[k][:], op=ALU.mult),
                          reads=[pcpB[k], rrB[k]], writes=[rrB[k]])
                    sy.op("dve", lambda e: e.scalar_tensor_tensor(out=of[:], in0=rr[k][:], scalar=nlam[:], in1=t1[:], op0=ALU.mult, op1=ALU.add),
                          reads=[rrB[k], t1B, nlamB], writes=[ofB])
                    deferred.append([2, lambda: subln_sq(qt)])
                    deferred.append([4, lambda: subln(qt)])

                def subln_sq(qt):
                    sy.op("dve", lambda e: e.tensor_tensor(out=osq[:], in0=of[:], in1=of[:], op=ALU.mult), reads=[ofB], writes=[osqB])

                def subln(qt):
                    pb = misc_bank()
                    sy.op("pe", lambda e: e.matmul(self.ps[pb][:], lhsT=self.ones_bf[:], rhs=osq[:], start=True, stop=True),
                          reads=[osqB, self.onesB], writes=[self.psB[pb]])
                    sy.op("act", lambda e: e.activation(out=rs[:], in_=self.ps[pb][:], func=AF.Ln, scale=1.0 / 128, bias=self.epsT[:]),
                          reads=[self.psB[pb], self.epsB], writes=[rsB])
                    sy.op("act", lambda e: e.activation(out=rs[:], in_=rs[:], func=AF.Exp, scale=-0.5), reads=[rsB], writes=[rsB])
                    sy.op("dve", lambda e: e.scalar_tensor_tensor(out=oth[ob][:, qt * TT:(qt + 1) * TT], in0=of[:], scalar=subs[:], in1=rs[:],
                                                                 op0=ALU.mult, op1=ALU.mult),
                          reads=[ofB, rsB, subsB], writes=[oB[ob][qt]])

                pend = {}
                for i in range(n + 1):
                    if i < n:
                        pend[i] = qk(i)
                    if i >= 1:
                        for d_ in list(deferred):
                            d_[0] -= 1
                            if d_[0] <= 0:
                                deferred.remove(d_)
                                d_[1]()
                        if extra is not None:
                            next(extra, None)
                        av(i - 1, pend.pop(i - 1))
                for d_ in deferred:
                    d_[1]()
                if extra is not None:
                    for _ in extra:
                        pass

            def outp(hd):
                hs = [hd - 1, hd]
                for tt in range(NTT):
                    for f2 in range(NFT // 2):
                        pbs = [0, 1]
                        fns = []
                        for q, pb in enumerate(pbs):
                            fo = 2 * f2 + q
                            for i_, h_ in enumerate(hs):
                                bb = h_ % 4
                                fns.append(lambda e, pb=pb, fo=fo, bb=bb, i_=i_: e.matmul(
                                    self.ps[pb][:], lhsT=woh[bb][:, fo * 128:(fo + 1) * 128], rhs=oth[bb][:, tt * TT:(tt + 1) * TT],
                                    start=(i_ == 0), stop=(i_ == 1)))
                        sy.group("pe", fns, reads=[woB[h_ % 4] for h_ in hs] + [oB[h_ % 4][tt] for h_ in hs], writes=[self.psB[0], self.psB[1]])
                        for q, pb in enumerate(pbs):
                            fo = 2 * f2 + q
                            sy.op("dve", lambda e, pb=pb, fo=fo: e.tensor_tensor(out=self.xt(fo, tt), in0=self.xt(fo, tt), in1=self.ps[pb][:], op=ALU.add),
                                  reads=[self.psB[pb], self.xB[fo][tt]], writes=[self.xB[fo][tt]])
                        yield

            def chain(*gens):
                for g_ in gens:
                    if g_ is not None:
                        yield from g_

            for _ in proj(0):
                pass
            load_w(2)
            for hd in range(NH):
                pj = proj(hd + 1) if hd + 1 < NH else None
                op_ = outp(hd - 1) if (hd % 2 == 0 and hd >= 2) else None
                core(hd, chain(pj, op_))
                if hd + 3 < NH:
                    load_w(hd + 3)
                if op_ is not None and hd + 2 < NH:
                    load_wo(hd + 2)
                    load_wo(hd + 3)
            for _ in outp(NH - 1):
                pass

    def s5(self, li, j):
        sy = self.sy
        NG = 64
        NP = 32
        NC = S // 8
        CT = 64
        TWO_PI = 2.0 * math.pi
        with self.scope() as es:
            U2 = self.sb(es, "U2", [128, NG, NC], BF16)
            U2B = [[Buf(f"U2_{gb}_{tt}") for tt in range(NTT)] for gb in range(8)]
            U2all = [U2B[gb][tt] for gb in range(8) for tt in range(NTT)]
            with self.scope() as es1:
                h = self.sb(es1, "hs", [128, NFT, S], BF16)
                hB = [[Buf(f"hs{ft}_{tt}") for tt in range(NTT)] for ft in range(NFT)]
                win = self.sb(es1, "win", [128, NFT, D], BF16)
                winB = Buf("win")
                sy.dma("pool", [lambda e, kt=kt: e.dma_start(out=win[:, kt, :], in_=self.s5w[j, 0, kt * 128:(kt + 1) * 128, :]) for kt in range(NFT)], writes=[winB])
                self.rmsnorm(es1, li, h, hB, inline=True)
                utm = [self.sb(es1, "utm", [128, 8, D], BF16) for _ in range(2)]
                utmB = [[Buf(f"utm{k}_{s_}") for s_ in range(8)] for k in range(2)]
                cnt = 0
                ev = 0
                for cb in range(2):
                    k = cb % 2
                    for s_ in range(8):
                        for fh in range(2):
                            pb = cnt % 4
                            cnt += 1
                            t0 = cb * 1024 + s_
                            sy.group("pe", [lambda e, kt=kt, pb=pb, fh=fh, t0=t0: e.matmul(
                                self.ps[pb][:], lhsT=h[:, kt, t0:t0 + 1017:8], rhs=win[:, kt, fh * 512:(fh + 1) * 512],
                                start=(kt == 0), stop=(kt == NFT - 1)) for kt in range(NFT)],
                                reads=[winB] + [hB[kt][2 * cb] for kt in range(NFT)] + [hB[kt][2 * cb + 1] for kt in range(NFT)], writes=[self.psB[pb]])
                            sy.op("act", lambda e, pb=pb, k=k, s_=s_, fh=fh: e.activation(
                                out=utm[k][:].rearrange("p s (g h) -> p (s g h)", h=16).rearrange("p (g s h) -> p g s h", s=8, h=16)[:, fh * 32:(fh + 1) * 32, s_, :],
                                in_=self.ps[pb][:].rearrange("p (g h) -> p g h", h=16), func=AF.Identity),
                                  reads=[self.psB[pb]], writes=[utmB[k][s_]])
                for cb in range(2):
                    k = cb % 2
                    for gq in range(16):
                        pb = 4 + gq % 2
                        psb = self.ps[pb][:].bitcast(BF16)
                        sy.group("pe", [lambda e, q=q, psb=psb, gq=gq, k=k: e.transpose(
                            psb[:, q * 128:(q + 1) * 128], utm[k][:].rearrange("p s f -> p (s f)")[:, (4 * gq + q) * 128:(4 * gq + q + 1) * 128], self.idb[:]) for q in range(4)],
                            reads=utmB[k] + [self.idbB], writes=[self.psB[pb]])
                        eng = "act" if ev % 2 == 0 else "dve"
                        ev += 1
                        dst = U2[:, 4 * gq:4 * gq + 4, cb * 128:(cb + 1) * 128]
                        src = psb[:, 0:512].rearrange("p (g c) -> p g c", g=4)
                        if eng == "act":
                            sy.op("act", lambda e, dst=dst, src=src: e.activation(out=dst, in_=src, func=AF.Identity),
                                  reads=[self.psB[pb]], writes=[U2B[gq // 2][2 * cb], U2B[gq // 2][2 * cb + 1]])
                        else:
                            sy.op("dve", lambda e, dst=dst, src=src: e.tensor_copy(out=dst, in_=src),
                                  reads=[self.psB[pb]], writes=[U2B[gq // 2][2 * cb], U2B[gq // 2][2 * cb + 1]])
            import os
            dbg = int(os.environ.get("S5DBG", "9"))
            if dbg < 2:
                return
            with self.scope() as es2:
                Tm = self.sb(es2, "Tm", [128, NG, 128], BF16)
                TmB = [Buf(f"Tm{i}") for i in range(16)]
                Btm = self.sb(es2, "Btm", [128, NP, 2, 128], BF16)
                BtB = [Buf(f"Btm{i}") for i in range(16)]
                CtR = self.sb(es2, "CtR", [128, NP, 128], BF16)
                CtI = self.sb(es2, "CtI", [128, NP, 128], BF16)
                CtB = [Buf(f"Ct{i}") for i in range(8)]
                PP1 = self.sb(es2, "PP1", [128, 8, 2, NP])
                PP2 = self.sb(es2, "PP2", [128, 8, 2, NP])
                AB = Buf("A12")
                A1, A2 = PP1[:, 0, :, :], PP2[:, 0, :, :]
                sc = self.s5c[j]
                for h_ in range(4):
                    sy.dma("sp", [lambda e, h_=h_: e.dma_start(out=Btm[:, h_ * 8:(h_ + 1) * 8, :, :].rearrange("p j r c -> p (j r c)"), in_=sc["Bt"][:, h_ * 2048:(h_ + 1) * 2048])],
                           writes=BtB[4 * h_:4 * h_ + 4])
                sy.dma("sp", [lambda e: e.dma_start(out=PP1[:].rearrange("p l a b -> p (l a b)"), in_=sc["A"][:, 0:512]),
                              lambda e: e.dma_start(out=PP2[:].rearrange("p l a b -> p (l a b)"), in_=sc["A"][:, 512:1024])], writes=[AB])

                sy.dma("sp", [lambda e: e.dma_start(out=CtR[:].rearrange("p j c -> p (j c)"), in_=sc["CtR"]),
                              lambda e: e.dma_start(out=CtI[:].rearrange("p j c -> p (j c)"), in_=sc["CtI"])], writes=CtB)
                sy.dma("sp", [lambda e, h_=h_: e.dma_start(out=Tm[:, h_ * 16:(h_ + 1) * 16, :].rearrange("p g c -> p (g c)"), in_=sc["Tm"][:, h_ * 2048:(h_ + 1) * 2048]) for h_ in range(4)], writes=TmB)
                if dbg < 3:
                    return
                St = [self.sb(es2, "St", [128, CT + 1, 2, NP]) for _ in range(2)]
                StB = [Buf("St0"), Buf("St1")]
                Sb = [self.sb(es2, "Sb", [128, CT, 2, NP], BF16) for _ in range(2)]
                SbB = [Buf("Sb0"), Buf("Sb1")]
                tA = self.sb(es2, "tA", [128, 8, 2, NP])
                tB_ = self.sb(es2, "tB", [128, 8, 2, NP])
                tAB, tBB = Buf("tA"), Buf("tB")
                tC = [tA[:, 0:7, :, :], tA[:, 0:7, :, :]]
                tD = [tB_[:, 0:7, :, :], tB_[:, 0:7, :, :]]
                tCB, tDB = [tAB, tAB], [tBB, tBB]
                XlB = [[Buf(f"Xl{k_}_{l_}") for l_ in range(8)] for k_ in range(2)]
                cnt = {"b": 0, "y": 0, "g": 0}

                def stage_a(tt):
                    c0 = tt * CT
                    k = tt % 2
                    for pq in range(NP // 4):
                        pb = cnt["b"] % 2
                        cnt["b"] += 1
                        fns = []
                        for jl in range(4):
                            jp = pq * 4 + jl
                            for e_ in range(2):
                                for ri in range(2):
                                    col = (jl * 2 + ri) * CT
                                    fns.append(lambda e, jp=jp, e_=e_, ri=ri, col=col, pb=pb: e.matmul(
                                        self.ps[pb][64 * e_:64 * e_ + 64, col:col + CT], lhsT=Btm[:, jp, ri, 64 * e_:64 * e_ + 64],
                                        rhs=U2[:, 2 * jp + e_, c0:c0 + CT], start=True, stop=True))
                        sy.group("pe", fns, reads=[BtB[pq * 2], BtB[pq * 2 + 1], U2B[pq][tt]], writes=[self.psB[pb]])
                        sy.op("act", lambda e, pb=pb, pq=pq: e.activation(
                            out=St[k][:, 1:CT + 1, :, pq * 4:pq * 4 + 4].rearrange("p c r j -> p j r c"),
                            in_=self.ps[pb][:].rearrange("p (j r c) -> p j r c", j=4, r=2), func=AF.Identity),
                            reads=[self.psB[pb]], writes=XlB[k])

                def stage_b(tt):
                    k = tt % 2
                    Sk = St[k]
                    Sv = Sk[:, 1:CT + 1, :, :].rearrange("p (b l) r j -> p b l r j", l=8)

                    def swp(ap_slot1_r1, nb):
                        t_ = ap_slot1_r1
                        if nb == 1:
                            return bass.AP(tensor=t_.tensor, offset=t_.offset, ap=[list(t_.ap[0]), [-NP, 2], [1, NP]])
                        return bass.AP(tensor=t_.tensor, offset=t_.offset, ap=[list(t_.ap[0]), [8 * 2 * NP, nb], [-NP, 2], [1, NP]])

                    def step(prev, prev_sw, cur, c1, c2, ta_, tb_, rB, wB):
                        sy.op("dve", lambda e: e.tensor_tensor(out=tb_, in0=prev_sw, in1=c2, op=ALU.mult), reads=rB + [AB], writes=[tBB])
                        sy.op("dve", lambda e: e.tensor_tensor(out=ta_, in0=prev, in1=c1, op=ALU.mult), reads=rB + [AB], writes=[tAB])
                        sy.op("dve", lambda e: e.tensor_tensor(out=tb_, in0=tb_, in1=cur, op=ALU.add), reads=[tBB] + wB, writes=[tBB])
                        sy.op("dve", lambda e: e.tensor_tensor(out=cur, in0=ta_, in1=tb_, op=ALU.add), reads=[tAB, tBB], writes=wB)

                    c0B = StB[k]
                    if tt == 0:
                        sy.op("dve", lambda e: e.memset(Sk[:, 0, :, :], 0.0), writes=[c0B])
                    else:
                        sy.op("dve", lambda e: e.tensor_copy(out=Sk[:, 0, :, :], in_=St[1 - k][:, CT, :, :]), reads=[XlB[1 - k][7]], writes=[c0B])
                    step(Sk[:, 0, :, :], swp(Sk[:, 0, 1, :], 1), Sk[:, 1, :, :], A1, A2, tA[:, 0, :, :], tB_[:, 0, :, :], [c0B], [XlB[k][0]])
                    A1b = A1.unsqueeze(1).broadcast_to([128, 8, 2, NP])
                    A2b = A2.unsqueeze(1).broadcast_to([128, 8, 2, NP])
                    for l in range(1, 8):
                        step(Sv[:, :, l - 1, :, :], swp(Sv[:, 0, l - 1, 1, :], 8), Sv[:, :, l, :, :], A1b, A2b, tA[:], tB_[:], [XlB[k][l - 1]], [XlB[k][l]])
                    for blk in range(1, 8):
                        step(Sv[:, blk - 1, 7, :, :], swp(Sv[:, blk - 1, 7, 1, :], 1), Sv[:, blk, 7, :, :], PP1[:, 7, :, :], PP2[:, 7, :, :],
                             tA[:, 0, :, :], tB_[:, 0, :, :], [XlB[k][7]], [XlB[k][7]])
                    Cv = Sv[:, 0:7, 7, :, :]
                    Cs = swp(Sv[:, 0, 7, 1, :], 7)
                    for l in range(7):
                        q = l % 2
                        p1 = PP1[:, l, :, :].unsqueeze(1).broadcast_to([128, 7, 2, NP])
                        p2 = PP2[:, l, :, :].unsqueeze(1).broadcast_to([128, 7, 2, NP])
                        cur = Sv[:, 1:8, l, :, :]
                        sy.op("dve", lambda e, q=q, p2=p2: e.tensor_tensor(out=tD[q], in0=Cs, in1=p2, op=ALU.mult), reads=[XlB[k][7], AB], writes=[tDB[q]])
                        sy.op("dve", lambda e, q=q, p1=p1: e.tensor_tensor(out=tC[q], in0=Cv, in1=p1, op=ALU.mult), reads=[XlB[k][7], AB], writes=[tCB[q]])
                        sy.op("dve", lambda e, q=q, cur=cur: e.tensor_tensor(out=tD[q], in0=tD[q], in1=cur, op=ALU.add), reads=[tDB[q], XlB[k][l]], writes=[tDB[q]])
                        sy.op("dve", lambda e, q=q, cur=cur: e.tensor_tensor(out=cur, in0=tC[q], in1=tD[q], op=ALU.add), reads=[tCB[q], tDB[q]], writes=[XlB[k][l]])

                def stage_c(tt):
                    c0 = tt * CT
                    k = tt % 2
                    sy.op("act", lambda e: e.activation(out=Sb[k][:], in_=St[k][:, 0:CT, :, :], func=AF.Identity), reads=[StB[k]] + XlB[k], writes=[SbB[k]])
                    for gb in range(8):
                        pb = 2 + cnt["y"] % 2
                        cnt["y"] += 1
                        fns = []
                        for gl in range(8):
                            g = gb * 8 + gl
                            jp, e_ = g // 2, g % 2
                            out = self.ps[pb][:, gl * CT:(gl + 1) * CT]
                            fns.append(lambda e, g=g, out=out: e.matmul(out, lhsT=Tm[:, g, :], rhs=U2[:, g, c0:c0 + CT], start=True, stop=False))
                            fns.append(lambda e, jp=jp, e_=e_, out=out: e.matmul(out, lhsT=CtR[64 * e_:64 * e_ + 64, jp, :], rhs=Sb[k][64 * e_:64 * e_ + 64, :, 0, jp],
                                                                              start=False, stop=False))
                            fns.append(lambda e, jp=jp, e_=e_, out=out: e.matmul(out, lhsT=CtI[64 * e_:64 * e_ + 64, jp, :], rhs=Sb[k][64 * e_:64 * e_ + 64, :, 1, jp],
                                                                              start=False, stop=True))
                        sy.group("pe", fns, reads=[TmB[gb * 2], TmB[gb * 2 + 1], CtB[gb], U2B[gb][tt], SbB[k]], writes=[self.psB[pb]])
                        ps = self.ps[pb]
                        sy.op("act", lambda e, ps=ps, gb=gb: e.activation(
                            out=U2[:, gb * 8:(gb + 1) * 8, c0:c0 + CT], in_=ps[:].rearrange("p (g c) -> p g c", g=8), func=AF.Gelu_apprx_tanh),
                            reads=[self.psB[pb]], writes=[U2B[gb][tt]])

                stage_a(0)
                stage_b(0)
                for tt in range(NTT):
                    if tt + 1 < NTT:
                        stage_a(tt + 1)
                        stage_b(tt + 1)
                    stage_c(tt)
            if dbg < 5:
                return
            with self.scope() as es3:
                zfm = self.sb(es3, "zfm", [128, NFT, 2, 8, 128], BF16)
                zfB = [[Buf(f"zfm{fo}_{cb}") for cb in range(2)] for fo in range(NFT)]
                ztm = [self.sb(es3, "ztm", [128, D], BF16) for _ in range(2)]
                ztmB = [Buf("ztm0"), Buf("ztm1")]
                czc = {"n": 0}

                def tr(cb, fo):
                    cz = czc["n"]
                    czc["n"] += 1
                    k = cz % 2
                    pa = 4 + cz % 2
                    pbk = 6 + cz % 2
                    psa = self.ps[pa][:].bitcast(BF16)
                    psb = self.ps[pbk][:].bitcast(BF16)
                    sy.group("pe", [lambda e, g8=g8: e.transpose(
                        psa[:, g8 * 128:(g8 + 1) * 128], U2[:, fo * 8 + g8, cb * 128:(cb + 1) * 128], self.idb[:]) for g8 in range(8)],
                        reads=[U2B[fo][2 * cb], U2B[fo][2 * cb + 1], self.idbB], writes=[self.psB[pa]])
                    sy.op("act", lambda e: e.activation(out=ztm[k][:].rearrange("p (t g h) -> p g t h", g=8, t=8),
                                                       in_=psa.rearrange("p (g t h) -> p g t h", g=8, t=8), func=AF.Identity),
                          reads=[self.psB[pa]], writes=[ztmB[k]])
                    sy.group("pe", [lambda e, t=t: e.transpose(psb[:, t * 128:(t + 1) * 128], ztm[k][:, t * 128:(t + 1) * 128], self.idb[:]) for t in range(8)],
                             reads=[ztmB[k], self.idbB], writes=[self.psB[pbk]])
                    sy.op("dve", lambda e: e.tensor_copy(out=zfm[:, fo, cb, :, :].rearrange("p t c -> p (t c)"), in_=psb),
                          reads=[self.psB[pbk]], writes=[zfB[fo][cb]])

                for fo in range(NFT):
                    tr(0, fo)
                wg = self.sb(es3, "wg", [128, NFT, D], BF16)
                wo_ = self.sb(es3, "wo5", [128, NFT, D], BF16)
                wgB, woB = Buf("wg"), Buf("wo5")
                sy.dma("pool", [lambda e, kt=kt: e.dma_start(out=wg[:, kt, :], in_=self.s5w[j, 1, kt * 128:(kt + 1) * 128, :]) for kt in range(NFT)], writes=[wgB])
                sy.dma("pool", [lambda e, kt=kt: e.dma_start(out=wo_[:, kt, :], in_=self.s5w[j, 2, kt * 128:(kt + 1) * 128, :]) for kt in range(NFT)], writes=[woB])
                z2 = [self.sb(es3, "z2", [128, NFT, TT], BF16) for _ in range(2)]
                z2B = [[Buf(f"z2_{b}_{fo}") for fo in range(NFT)] for b in range(2)]
                sg = [self.sb(es3, "sg", [128, TT]) for _ in range(2)]
                sgB = [Buf("sg0"), Buf("sg1")]
                cnt = {"a": 0, "b": 0, "s": 0}

                tiles = [(0, 0), (0, 1), (1, 0), (1, 1)]

                def zcols(kt, ti):
                    cb, th = tiles[ti]
                    return zfm[:, kt, cb, 4 * th:4 * th + 4, :].rearrange("p t c -> p (t c)")

                def gate(ti):
                    b = ti % 2
                    cb, th = tiles[ti]
                    for fo in range(NFT):
                        pb = cnt["a"] % 2
                        cnt["a"] += 1
                        k = cnt["s"] % 2
                        cnt["s"] += 1
                        sy.group("pe", [lambda e, kt=kt, pb=pb, fo=fo: e.matmul(
                            self.ps[pb][:], lhsT=wg[:, kt, fo * 128:(fo + 1) * 128], rhs=zcols(kt, ti),
                            start=(kt == 0), stop=(kt == NFT - 1)) for kt in range(NFT)],
                            reads=[wgB] + [zfB[kt][cb] for kt in range(NFT)], writes=[self.psB[pb]])
                        sy.op("act", lambda e, pb=pb, k=k: e.activation(out=sg[k][:], in_=self.ps[pb][:], func=AF.Sigmoid), reads=[self.psB[pb]], writes=[sgB[k]])
                        sy.op("pool", lambda e, k=k, fo=fo: e.tensor_tensor(out=z2[b][:, fo, :], in0=zcols(fo, ti), in1=sg[k][:], op=ALU.mult),
                              reads=[sgB[k], zfB[fo][cb]], writes=[z2B[b][fo]])

                def outp(ti):
                    b = ti % 2
                    cb, th = tiles[ti]
                    xv = self.x[:].rearrange("p f (c s) -> p f s c", s=8)
                    for fo in range(NFT):
                        pb = 2 + cnt["b"] % 2
                        cnt["b"] += 1
                        sy.group("pe", [lambda e, kt=kt, pb=pb, fo=fo: e.matmul(
                            self.ps[pb][:], lhsT=wo_[:, kt, fo * 128:(fo + 1) * 128], rhs=z2[b][:, kt, :], start=(kt == 0), stop=(kt == NFT - 1))
                            for kt in range(NFT)], reads=[woB] + z2B[b], writes=[self.psB[pb]])
                        xs = xv[:, fo, 4 * th:4 * th + 4, cb * 128:(cb + 1) * 128]
                        sy.op("dve", lambda e, pb=pb, xs=xs: e.tensor_tensor(out=xs, in0=xs, in1=self.ps[pb][:].rearrange("p (t c) -> p t c", t=4), op=ALU.add),
                              reads=[self.psB[pb]] + self.xB[fo], writes=self.xB[fo])

                gate(0)
                for fo in range(0, 4):
                    tr(1, fo)
                gate(1)
                outp(0)
                for fo in range(4, 8):
                    tr(1, fo)
                gate(2)
                outp(1)
                gate(3)
                outp(2)
                outp(3)

    def s5_precompute(self, j):
        nc, sy = self.nc, self.sy
        NG, NP = 64, 32
        sc = {
            "Tm": nc.dram_tensor(f"s5c_Tm{j}", [128, NG * 128], BF16, kind="Internal").ap(),
            "Bt": nc.dram_tensor(f"s5c_Bt{j}", [128, NP * 2 * 128], BF16, kind="Internal").ap(),
            "CtR": nc.dram_tensor(f"s5c_CtR{j}", [128, NP * 128], BF16, kind="Internal").ap(),
            "CtI": nc.dram_tensor(f"s5c_CtI{j}", [128, NP * 128], BF16, kind="Internal").ap(),
            "A": nc.dram_tensor(f"s5c_A{j}", [128, 1024], F32, kind="Internal").ap(),
        }
        self.s5c[j] = sc
        with self.scope() as es2:
            Tm = self.sb(es2, "Tm", [128, NG, 128], BF16)
            TmB = [Buf(f"Tm{i}") for i in range(16)]
            Btm = self.sb(es2, "Btm", [128, NP, 2, 128], BF16)
            BtB = [Buf(f"Btm{i}") for i in range(16)]
            CtR = self.sb(es2, "CtR", [128, NP, 128], BF16)
            CtI = self.sb(es2, "CtI", [128, NP, 128], BF16)
            CtB = [Buf(f"Ct{i}") for i in range(8)]
            A1 = self.sb(es2, "A1", [128, 8, 2, NP])
            A2 = self.sb(es2, "A2", [128, 8, 2, NP])
            AB = Buf("A12")
            self.s5_prep(es2, j, Tm, TmB, Btm, BtB, CtR, CtI, CtB, A1, A2, AB)
            sy.dma("sp", [lambda e, h_=h_: e.dma_start(out=sc["Tm"][:, h_ * 2048:(h_ + 1) * 2048], in_=Tm[:, h_ * 16:(h_ + 1) * 16, :].rearrange("p g c -> p (g c)")) for h_ in range(4)], reads=TmB)
            sy.dma("sp", [lambda e, h_=h_: e.dma_start(out=sc["Bt"][:, h_ * 2048:(h_ + 1) * 2048], in_=Btm[:, h_ * 8:(h_ + 1) * 8, :, :].rearrange("p j r c -> p (j r c)")) for h_ in range(4)], reads=BtB)
            sy.dma("sp", [lambda e: e.dma_start(out=sc["CtR"], in_=CtR[:].rearrange("p j c -> p (j c)")),
                          lambda e: e.dma_start(out=sc["CtI"], in_=CtI[:].rearrange("p j c -> p (j c)"))], reads=CtB)
            sy.dma("sp", [lambda e: e.dma_start(out=sc["A"][:, 0:512], in_=A1[:].rearrange("p l a b -> p (l a b)")),
                          lambda e: e.dma_start(out=sc["A"][:, 512:1024], in_=A2[:].rearrange("p l a b -> p (l a b)"))], reads=[AB])

    def s5_prep(self, es_out, j, Tm, TmB, Btm, BtB, CtR, CtI, CtB, A1, A2, AB):
        sy = self.sy
        NP = 32
        TWO_PI = 2.0 * math.pi
        with self.scope() as es:
            prm = self.sb(es, "prm", [128, S5NC])
            prmB = Buf("prm")
            sy.dma("sp", [lambda e: e.dma_start(out=prm[:], in_=self.s5p[j])], writes=[prmB])
            lre, lim, lst = prm[:, 0:32], prm[:, 32:64], prm[:, 64:96]
            bre = prm[:, 96:608].rearrange("p (j h) -> p j h", h=16)
            bim = prm[:, 608:1120].rearrange("p (j h) -> p j h", h=16)
            cre = prm[:, 1120:1632].rearrange("p (j h) -> p j h", h=16)
            cim = prm[:, 1632:2144].rearrange("p (j h) -> p j h", h=16)
            dvec = prm[:, 2144:2208]
            names = ["lr", "dt", "lrdt", "lidt", "nr", "den", "fr", "fi", "w1", "w2"]
            sm = {n: self.sb(es, "s5" + n, [128, NP]) for n in names}
            smB = {n: Buf("s5" + n) for n in names}

            def tt_(eng, out, oB, a, aB, b, bB, op):
                sy.op(eng, lambda e: e.tensor_tensor(out=out, in0=a, in1=b, op=op), reads=[aB, bB], writes=[oB])

            def V(n):
                return sm[n][:]
            sy.op("dve", lambda e: e.tensor_scalar(out=V("lr"), in0=lre, scalar1=-1e-4, scalar2=None, op0=ALU.min), reads=[prmB], writes=[smB["lr"]])
            sy.op("act", lambda e: e.activation(out=V("dt"), in_=lst, func=AF.Exp), reads=[prmB], writes=[smB["dt"]])
            tt_("dve", V("lrdt"), smB["lrdt"], V("lr"), smB["lr"], V("dt"), smB["dt"], ALU.mult)
            tt_("dve", V("lidt"), smB["lidt"], lim, prmB, V("dt"), smB["dt"], ALU.mult)
            NK = 24
            kv = self.sb(es, "kv", [128, NK])
            kvB = Buf("kv")
            sy.op("pool", lambda e: e.iota(kv[:, 0:16], pattern=[[1, 16]], base=-7, channel_multiplier=0, allow_small_or_imprecise_dtypes=True), writes=[kvB])
            sy.op("pool", lambda e: e.iota(kv[:, 16:NK], pattern=[[8, NK - 16]], base=16, channel_multiplier=0, allow_small_or_imprecise_dtypes=True), writes=[kvB])
            big = {n: self.sb(es, "s5" + n, [128, NK, NP]) for n in ["mag", "ang", "nf", "AR", "AI"]}
            bigB = {n: Buf("s5" + n) for n in big}
            ni = self.sb(es, "s5ni", [128, NK, NP], I32)
            niB = Buf("s5ni")
            kvb = kv[:].unsqueeze(2).broadcast_to([128, NK, NP])
            sy.op("dve", lambda e: e.tensor_tensor(out=big["mag"][:], in0=kvb, in1=V("lrdt").unsqueeze(1).broadcast_to([128, NK, NP]), op=ALU.mult),
                  reads=[kvB, smB["lrdt"]], writes=[bigB["mag"]])
            sy.op("act", lambda e: e.activation(out=big["mag"][:], in_=big["mag"][:], func=AF.Exp), reads=[bigB["mag"]], writes=[bigB["mag"]])
            sy.op("dve", lambda e: e.tensor_tensor(out=big["ang"][:], in0=kvb, in1=V("lidt").unsqueeze(1).broadcast_to([128, NK, NP]), op=ALU.mult),
                  reads=[kvB, smB["lidt"]], writes=[bigB["ang"]])

            def sin_of(dst, dstB, shift):
                nf, nfB, ang, angB = big["nf"], bigB["nf"], big["ang"], bigB["ang"]
                sy.op("dve", lambda e: e.tensor_scalar(out=nf[:], in0=ang[:], scalar1=shift, scalar2=1.0 / TWO_PI, op0=ALU.add, op1=ALU.mult), reads=[angB], writes=[nfB])
                sy.op("dve", lambda e: e.tensor_copy(out=ni[:], in_=nf[:]), reads=[nfB], writes=[niB])
                sy.op("dve", lambda e: e.tensor_copy(out=nf[:], in_=ni[:]), reads=[niB], writes=[nfB])
                sy.op("dve", lambda e: e.scalar_tensor_tensor(out=nf[:], in0=nf[:], scalar=-TWO_PI, in1=ang[:], op0=ALU.mult, op1=ALU.add), reads=[nfB, angB], writes=[nfB])
                sy.op("dve", lambda e: e.tensor_scalar(out=nf[:], in0=nf[:], scalar1=shift, scalar2=math.pi, op0=ALU.add, op1=ALU.min), reads=[nfB], writes=[nfB])
                sy.op("dve", lambda e: e.tensor_scalar(out=nf[:], in0=nf[:], scalar1=-math.pi, scalar2=None, op0=ALU.max), reads=[nfB], writes=[nfB])
                sy.op("act", lambda e: e.activation(out=dst[:], in_=nf[:], func=AF.Sin), reads=[nfB], writes=[dstB])
            sin_of(big["AI"], bigB["AI"], 0.0)
            sin_of(big["AR"], bigB["AR"], math.pi / 2)
            for n in ("AI", "AR"):
                sy.op("dve", lambda e, n=n: e.tensor_tensor(out=big[n][:], in0=big[n][:], in1=big["mag"][:], op=ALU.mult), reads=[bigB[n], bigB["mag"]], writes=[bigB[n]])
            AR, AI, ARB, AIB = big["AR"], big["AI"], bigB["AR"], bigB["AI"]
            import os
            pd_ = int(os.environ.get("PREPDBG", "9"))
            if pd_ < 2:
                return
            for r_ in range(2):
                sy.op("dve", lambda e, r_=r_: e.tensor_copy(out=A1[:, :, r_, :], in_=AR[:, 15:23, :]), reads=[ARB], writes=[AB])
            sy.op("dve", lambda e: e.tensor_copy(out=A2[:, :, 1, :], in_=AI[:, 15:23, :]), reads=[AIB], writes=[AB])
            sy.op("dve", lambda e: e.tensor_scalar(out=A2[:, :, 0, :], in0=AI[:, 15:23, :], scalar1=-1.0, scalar2=None, op0=ALU.mult), reads=[AIB], writes=[AB])
            a1r, a1i = AR[:, 8, :], AI[:, 8, :]
            sy.op("dve", lambda e: e.tensor_scalar(out=V("nr"), in0=a1r, scalar1=-1.0, scalar2=None, op0=ALU.add), reads=[ARB], writes=[smB["nr"]])
            tt_("dve", V("den"), smB["den"], V("lr"), smB["lr"], V("lr"), smB["lr"], ALU.mult)
            tt_("dve", V("w1"), smB["w1"], lim, prmB, lim, prmB, ALU.mult)
            tt_("dve", V("den"), smB["den"], V("den"), smB["den"], V("w1"), smB["w1"], ALU.add)
            sy.op("dve", lambda e: e.reciprocal(out=V("den"), in_=V("den")), reads=[smB["den"]], writes=[smB["den"]])
            tt_("dve", V("w1"), smB["w1"], V("nr"), smB["nr"], V("lr"), smB["lr"], ALU.mult)
            tt_("dve", V("w2"), smB["w2"], a1i, AIB, lim, prmB, ALU.mult)
            tt_("dve", V("fr"), smB["fr"], V("w1"), smB["w1"], V("w2"), smB["w2"], ALU.add)
            tt_("dve", V("fr"), smB["fr"], V("fr"), smB["fr"], V("den"), smB["den"], ALU.mult)
            tt_("dve", V("w1"), smB["w1"], a1i, AIB, V("lr"), smB["lr"], ALU.mult)
            tt_("dve", V("w2"), smB["w2"], V("nr"), smB["nr"], lim, prmB, ALU.mult)
            tt_("dve", V("fi"), smB["fi"], V("w1"), smB["w1"], V("w2"), smB["w2"], ALU.subtract)
            tt_("dve", V("fi"), smB["fi"], V("fi"), smB["fi"], V("den"), smB["den"], ALU.mult)
            BBr = self.sb(es, "BBr", [128, NP, 16])
            BBi = self.sb(es, "BBi", [128, NP, 16])
            BBrB, BBiB = Buf("BBr"), Buf("BBi")
            IZ = self.sb(es, "IZ", [128, 2, 256])
            IZB = Buf("IZ")
            w3a_t = IZ[:].rearrange("p a b -> p (a b)").rearrange("p (j h) -> p j h", h=16)

            class _V:
                def __init__(self, ap): self.ap_ = ap
                def __getitem__(self, k): return self.ap_
            w3a = _V(w3a_t)
            w3b = self.sb(es, "w3b", [128, NP, 16])
            w3aB, w3bB = IZB, Buf("w3b")
            frb = V("fr").unsqueeze(2).broadcast_to([128, NP, 16])
            fib = V("fi").unsqueeze(2).broadcast_to([128, NP, 16])
            sy.op("dve", lambda e: e.tensor_tensor(out=w3a[:], in0=bre, in1=frb, op=ALU.mult), reads=[prmB, smB["fr"]], writes=[w3aB])
            sy.op("dve", lambda e: e.tensor_tensor(out=w3b[:], in0=bim, in1=fib, op=ALU.mult), reads=[prmB, smB["fi"]], writes=[w3bB])
            tt_("dve", BBr[:], BBrB, w3a[:], w3aB, w3b[:], w3bB, ALU.subtract)
            sy.op("dve", lambda e: e.tensor_tensor(out=w3a[:], in0=bim, in1=frb, op=ALU.mult), reads=[prmB, smB["fr"]], writes=[w3aB])
            sy.op("dve", lambda e: e.tensor_tensor(out=w3b[:], in0=bre, in1=fib, op=ALU.mult), reads=[prmB, smB["fi"]], writes=[w3bB])
            tt_("dve", BBi[:], BBiB, w3a[:], w3aB, w3b[:], w3bB, ALU.add)
            mask = self.sb(es, "msk", [128, 4, 8, 16])
            maskB = Buf("msk")
            sy.op("pool", lambda e: e.memset(mask[:], 1.0), writes=[maskB])
            sy.op("pool", lambda e: e.affine_select(out=mask[:], in_=mask[:], pattern=[[0, 4], [16, 8], [0, 16]], compare_op=ALU.is_ge, fill=0.0,
                                                   base=15, channel_multiplier=-1), reads=[maskB], writes=[maskB])
            ident = self.sb(es, "idn", [128, 128])
            identB = Buf("idn")
            sy.op("pool", lambda e: e.memset(ident[:], 0.0), writes=[identB])
            sy.op("pool", lambda e: e.affine_select(out=ident[:], in_=ident[:], pattern=[[-1, 128]], compare_op=ALU.not_equal, fill=1.0, base=0, channel_multiplier=1),
                  reads=[identB], writes=[identB])
            if pd_ < 3:
                return
            PBN = 4
            shp = [128, PBN, 8, 16]
            shp9 = [128, PBN, 9, 16]
            tmp_sets = [{n: self.sb(es, "s5t" + n, shp9 if n in ("CpR", "CpI") else shp) for n in ["BnR", "BnI", "BtR", "BtI", "CpR", "CpI"]}
                        for _ in range(2)]
            tmpB_sets = [{n: Buf("s5t" + n) for n in tmp_sets[0]} for _ in range(2)]
            scr = {n: self.sb(es, "s5t" + n, shp9 if n in ("x1", "x2") else shp) for n in ["x1", "x2", "y1", "y2"]}
            scrB = {n: Buf("s5t" + n) for n in scr}
            for d_, dB_ in zip(tmp_sets, tmpB_sets):
                d_.update(scr)
                dB_.update(scrB)
            tmp, tmpB = dict(tmp_sets[0]), dict(tmpB_sets[0])
            CpZ_sets = []
            for _ in range(2):
                zr = self.sb(es, "CpZR", [128, PBN, 2, 128])
                zi = self.sb(es, "CpZI", [128, PBN, 2, 128])
                zb = Buf("CpZ")
                sy.op("pool", lambda e, zr=zr: e.memset(zr[:], 0.0), writes=[zb])
                sy.op("pool", lambda e, zi=zi: e.memset(zi[:], 0.0), writes=[zb])
                CpZ_sets.append((zr, zi, zb))
            cnt = 0

            def cprod(eng, x1, x2, k0, Zr, Zi, ZB, p0, outR, outI, negI, outRB, outIB, npow=8):
                kk0, sgn = k0
                def pw(A):
                    t = A[:, kk0, p0:p0 + PBN]
                    return bass.AP(tensor=t.tensor, offset=t.offset, ap=[list(t.ap[0]), [1, PBN], [sgn * NP, npow], [0, 16]])
                def zz(Z):
                    t = Z[:, p0:p0 + PBN, :]
                    return bass.AP(tensor=t.tensor, offset=t.offset, ap=[list(t.ap[0]), [16, PBN], [0, npow], [1, 16]])
                X1, X2 = tmp[x1][:, :, 0:npow, :], tmp[x2][:, :, 0:npow, :]
                sy.op(eng, lambda e: e.tensor_tensor(out=X1[:], in0=pw(AR), in1=zz(Zr), op=ALU.mult), reads=[ARB] + ZB, writes=[tmpB[x1]])
                sy.op(eng, lambda e: e.tensor_tensor(out=X2[:], in0=pw(AI), in1=zz(Zi), op=ALU.mult), reads=[AIB] + ZB, writes=[tmpB[x2]])
                sy.op(eng, lambda e: e.tensor_tensor(out=outR, in0=X1[:], in1=X2[:], op=ALU.subtract), reads=[tmpB[x1], tmpB[x2]], writes=[outRB])
                sy.op(eng, lambda e: e.tensor_tensor(out=X1[:], in0=pw(AR), in1=zz(Zi), op=ALU.mult), reads=[ARB] + ZB, writes=[tmpB[x1]])
                sy.op(eng, lambda e: e.tensor_tensor(out=X2[:], in0=pw(AI), in1=zz(Zr), op=ALU.mult), reads=[AIB] + ZB, writes=[tmpB[x2]])
                if negI:
                    sy.op(eng, lambda e: e.tensor_scalar(out=X1[:], in0=X1[:], scalar1=-1.0, scalar2=None, op0=ALU.mult), reads=[tmpB[x1]], writes=[tmpB[x1]])
                    sy.op(eng, lambda e: e.tensor_tensor(out=outI, in0=X1[:], in1=X2[:], op=ALU.subtract), reads=[tmpB[x1], tmpB[x2]], writes=[outIB])
                else:
                    sy.op(eng, lambda e: e.tensor_tensor(out=outI, in0=X1[:], in1=X2[:], op=ALU.add), reads=[tmpB[x1], tmpB[x2]], writes=[outIB])

            CB = [prmB]
            sy.op("pool", lambda e: e.memset(IZ[:], 0.0), writes=[IZB])
            sy.op("pool", lambda e: e.tensor_copy(out=IZ[:, 0, 0:128], in_=ident[:]), reads=[identB], writes=[IZB])
            sy.op("pool", lambda e: e.tensor_copy(out=IZ[:, 1, 128:256], in_=ident[:]), reads=[identB], writes=[IZB])
            Lb = [self.sb(es, "Lb", [128, 128]) for _ in range(4)]
            LbB = [Buf(f"Lb{i}") for i in range(4)]
            prev_evacs = []
            for bi in range(NP // PBN):
                p0 = bi * PBN
                tmp.update(tmp_sets[bi % 2])
                tmpB.update(tmpB_sets[bi % 2])
                CpZR, CpZI, CpZB = CpZ_sets[bi % 2]
                POOL_ = "pool" if pd_ != 5 else "dve"
                cprod("dve", "x1", "x2", (7, -1), BBr, BBi, [BBrB, BBiB], p0, tmp["BnR"][:], tmp["BnI"][:], False, tmpB["BnR"], tmpB["BnI"])
                if pd_ < 4:
                    continue
                cprod(POOL_, "y1", "y2", (14, -1), BBr, BBi, [BBrB, BBiB], p0, tmp["BtR"][:], tmp["BtI"][:], False, tmpB["BtR"], tmpB["BtI"])
                cprod("dve", "x1", "x2", (7, 1), cre, cim, CB, p0, tmp["CpR"][:], tmp["CpI"][:], True, tmpB["CpR"], tmpB["CpI"], npow=9)
                ctb = CtB[0]
                for nm, dst in (("CpR", CtR), ("CpI", CtI)):
                    sy.op("act", lambda e, nm=nm, dst=dst: e.activation(out=dst[:, p0:p0 + PBN, :].rearrange("p j (t h) -> p j t h", h=16),
                                                                      in_=tmp[nm][:, :, 1:9, :], func=AF.Identity),
                          reads=[tmpB[nm]], writes=[ctb])
                if pd_ < 6:
                    continue
                for nm, zn in (("CpR", CpZR), ("CpI", CpZI)):
                    for e_ in range(2):
                        r0, r1 = 64 * e_, 64 * e_ + 64
                        sy.op("act", lambda e, nm=nm, zn=zn, e_=e_, r0=r0, r1=r1: e.activation(
                            out=zn[r0:r1, :, e_, :].rearrange("p j (s h) -> p j s h", h=16), in_=tmp[nm][r0:r1, :, 0:8, :], func=AF.Identity),
                            reads=[tmpB[nm]], writes=[CpZB])
                evacs = []
                for half in range(PBN // 2):
                    pb = 2 * (bi % 2) + half
                    fns = []
                    gs = []
                    for q in range(2):
                        jl = half * 2 + q
                        gs += [2 * (p0 + jl), 2 * (p0 + jl) + 1]
                        out = self.ps[pb][:, q * 256:(q + 1) * 256]
                        fns.append(lambda e, jl=jl, out=out: e.matmul(
                            out, lhsT=tmp["BnR"][:, jl, :, :].rearrange("p s h -> p (s h)"), rhs=CpZR[:, jl, :, :].rearrange("p e c -> p (e c)"),
                            start=True, stop=False))
                        fns.append(lambda e, jl=jl, out=out: e.matmul(
                            out, lhsT=tmp["BnI"][:, jl, :, :].rearrange("p s h -> p (s h)"), rhs=CpZI[:, jl, :, :].rearrange("p e c -> p (e c)"),
                            start=False, stop=False))
                        for e_ in range(2):
                            g = 2 * (p0 + jl) + e_
                            li_ = q * 2 + e_
                            sy.op("act", lambda e, g=g, li_=li_: e.activation(out=Lb[li_][:], in_=ident[:], func=AF.Identity, scale=dvec[:, g:g + 1]),
                                  reads=[identB, prmB], writes=[LbB[li_]])
                            fns.append(lambda e, out=out, li_=li_, e_=e_: e.matmul(out, lhsT=Lb[li_][:], rhs=IZ[:, e_, :], start=False, stop=(e_ == 1)))
                    sy.group("pe", fns, reads=[tmpB["BnR"], tmpB["BnI"], CpZB, IZB] + LbB, writes=[self.psB[pb]])

                    def ev_t(pb=pb, gs=gs):
                        sy.op("dve", lambda e: e.tensor_tensor(out=Tm[:, gs[0]:gs[0] + 4, :].rearrange("p g c -> p (g c)"), in0=self.ps[pb][:],
                                                               in1=mask[:].rearrange("p a t h -> p (a t h)"), op=ALU.mult),
                              reads=[self.psB[pb], maskB], writes=[TmB[gs[0] // 4]])
                    evacs.append(ev_t)
                for hp in range(PBN // 2):
                    pb = 4 + 2 * (bi % 2) + hp
                    fns = []
                    for q in range(2):
                        jl = hp * 2 + q
                        for ri, nm in enumerate(("BtR", "BtI")):
                            col = (q * 2 + ri) * 128
                            fns.append(lambda e, jl=jl, nm=nm, col=col, pb=pb: e.transpose(
                                self.ps[pb][:, col:col + 128], tmp[nm][:, jl, :, :].rearrange("p s h -> p (s h)"), ident[:]))
                    sy.group("pe", fns, reads=[tmpB["BtR"], tmpB["BtI"], identB], writes=[self.psB[pb]])
                    jp0 = p0 + hp * 2

                    def ev_b(pb=pb, jp0=jp0):
                        sy.op("act", lambda e: e.activation(out=Btm[:, jp0:jp0 + 2, :, :].rearrange("p j r c -> p (j r c)"), in_=self.ps[pb][:], func=AF.Identity),
                              reads=[self.psB[pb]], writes=[BtB[jp0 // 2]])
                    evacs.append(ev_b)
                for f_ in prev_evacs:
                    f_()
                prev_evacs = evacs
            for f_ in prev_evacs:
                f_()

    def final(self, s, next_s=None):
        sy = self.sy
        with self.scope() as es:
            sq = [self.sb(es, "fsq", [128, NFT, TT], BF16) for _ in range(2)]
            sqB = [Buf("fsq0"), Buf("fsq1")]
            rs = [self.sb(es, "frs", [128, TT]) for _ in range(2)]
            rsB = [Buf("frs0"), Buf("frs1")]
            ob = [self.sb(es, "fo", [128, NFT, TT]) for _ in range(2)]
            obB = [Buf("fo0"), Buf("fo1")]

            def square(tt):
                b = tt % 2
                xin = self.x[:, :, tt * TT:(tt + 1) * TT]
                sy.op("act", lambda e: e.activation(out=sq[b][:], in_=xin, func=AF.Square),
                      reads=[self.xB[ft][tt] for ft in range(NFT)], writes=[sqB[b]])

            square(0)
            for tt in range(NTT):
                b = tt % 2
                if tt + 1 < NTT:
                    square(tt + 1)
                pb = 6 + b
                sy.group("pe", [lambda e, ft=ft, pb=pb, b=b: e.matmul(self.ps[pb][:], lhsT=self.ones_bf[:], rhs=sq[b][:, ft, :],
                                                                      start=(ft == 0), stop=(ft == NFT - 1)) for ft in range(NFT)],
                         reads=[sqB[b], self.onesB], writes=[self.psB[pb]])
                sy.op("act", lambda e, pb=pb, b=b: e.activation(out=rs[b][:], in_=self.ps[pb][:], func=AF.Ln, scale=1.0 / D, bias=self.epsT[:]),
                      reads=[self.psB[pb], self.epsB], writes=[rsB[b]])
                sy.op("act", lambda e, b=b: e.activation(out=rs[b][:], in_=rs[b][:], func=AF.Exp, scale=-0.5), reads=[rsB[b]], writes=[rsB[b]])
                for ft in range(NFT):
                    sy.op("dve", lambda e, b=b, ft=ft, tt=tt: e.scalar_tensor_tensor(
                        out=ob[b][:, ft, :], in0=self.xt(ft, tt), scalar=self.gcol(8, ft), in1=rs[b][:],
                        op0=ALU.mult, op1=ALU.mult),
                        reads=[self.xB[ft][tt], rsB[b], self.gB], writes=[obB[b]])
                sy.dma("sp", [lambda e, b=b, tt=tt: e.dma_start(
                    out=self.yT[s, :, tt * TT:(tt + 1) * TT].rearrange("(ft p) t -> p ft t", p=128), in_=ob[b][:])],
                    reads=[obB[b]])
                if next_s is not None:
                    for ft in range(NFT):
                        sy.dma("sp", [lambda e, ft=ft, tt=tt: e.dma_start(out=self.xt(ft, tt), in_=self.xT[next_s, ft * 128:(ft + 1) * 128, tt * TT:(tt + 1) * TT])],
                               writes=[self.xB[ft][tt]])


def _vecs(inp):
    allg = np.concatenate([inp["norm_mix"], inp["norm_mlp"], np.asarray(inp["norm_final"])[None]], 0)
    v = np.zeros((128, NV), np.float32)
    v[:, 0:72] = allg.reshape(9, NFT, 128).transpose(2, 0, 1).reshape(128, 72)
    v[:, V_PSCALE:V_PSCALE + 8] = np.asarray(inp["pool_scale"])[0].reshape(NFT, 128).T
    v[:, V_SUBLN] = np.asarray(inp["da_subln"])[0]
    lam = np.concatenate([np.asarray(inp[k])[0] for k in ("da_lam_q1", "da_lam_k1", "da_lam_q2", "da_lam_k2")])
    v[:, V_LAM:V_LAM + 256] = lam[None, :]
    return v


def _s5p(inp):
    out = np.zeros((2, 128, S5NC), np.float32)
    for j in range(2):
        def pl(a):
            a = np.asarray(a)
            sh = a.shape[2:]
            a = a.reshape((32, 2, 64) + sh)
            a = np.moveaxis(a, 0, 2)
            return a.reshape((128, 32) + sh)
        lre = pl(inp["s5_lam_re"][j])
        lim = pl(inp["s5_lam_im"][j])
        lst = pl(np.repeat(np.asarray(inp["s5_log_step"][j])[:, None], 64, 1))
        bre = pl(inp["s5_b_re"][j])
        bim = pl(inp["s5_b_im"][j])
        cre = pl(np.asarray(inp["s5_c_re"][j]).transpose(0, 2, 1))
        cim = pl(np.asarray(inp["s5_c_im"][j]).transpose(0, 2, 1))
        dv = np.tile(np.asarray(inp["s5_d"][j]).reshape(64, 16).T, (8, 1))
        o = out[j]
        o[:, 0:32] = lre
        o[:, 32:64] = lim
        o[:, 64:96] = lst
        o[:, 96:608] = bre.reshape(128, 512)
        o[:, 608:1120] = bim.reshape(128, 512)
        o[:, 1120:1632] = cre.reshape(128, 512)
        o[:, 1632:2144] = cim.reshape(128, 512)
        o[:, 2144:2208] = dv
    return out


def host_shared(inp):
    f = lambda a: np.ascontiguousarray(np.asarray(a, np.float32))
    return {
        "vecs": _vecs(inp),
        "w1": f(inp["mlp_w1"]),
        "w2": f(inp["mlp_w2"]),
        "poolw": f(inp["pool_w"][0]),
        "wqkv": f(inp["da_w_qkv"][0]),
        "wo": f(inp["da_w_o"][0]),
        "s5w": f(np.stack([inp["s5_w_in"], inp["s5_w_gate"], inp["s5_w_out"]], 1)),
        "s5p": _s5p(inp),
    }


def kernel(**inputs):
    x = np.asarray(inputs["x"], np.float32)
    prog = Prog()
    nc = prog.build()
    shared = host_shared(inputs)
    in_maps = []
    for c in range(N_CORES):
        xs = x[c * SEQ_PER_CORE:(c + 1) * SEQ_PER_CORE]
        m = dict(shared)
        m["xT"] = np.ascontiguousarray(xs.transpose(0, 2, 1))
        in_maps.append(m)
    res = run_bass_kernel_spmd(nc, in_maps, core_ids=list(range(N_CORES)))
    out = np.empty_like(x)
    for c in range(N_CORES):
        out[c * SEQ_PER_CORE:(c + 1) * SEQ_PER_CORE] = res.results[c]["yT"].transpose(0, 2, 1)
    return out
```

```python
import contextlib
import math
import numpy as np
import concourse.bass as bass
import concourse.mybir as mybir
from concourse.bass_utils import run_bass_kernel_spmd

F32 = mybir.dt.float32
BF16 = mybir.dt.bfloat16
I32 = mybir.dt.int32
AF = mybir.ActivationFunctionType
ALU = mybir.AluOpType

D = 1024
S = 2048
NFT = 8
TT = 512
NTT = S // TT
DFF = 4096
EPS = 1e-6
DEPTH = 4
N_CORES = 8
SEQ_PER_CORE = 2
V_PSCALE = 72
V_SUBLN = 80
V_LAM = 81
NV = 81 + 256
S5NC = 96 + 2048 + 64
LAM_INIT = 0.8 - 0.6 * math.exp(-0.3 * 1)
POOL_W = (2, 4, 8, 16)


class Buf:
    __slots__ = ("name", "w", "r")

    def __init__(self, name):
        self.name = name
        self.w = None
        self.r = {}


class Eng:
    def __init__(self, name, e, sem):
        self.name = name
        self.e = e
        self.sem = sem
        self.count = 0
        self.waited = {}


class Sync:
    def __init__(self, nc, n_dma_sems=12):
        self.nc = nc
        self.engs = {}
        for name, e in (("pe", nc.tensor), ("act", nc.scalar), ("dve", nc.vector),
                        ("pool", nc.gpsimd), ("sp", nc.sync)):
            self.engs[name] = Eng(name, e, nc.alloc_semaphore(name="s_" + name))
        self.dma_sems = {q: [[nc.alloc_semaphore(name=f"d{q}{i}"), 0] for i in range(n_dma_sems)] for q in ("sp", "pool")}
        self.dma_rr = {"sp": 0, "pool": 0}
        self.ninst = 0

    def _wait(self, eng, ticket):
        sem, val, src = ticket
        if src == "pe" and eng.name == "pe":
            return
        k = id(sem)
        if eng.waited.get(k, 0) >= val:
            return
        eng.e.wait_ge(sem, val)
        eng.waited[k] = val

    def _deps(self, eng, reads, writes):
        need = {}

        def add(t):
            if t is None:
                return
            k = id(t[0])
            if k not in need or need[k][1] < t[1]:
                need[k] = t
        for b in reads:
            add(b.w)
        for b in writes:
            add(b.w)
            for t in b.r.values():
                add(t)
        for t in need.values():
            self._wait(eng, t)

    @staticmethod
    def _mark(ticket, key, reads, writes):
        for b in reads:
            b.r[key] = ticket
        for b in writes:
            b.w = ticket
            b.r = {}

    def op(self, engname, fn, reads=(), writes=()):
        eng = self.engs[engname]
        self._deps(eng, reads, writes)
        inst = fn(eng.e)
        eng.count += 1
        inst.then_inc(eng.sem, 1)
        t = (eng.sem, eng.count, eng.name)
        self._mark(t, eng.name, reads, writes)
        self.ninst += 1
        return t

    def group(self, engname, fns, reads=(), writes=()):
        eng = self.engs[engname]
        self._deps(eng, reads, writes)
        inst = None
        for fn in fns:
            inst = fn(eng.e)
            self.ninst += 1
        eng.count += 1
        inst.then_inc(eng.sem, 1)
        t = (eng.sem, eng.count, eng.name)
        self._mark(t, eng.name, reads, writes)
        return t

    def dma(self, engname, fns, reads=(), writes=()):
        eng = self.engs[engname]
        self._deps(eng, reads, writes)
        pool_ = self.dma_sems[engname]
        slot = pool_[self.dma_rr[engname]]
        self.dma_rr[engname] = (self.dma_rr[engname] + 1) % len(pool_)
        sem, total = slot
        if total > 0:
            self._wait(eng, (sem, total, None))
        for fn in fns:
            fn(eng.e).then_inc(sem, 16)
            total += 16
            self.ninst += 1
        slot[1] = total
        t = (sem, total, None)
        self._mark(t, "dma%d" % id(sem), reads, writes)
        return t

    def barrier(self, names=("pe", "act", "dve", "pool", "sp")):
        for n in names:
            eng = self.engs[n]
            for m in names:
                if m != n:
                    o = self.engs[m]
                    if o.count > 0:
                        self._wait(eng, (o.sem, o.count, o.name))
            for q in self.dma_sems:
                for sem, total in self.dma_sems[q]:
                    if total > 0:
                        self._wait(eng, (sem, total, None))


class Prog:
    def __init__(self, n_seq=SEQ_PER_CORE, layers=None):
        if layers is None:
            layers = [(i, i % 3, True) for i in range(DEPTH)]
        self.layers = layers
        self.n_seq = n_seq
        self.nc = bass.Bass("TRN2", target_bir_lowering=False)
        self.es = contextlib.ExitStack()
        self._uid = 0

    def dram_in(self, name, shape):
        return self.nc.dram_tensor(name, list(shape), F32, kind="ExternalInput").ap()

    def sb(self, es, name, shape, dt=F32):
        self._uid += 1
        return es.enter_context(self.nc.sbuf_tensor(f"{name}_{self._uid}", list(shape), dt))

    @contextlib.contextmanager
    def scope(self):
        with contextlib.ExitStack() as es:
            yield es
            self.sy.barrier()

    def build(self):
        nc = self.nc
        ns = self.n_seq
        self.xT = self.dram_in("xT", [ns, D, S])
        self.gall = self.dram_in("vecs", [128, NV])
        self.w1 = self.dram_in("w1", [DEPTH, D, DFF])
        self.w2 = self.dram_in("w2", [DEPTH, DFF, D])
        self.poolw = self.dram_in("poolw", [4, 256, 256])
        self.wqkv = self.dram_in("wqkv", [D, 3 * D])
        self.wo = self.dram_in("wo", [D, D])
        self.s5w = self.dram_in("s5w", [2, 3, D, D])
        self.s5p = self.dram_in("s5p", [2, 128, S5NC])
        self.yT = nc.dram_tensor("yT", [ns, D, S], F32, kind="ExternalOutput").ap()
        with self.es as es:
            self.sy = Sync(nc)
            sy = self.sy
            self.ps = [es.enter_context(nc.psum_tensor(f"psb{i}", [128, 512], F32)) for i in range(8)]
            self.psB = [Buf(f"ps{i}") for i in range(8)]
            self.ones_bf = self.sb(es, "ones", [128, 128], BF16)
            self.onesB = Buf("ones")
            sy.op("dve", lambda e: e.memset(self.ones_bf[:], 1.0), writes=[self.onesB])
            self.idb = self.sb(es, "idb", [128, 128], BF16)
            self.idbB = Buf("idb")
            with self.scope() as es0:
                idf = self.sb(es0, "idf0", [128, 128])
                idfB = Buf("idf0")
                sy.op("pool", lambda e: e.memset(idf[:], 0.0), writes=[idfB])
                sy.op("pool", lambda e: e.affine_select(out=idf[:], in_=idf[:], pattern=[[-1, 128]], compare_op=ALU.not_equal, fill=1.0, base=0, channel_multiplier=1),
                      reads=[idfB], writes=[idfB])
                sy.op("dve", lambda e: e.tensor_copy(out=self.idb[:], in_=idf[:]), reads=[idfB], writes=[self.idbB])
            self.epsT = self.sb(es, "eps", [128, 1])
            self.epsB = Buf("eps")
            sy.op("dve", lambda e: e.memset(self.epsT[:], EPS), writes=[self.epsB])
            self.g = self.sb(es, "gall", [128, NV])
            self.gB = Buf("g")
            sy.dma("sp", [lambda e: e.dma_start(out=self.g[:], in_=self.gall)], writes=[self.gB])
            self.x = self.sb(es, "x", [128, NFT, S])
            self.xB = [[Buf(f"x{ft}_{tt}") for tt in range(NTT)] for ft in range(NFT)]
            self.load_x(0)
            self.s5c = {}
            self.rope_c = None
            for (li_, kind_, _m) in self.layers:
                if kind_ == 0 and (li_ // 3) not in self.s5c:
                    self.s5_precompute(li_ // 3)
            for s in range(ns):
                self.seq(s, s + 1 if s + 1 < ns else None)
            sy.barrier()
        return nc

    def gcol(self, idx, ft):
        c = idx * NFT + ft
        return self.g[:, c:c + 1]

    def xt(self, ft, tt):
        return self.x[:, ft, tt * TT:(tt + 1) * TT]

    def load_x(self, s):
        sy = self.sy
        for tt in range(NTT):
            for ft in range(NFT):
                sy.dma("sp", [lambda e, ft=ft, tt=tt: e.dma_start(out=self.xt(ft, tt), in_=self.xT[s, ft * 128:(ft + 1) * 128, tt * TT:(tt + 1) * TT])],
                       writes=[self.xB[ft][tt]])

    def seq(self, s, next_s=None):
        sy = self.sy
        for (li, kind, do_mlp) in self.layers:
            if kind == 0:
                self.s5(li, li // 3)
            elif kind == 1:
                self.attn(li)
            elif kind == 2:
                self.pool(li)
            if do_mlp:
                self.mlp(li)
        self.final(s, next_s)

    def rmsnorm(self, es, gidx, h, hB, inline=False):
        sy = self.sy
        if inline:
            self._rmsnorm(es, gidx, h, hB)
            return
        with self.scope() as es2:
            self._rmsnorm(es2, gidx, h, hB)

    def _rmsnorm(self, es, gidx, h, hB):
        sy = self.sy
        sq = [self.sb(es, "sq", [128, NFT, TT], BF16) for _ in range(2)]
        sqB = [Buf("sq0"), Buf("sq1")]
        rs = [self.sb(es, "rs", [128, TT]) for _ in range(2)]
        rsB = [Buf("rs0"), Buf("rs1")]
        pbank = [6, 7]

        def square(tt):
            b = tt % 2
            xin = self.x[:, :, tt * TT:(tt + 1) * TT]
            sy.op("act", lambda e: e.activation(out=sq[b][:], in_=xin, func=AF.Square),
                  reads=[self.xB[ft][tt] for ft in range(NFT)], writes=[sqB[b]])

        square(0)
        for tt in range(NTT):
            b = tt % 2
            if tt + 1 < NTT:
                square(tt + 1)
            pb = pbank[b]
            sy.group("pe", [lambda e, b=b, ft=ft, pb=pb: e.matmul(self.ps[pb][:], lhsT=self.ones_bf[:], rhs=sq[b][:, ft, :],
                                                                start=(ft == 0), stop=(ft == NFT - 1)) for ft in range(NFT)],
                     reads=[sqB[b], self.onesB], writes=[self.psB[pb]])
            sy.op("act", lambda e, b=b, pb=pb: e.activation(out=rs[b][:], in_=self.ps[pb][:], func=AF.Ln, scale=1.0 / D, bias=self.epsT[:]),
                  reads=[self.psB[pb], self.epsB], writes=[rsB[b]])
            sy.op("act", lambda e, b=b: e.activation(out=rs[b][:], in_=rs[b][:], func=AF.Exp, scale=-0.5), reads=[rsB[b]], writes=[rsB[b]])
            for ft in range(NFT):
                sy.op("dve", lambda e, b=b, ft=ft, tt=tt: e.scalar_tensor_tensor(
                    out=h[:, ft, tt * TT:(tt + 1) * TT], in0=self.xt(ft, tt), scalar=self.gcol(gidx, ft), in1=rs[b][:],
                    op0=ALU.mult, op1=ALU.mult),
                    reads=[self.xB[ft][tt], rsB[b], self.gB], writes=[hB[ft][tt]])

    def mlp(self, li):
        sy = self.sy
        HC = 4
        NHC = DFF // (HC * 128)
        with self.scope() as es:
            h = self.sb(es, "h", [128, NFT, S], BF16)
            hB = [[Buf(f"h{ft}_{tt}") for tt in range(NTT)] for ft in range(NFT)]
            hid = [self.sb(es, "hid", [128, HC, S], BF16) for _ in range(2)]
            hidB = [[[Buf(f"hid{b}_{hi}_{tt}") for tt in range(NTT)] for hi in range(HC)] for b in range(2)]
            w1c = [self.sb(es, "w1c", [128, NFT, HC * 128], BF16) for _ in range(2)]
            w1B = [Buf("w1c0"), Buf("w1c1")]
            w2c = [self.sb(es, "w2c", [128, HC, D], BF16) for _ in range(2)]
            w2B = [Buf("w2c0"), Buf("w2c1")]
            rt = [self.sb(es, "rt", [128, TT]) for _ in range(3)]
            rtB = [Buf(f"rt{i}") for i in range(3)]
            cnt = {"f": 0, "s": 0, "r": 0}

            def load(hc):
                b = hc % 2
                c0 = hc * HC * 128
                sy.dma("pool", [lambda e: e.dma_start(out=w1c[b][:], in_=self.w1[li, :, c0:c0 + HC * 128].rearrange("(ft p) n -> p ft n", p=128))],
                       writes=[w1B[b]])
                sy.dma("pool", [lambda e: e.dma_start(out=w2c[b][:], in_=self.w2[li, c0:c0 + HC * 128, :].rearrange("(hi p) n -> p hi n", p=128))],
                       writes=[w2B[b]])

            def first(hc):
                b = hc % 2
                for tt in range(NTT):
                    for h2 in range(HC // 2):
                        pbs = [2 * (cnt["f"] % 2), 2 * (cnt["f"] % 2) + 1]
                        cnt["f"] += 1
                        fns = []
                        for q, pb in enumerate(pbs):
                            hi = 2 * h2 + q
                            for ft in range(NFT):
                                fns.append(lambda e, ft=ft, pb=pb, hi=hi: e.matmul(
                                    self.ps[pb][:], lhsT=w1c[b][:, ft, hi * 128:(hi + 1) * 128], rhs=h[:, ft, tt * TT:(tt + 1) * TT],
                                    start=(ft == 0), stop=(ft == NFT - 1)))
                        sy.group("pe", fns, reads=[w1B[b]] + [hB[ft][tt] for ft in range(NFT)], writes=[self.psB[pb] for pb in pbs])
                        for q, pb in enumerate(pbs):
                            hi = 2 * h2 + q
                            r = cnt["r"] % 3
                            cnt["r"] += 1
                            sy.op("act", lambda e, pb=pb, r=r: e.activation(out=rt[r][:], in_=self.ps[pb][:], func=AF.Relu),
                                  reads=[self.psB[pb]], writes=[rtB[r]])
                            sy.op("pool", lambda e, r=r, hi=hi: e.tensor_tensor(
                                out=hid[b][:, hi, tt * TT:(tt + 1) * TT], in0=rt[r][:], in1=rt[r][:], op=ALU.mult),
                                reads=[rtB[r]], writes=[hidB[b][hi][tt]])

            def second(hc):
                b = hc % 2
                for tt in range(NTT):
                    for f2 in range(NFT // 2):
                        pbs = [4 + 2 * (cnt["s"] % 2), 5 + 2 * (cnt["s"] % 2)]
                        cnt["s"] += 1
                        fns = []
                        for q, pb in enumerate(pbs):
                            fo = 2 * f2 + q
                            for hi in range(HC):
                                fns.append(lambda e, hi=hi, pb=pb, fo=fo: e.matmul(
                                    self.ps[pb][:], lhsT=w2c[b][:, hi, fo * 128:(fo + 1) * 128], rhs=hid[b][:, hi, tt * TT:(tt + 1) * TT],
                                    start=(hi == 0), stop=(hi == HC - 1)))
                        sy.group("pe", fns, reads=[w2B[b]] + [hidB[b][hi][tt] for hi in range(HC)], writes=[self.psB[pb] for pb in pbs])
                        for q, pb in enumerate(pbs):
                            fo = 2 * f2 + q
                            sy.op("dve", lambda e, pb=pb, fo=fo: e.tensor_tensor(
                                out=self.xt(fo, tt), in0=self.xt(fo, tt), in1=self.ps[pb][:], op=ALU.add),
                                reads=[self.psB[pb], self.xB[fo][tt]], writes=[self.xB[fo][tt]])

            import os
            dbg = int(os.environ.get("MLPDBG", "9"))
            load(0)
            load(1)
            self.rmsnorm(es, 4 + li, h, hB, inline=True)
            if dbg >= 2:
                first(0)
            for hc in range(NHC):
                if hc + 1 < NHC and dbg >= 2:
                    first(hc + 1)
                if dbg >= 3:
                    second(hc)
                if hc + 2 < NHC:
                    load(hc + 2)

    def pool(self, li):
        sy = self.sy
        with self.scope() as es:
            h = self.sb(es, "hp", [128, NFT, S])
            hB = [[Buf(f"hp{ft}_{tt}") for tt in range(NTT)] for ft in range(NFT)]
            self.rmsnorm(es, li, h, hB)
            p = self.sb(es, "pp", [128, NFT, S], BF16)
            pB = [Buf(f"pp{ft}") for ft in range(NFT)]
            wp = self.sb(es, "wp", [128, 4, 2, 256], BF16)
            wpB = Buf("wp")
            sy.dma("pool", [lambda e: e.dma_start(out=wp[:], in_=self.poolw.rearrange("g (kt p) d -> p g kt d", p=128))], writes=[wpB])
            rc = self.sb(es, "rc", [128, 16])
            rcB = Buf("rc")
            sy.op("pool", lambda e: e.iota(rc[:], pattern=[[1, 16]], base=1, channel_multiplier=0, allow_small_or_imprecise_dtypes=True), writes=[rcB])
            sy.op("dve", lambda e: e.reciprocal(out=rc[:], in_=rc[:]), reads=[rcB], writes=[rcB])
            ab = [[self.sb(es, "pa", [128, S]) for _ in range(2)] for _ in range(2)]
            abB = [[Buf("pa"), Buf("pb")] for _ in range(2)]
            tmpc = [self.sb(es, "ptc", [128, 16]) for _ in range(2)]
            tmpB = [Buf("ptc0"), Buf("ptc1")]
            for ft in range(NFT):
                w = POOL_W[ft // 2]
                eng = "dve" if ft % 2 == 0 else "pool"
                k = ft % 2
                hall = [hB[ft][tt] for tt in range(NTT)]
                src, srcB = h[:, ft, :], hall
                sh = 1
                i = 0
                while sh < w:
                    dst, dstB = ab[k][i % 2], [abB[k][i % 2]]
                    sy.op(eng, lambda e, dst=dst, src=src, sh=sh: e.tensor_tensor(out=dst[:, sh:], in0=src[:, sh:], in1=src[:, :S - sh], op=ALU.add),
                          reads=srcB, writes=dstB)
                    sy.op(eng, lambda e, dst=dst, src=src, sh=sh: e.tensor_copy(out=dst[:, 0:sh], in_=src[:, 0:sh]), reads=srcB, writes=dstB)
                    src, srcB = dst[:], dstB
                    sh *= 2
                    i += 1
                sy.op("dve", lambda e, src=src, ft=ft, w=w: e.scalar_tensor_tensor(out=p[:, ft, :], in0=src, scalar=1.0 / w, in1=h[:, ft, :],
                                                                                 op0=ALU.mult, op1=ALU.subtract),
                      reads=srcB + hall, writes=[pB[ft]])
                sy.op("dve", lambda e, src=src, k=k, w=w: e.tensor_tensor(out=tmpc[k][:, 0:w - 1], in0=src[:, 0:w - 1], in1=rc[:, 0:w - 1], op=ALU.mult),
                      reads=srcB + [rcB], writes=[tmpB[k]])
                sy.op("dve", lambda e, k=k, ft=ft, w=w: e.tensor_tensor(out=p[:, ft, 0:w - 1], in0=tmpc[k][:, 0:w - 1], in1=h[:, ft, 0:w - 1], op=ALU.subtract),
                      reads=[tmpB[k]] + hall, writes=[pB[ft]])
            cnt = 0
            for tt in range(NTT):
                for g in range(4):
                    for oc in range(2):
                        pb = cnt % 4
                        cnt += 1
                        fo = 2 * g + oc
                        sy.group("pe", [lambda e, kt=kt, pb=pb, g=g, oc=oc, tt=tt: e.matmul(
                            self.ps[pb][:], lhsT=wp[:, g, kt, oc * 128:(oc + 1) * 128], rhs=p[:, 2 * g + kt, tt * TT:(tt + 1) * TT],
                            start=(kt == 0), stop=(kt == 1)) for kt in range(2)],
                            reads=[wpB, pB[2 * g], pB[2 * g + 1]], writes=[self.psB[pb]])
                        sy.op("dve", lambda e, pb=pb, fo=fo, tt=tt: e.scalar_tensor_tensor(
                            out=self.xt(fo, tt), in0=self.ps[pb][:], scalar=self.g[:, V_PSCALE + fo:V_PSCALE + fo + 1], in1=self.xt(fo, tt),
                            op0=ALU.mult, op1=ALU.add),
                            reads=[self.psB[pb], self.xB[fo][tt], self.gB], writes=[self.xB[fo][tt]])

    def attn(self, li):
        sy = self.sy
        NH = 8
        TWO_PI = 2.0 * math.pi
        with self.scope() as es:
            h = self.sb(es, "ha", [128, NFT, S], BF16)
            hB = [[Buf(f"ha{ft}_{tt}") for tt in range(NTT)] for ft in range(NFT)]
            hall = [hB[ft][tt] for ft in range(NFT) for tt in range(NTT)]
            cosT = self.sb(es, "cosT", [128, S])
            sinS = self.sb(es, "sinS", [128, S])
            cosB, sinB = Buf("cosT"), Buf("sinS")
            perm = self.sb(es, "perm", [128, 128])
            permB = Buf("perm")
            nlam = self.sb(es, "nlam", [128, 1])
            nlamB = Buf("nlam")
            subs = self.sb(es, "subs", [128, 1])
            subsB = Buf("subs")
            with self.scope() as es2:
                if self.rope_c is None:
                    jf = self.sb(es2, "jf", [128, 1])
                    jB = Buf("jf")
                    pi_ = self.sb(es2, "pi", [128, 1])
                    piB = Buf("pi")
                    sy.op("pool", lambda e: e.iota(pi_[:], pattern=[[0, 1]], base=0, channel_multiplier=1, allow_small_or_imprecise_dtypes=True), writes=[piB])
                    qi = self.sb(es2, "qi", [128, 1], I32)
                    qB = Buf("qi")
                    sy.op("dve", lambda e: e.tensor_scalar(out=jf[:], in0=pi_[:], scalar1=-15.5, scalar2=1.0 / 32, op0=ALU.add, op1=ALU.mult), reads=[piB], writes=[jB])
                    sy.op("dve", lambda e: e.tensor_copy(out=qi[:], in_=jf[:]), reads=[jB], writes=[qB])
                    sy.op("dve", lambda e: e.tensor_copy(out=jf[:], in_=qi[:]), reads=[qB], writes=[jB])
                    sy.op("dve", lambda e: e.scalar_tensor_tensor(out=jf[:], in0=jf[:], scalar=-32.0, in1=pi_[:], op0=ALU.mult, op1=ALU.add), reads=[jB, piB], writes=[jB])
                    invf = self.sb(es2, "invf", [128, 1])
                    ifB = Buf("invf")
                    sy.op("act", lambda e: e.activation(out=invf[:], in_=jf[:], func=AF.Exp, scale=-math.log(10000.0) * 2.0 / 64.0), reads=[jB], writes=[ifB])
                    sgn = self.sb(es2, "sgn", [128, 1])
                    sgB = Buf("sgn")
                    sy.op("pool", lambda e: e.memset(sgn[:], 1.0), writes=[sgB])
                    sy.op("pool", lambda e: e.memset(sgn[0:32, :], -1.0), reads=[], writes=[sgB])
                    sy.op("pool", lambda e: e.memset(sgn[64:96, :], -1.0), reads=[], writes=[sgB])
                    ang = self.sb(es2, "ang", [128, S])
                    angB = Buf("ang")
                    ni = self.sb(es2, "ni", [128, S], I32)
                    niB = Buf("ni")
                    nf = self.sb(es2, "nf", [128, S])
                    nfB = Buf("nf")
                    sy.op("pool", lambda e: e.iota(ang[:], pattern=[[1, S]], base=0, channel_multiplier=0, allow_small_or_imprecise_dtypes=True), writes=[angB])
                    sy.op("dve", lambda e: e.tensor_scalar(out=ang[:], in0=ang[:], scalar1=invf[:], scalar2=None, op0=ALU.mult), reads=[angB, ifB], writes=[angB])

                    def sin_of(dst, dstB, shift, signed):
                        sy.op("dve", lambda e: e.tensor_scalar(out=nf[:], in0=ang[:], scalar1=shift, scalar2=1.0 / TWO_PI, op0=ALU.add, op1=ALU.mult), reads=[angB], writes=[nfB])
                        sy.op("dve", lambda e: e.tensor_copy(out=ni[:], in_=nf[:]), reads=[nfB], writes=[niB])
                        sy.op("dve", lambda e: e.tensor_copy(out=nf[:], in_=ni[:]), reads=[niB], writes=[nfB])
                        sy.op("dve", lambda e: e.scalar_tensor_tensor(out=nf[:], in0=nf[:], scalar=-TWO_PI, in1=ang[:], op0=ALU.mult, op1=ALU.add), reads=[nfB, angB], writes=[nfB])
                        sy.op("dve", lambda e: e.tensor_scalar(out=nf[:], in0=nf[:], scalar1=shift, scalar2=math.pi, op0=ALU.add, op1=ALU.min), reads=[nfB], writes=[nfB])
                        sy.op("dve", lambda e: e.tensor_scalar(out=nf[:], in0=nf[:], scalar1=-math.pi, scalar2=None, op0=ALU.max), reads=[nfB], writes=[nfB])
                        if signed:
                            sy.op("act", lambda e: e.activation(out=dst[:], in_=nf[:], func=AF.Sin), reads=[nfB], writes=[dstB])
                            sy.op("dve", lambda e: e.tensor_scalar(out=dst[:], in0=dst[:], scalar1=sgn[:], scalar2=None, op0=ALU.mult), reads=[dstB, sgB], writes=[dstB])
                        else:
                            sy.op("act", lambda e: e.activation(out=dst[:], in_=nf[:], func=AF.Sin), reads=[nfB], writes=[dstB])
                    sin_of(sinS, sinB, 0.0, True)
                    sin_of(cosT, cosB, math.pi / 2, False)
                    self.rope_c = self.nc.dram_tensor("rope_c", [128, 2 * S], F32, kind="Internal").ap()
                    sy.dma("sp", [lambda e: e.dma_start(out=self.rope_c[:, 0:S], in_=cosT[:]),
                                  lambda e: e.dma_start(out=self.rope_c[:, S:2 * S], in_=sinS[:])], reads=[cosB, sinB])
                else:
                    sy.dma("sp", [lambda e: e.dma_start(out=cosT[:], in_=self.rope_c[:, 0:S]),
                                  lambda e: e.dma_start(out=sinS[:], in_=self.rope_c[:, S:2 * S])], writes=[cosB, sinB])
                sy.op("pool", lambda e: e.memset(perm[:], 0.0), writes=[permB])
                for (c0, off) in ((0, 32), (32, -32), (64, 32), (96, -32)):
                    sy.op("pool", lambda e, c0=c0, off=off: e.affine_select(
                        out=perm[:, c0:c0 + 32], in_=perm[:, c0:c0 + 32], pattern=[[-1, 32]], compare_op=ALU.not_equal, fill=1.0,
                        base=-(c0 + off), channel_multiplier=1), reads=[permB], writes=[permB])
                lt = self.sb(es2, "lt", [128, 2, 64])
                ltB = Buf("lt")
                ls = self.sb(es2, "ls", [128, 2])
                lsB = Buf("ls")
                lv = self.g[:, V_LAM:V_LAM + 256].rearrange("p (a b c) -> p a b c", a=2, b=2)
                sy.op("dve", lambda e: e.tensor_tensor(out=lt[:], in0=lv[:, :, 0, :], in1=lv[:, :, 1, :], op=ALU.mult), reads=[self.gB], writes=[ltB])
                sy.op("dve", lambda e: e.reduce_sum(out=ls[:], in_=lt[:], axis=mybir.AxisListType.X), reads=[ltB], writes=[lsB])
                sy.op("act", lambda e: e.activation(out=ls[:], in_=ls[:], func=AF.Exp), reads=[lsB], writes=[lsB])
                sy.op("dve", lambda e: e.tensor_tensor(out=nlam[:], in0=ls[:, 1:2], in1=ls[:, 0:1], op=ALU.subtract), reads=[lsB], writes=[nlamB])
                sy.op("dve", lambda e: e.tensor_scalar(out=nlam[:], in0=nlam[:], scalar1=-LAM_INIT, scalar2=None, op0=ALU.add), reads=[nlamB], writes=[nlamB])
                sy.op("dve", lambda e: e.tensor_scalar(out=subs[:], in0=self.g[:, V_SUBLN:V_SUBLN + 1], scalar1=1.0 - LAM_INIT, scalar2=None, op0=ALU.mult),
                      reads=[self.gB], writes=[subsB])
            wq = [self.sb(es, "wq", [128, NFT, 3, 128], BF16) for _ in range(2)]
            wqB = [Buf("wq0"), Buf("wq1")]
            woh = [self.sb(es, "woh", [128, D], BF16) for _ in range(4)]
            woB = [Buf(f"wo{i}") for i in range(4)]

            def load_w(hd):
                b = hd % 2
                fns = []
                for j in range(3):
                    c0 = j * D + hd * 128
                    fns.append(lambda e, j=j, c0=c0: e.dma_start(out=wq[b][:, :, j, :], in_=self.wqkv[:, c0:c0 + 128].rearrange("(kt p) n -> p kt n", p=128)))
                sy.dma("pool", fns, writes=[wqB[b]])

            def load_wo(hd):
                b = hd % 4
                sy.dma("pool", [lambda e: e.dma_start(out=woh[b][:], in_=self.wo[hd * 128:(hd + 1) * 128, :])], writes=[woB[b]])

            load_w(0)
            load_w(1)
            for h_ in range(4):
                load_wo(h_)
            self.rmsnorm(es, li, h, hB)
            qh = [self.sb(es, "qh", [128, S], BF16) for _ in range(2)]
            kh = [self.sb(es, "kh", [128, S], BF16) for _ in range(2)]
            vh = [self.sb(es, "vh", [128, 16, 128], BF16) for _ in range(2)]
            oth = [self.sb(es, "oth", [128, S], BF16) for _ in range(4)]
            qB = [[Buf(f"qh{b}_{tt}") for tt in range(NTT)] for b in range(2)]
            kB = [[Buf(f"kh{b}_{tt}") for tt in range(NTT)] for b in range(2)]
            vB = [[Buf(f"vh{b}_{j}") for j in range(4)] for b in range(2)]
            oB = [[Buf(f"oth{b}_{tt}") for tt in range(NTT)] for b in range(4)]
            qf = [self.sb(es, "qf", [128, TT]) for _ in range(2)]
            qfB = [Buf("qf0"), Buf("qf1")]
            ta = [self.sb(es, "ta", [128, TT]) for _ in range(2)]
            taB = [Buf("ta0"), Buf("ta1")]
            tb = [self.sb(es, "tb", [128, TT]) for _ in range(2)]
            tbB = [Buf("tb0"), Buf("tb1")]
            pT = [self.sb(es, "pT", [128, TT], BF16) for _ in range(4)]
            pTB = [Buf(f"pT{i}") for i in range(4)]
            t1 = self.sb(es, "t1", [128, TT])
            t1B = Buf("t1")
            rr = [self.sb(es, "rr", [128, TT]) for _ in range(2)]
            rrB = [Buf("rr0"), Buf("rr1")]
            rs = self.sb(es, "rs_a", [128, TT])
            rsB = Buf("rs_a")
            pcp = [self.sb(es, "pcp", [128, TT]) for _ in range(2)]
            pcpB = [Buf("pcp0"), Buf("pcp1")]
            of = self.sb(es, "of", [128, TT])
            ofB = Buf("of")
            osq = self.sb(es, "osq", [128, TT], BF16)
            osqB = Buf("osq")
            cnt = {"m": 0, "s": 0, "o": 0, "p": 0, "q": 0}

            def misc_bank():
                cnt["m"] += 1
                return cnt["m"] % 2

            def proj(hd):
                b = hd % 2
                for j, (dst, dB, sc) in enumerate(((qh[b], qB[b], 0.125), (kh[b], kB[b], 1.0))):
                    for tt in range(NTT):
                        pb = misc_bank()
                        i2 = cnt["q"] % 2
                        cnt["q"] += 1
                        sy.group("pe", [lambda e, kt=kt, pb=pb, j=j, tt=tt: e.matmul(
                            self.ps[pb][:], lhsT=wq[b][:, kt, j, :], rhs=h[:, kt, tt * TT:(tt + 1) * TT], start=(kt == 0), stop=(kt == NFT - 1))
                            for kt in range(NFT)], reads=[wqB[b]] + [hB[kt][tt] for kt in range(NFT)], writes=[self.psB[pb]])
                        sy.op("dve", lambda e, pb=pb, i2=i2, sc=sc: e.tensor_scalar(out=qf[i2][:], in0=self.ps[pb][:], scalar1=sc, scalar2=None, op0=ALU.mult),
                              reads=[self.psB[pb]], writes=[qfB[i2]])
                        yield
                        pb2 = misc_bank()
                        sy.op("pe", lambda e, pb2=pb2, i2=i2: e.matmul(self.ps[pb2][:], lhsT=perm[:], rhs=qf[i2][:], start=True, stop=True),
                              reads=[permB, qfB[i2]], writes=[self.psB[pb2]])
                        sy.op("dve", lambda e, i2=i2, tt=tt: e.tensor_tensor(out=ta[i2][:], in0=qf[i2][:], in1=cosT[:, tt * TT:(tt + 1) * TT], op=ALU.mult),
                              reads=[qfB[i2], cosB], writes=[taB[i2]])
                        sy.op("dve", lambda e, i2=i2, tt=tt, pb2=pb2: e.tensor_tensor(out=tb[i2][:], in0=self.ps[pb2][:], in1=sinS[:, tt * TT:(tt + 1) * TT], op=ALU.mult),
                              reads=[self.psB[pb2], sinB], writes=[tbB[i2]])
                        sy.op("pool", lambda e, i2=i2, tt=tt, dst=dst: e.tensor_tensor(out=dst[:, tt * TT:(tt + 1) * TT], in0=ta[i2][:], in1=tb[i2][:], op=ALU.add),
                              reads=[taB[i2], tbB[i2]], writes=[dB[tt]])
                        yield
                for jq in range(4):
                    pb = misc_bank()
                    fns = []
                    for jj in range(4):
                        t0 = (jq * 4 + jj) * 128
                        for kt in range(NFT):
                            fns.append(lambda e, kt=kt, jj=jj, t0=t0, pb=pb: e.matmul(
                                self.ps[pb][:, jj * 128:(jj + 1) * 128], lhsT=h[:, kt, t0:t0 + 128], rhs=wq[b][:, kt, 2, :],
                                start=(kt == 0), stop=(kt == NFT - 1)))
                    sy.group("pe", fns, reads=[wqB[b]] + [hB[kt][jq] for kt in range(NFT)], writes=[self.psB[pb]])
                    sy.op("dve", lambda e, pb=pb, jq=jq: e.tensor_copy(out=vh[b][:, jq * 4:(jq + 1) * 4, :].rearrange("p a b -> p (a b)"), in_=self.ps[pb][:]),
                          reads=[self.psB[pb]], writes=[vB[b][jq]])
                    yield

            G = 2

            deferred = []

            def core(hd, extra=None):
                b = hd % 2
                ob = hd % 4
                batches = []
                for qt in range(NTT):
                    nkt = 4 * qt + 4
                    for m in range(2):
                        for k0 in range(0, nkt, G):
                            batches.append((qt, m, k0, nkt))
                n = len(batches)
                po, pd = 6, 7

                def qk(i):
                    qt, m, k0, nkt = batches[i]
                    r0, r1 = 64 * m, 64 * m + 64
                    sset = cnt["s"] % 2
                    cnt["s"] += 1
                    fns, tl = [], []
                    for g_ in range(G):
                        kt = k0 + g_
                        r = kt - 4 * qt
                        c0 = 128 * r if r > 0 else 0
                        pst = 2 + 2 * sset + g_
                        ip = cnt["p"] % 4
                        cnt["p"] += 1
                        tl.append((kt, r, c0, pst, ip))
                        fns.append(lambda e, kt=kt, c0=c0, pst=pst: e.matmul(
                            self.ps[pst][:, c0:TT], lhsT=kh[b][r0:r1, kt * 128:(kt + 1) * 128], rhs=qh[b][r0:r1, qt * TT + c0:(qt + 1) * TT],
                            start=True, stop=True))
                    sy.group("pe", fns, reads=[kB[b][k0 // 4], qB[b][qt]], writes=[self.psB[t[3]] for t in tl])
                    for (kt, r, c0, pst, ip) in tl:
                        sy.op("act", lambda e, c0=c0, pst=pst, ip=ip: e.activation(out=pT[ip][:, c0:TT], in_=self.ps[pst][:, c0:TT], func=AF.Exp),
                              reads=[self.psB[pst]], writes=[pTB[ip]])
                        if r >= 0:
                            sy.op("pool", lambda e, ip=ip, c0=c0: e.memset(pT[ip][64:128, c0:c0 + 64], 0.0), reads=[], writes=[pTB[ip]])
                    return tl

                def av(i, tl):
                    qt, m, k0, nkt = batches[i]
                    fns = []
                    for (kt, r, c0, pst, ip) in tl:
                        fns.append(lambda e, kt=kt, c0=c0, ip=ip: e.matmul(self.ps[po][:, c0:TT], lhsT=vh[b][:, kt, :], rhs=pT[ip][:, c0:TT],
                                                                         start=(kt == 0), stop=(kt == nkt - 1)))
                        fns.append(lambda e, kt=kt, c0=c0, ip=ip: e.matmul(self.ps[pd][:, c0:TT], lhsT=self.ones_bf[:], rhs=pT[ip][:, c0:TT],
                                                                         start=(kt == 0), stop=(kt == nkt - 1)))
                    sy.group("pe", fns, reads=[vB[b][k0 // 4], self.onesB] + [pTB[t[4]] for t in tl], writes=[self.psB[po], self.psB[pd]])
                    if k0 + G >= nkt:
                        epi(qt, m, po, pd)

                def epi(qt, m, po, pd):
                    k = m
                    sy.op("dve", lambda e: e.tensor_copy(out=pcp[k][:], in_=self.ps[po][:]), reads=[self.psB[po]], writes=[pcpB[k]])
                    sy.op("act", lambda e: e.activation(out=rr[k][:], in_=self.ps[pd][:], func=AF.Ln), reads=[self.psB[pd]], writes=[rrB[k]])
                    sy.op("act", lambda e: e.activation(out=rr[k][:], in_=rr[k][:], func=AF.Exp, scale=-1.0), reads=[rrB[k]], writes=[rrB[k]])
                    if m == 0:
                        sy.op("dve", lambda e: e.tensor_tensor(out=t1[:], in0=pcp[k][:], in1=rr[k][:], op=ALU.mult),
                              reads=[pcpB[k], rrB[k]], writes=[t1B])
                        return
                    sy.op("dve", lambda e: e.tensor_tensor(out=rr[k][:], in0=pcp[k][:], in1=rr[k][:], op=ALU.mult),
                          reads=[pcpB[k], rrB[k]], writes=[rrB[k]])
                    sy.op("dve", lambda e: e.scalar_tensor_tensor(out=of[:], in0=rr[k][:], scalar=nlam[:], in1=t1[:], op0=ALU.mult, op1=ALU.add),
                          reads=[rrB[k], t1B, nlamB], writes=[ofB])
                    deferred.append([2, lambda: subln_sq(qt)])
                    deferred.append([4, lambda: subln(qt)])

                def subln_sq(qt):
                    sy.op("dve", lambda e: e.tensor_tensor(out=osq[:], in0=of[:], in1=of[:], op=ALU.mult), reads=[ofB], writes=[osqB])

                def subln(qt):
                    pb = misc_bank()
                    sy.op("pe", lambda e: e.matmul(self.ps[pb][:], lhsT=self.ones_bf[:], rhs=osq[:], start=True, stop=True),
                          reads=[osqB, self.onesB], writes=[self.psB[pb]])
                    sy.op("act", lambda e: e.activation(out=rs[:], in_=self.ps[pb][:], func=AF.Ln, scale=1.0 / 128, bias=self.epsT[:]),
                          reads=[self.psB[pb], self.epsB], writes=[rsB])
                    sy.op("act", lambda e: e.activation(out=rs[:], in_=rs[:], func=AF.Exp, scale=-0.5), reads=[rsB], writes=[rsB])
                    sy.op("dve", lambda e: e.scalar_tensor_tensor(out=oth[ob][:, qt * TT:(qt + 1) * TT], in0=of[:], scalar=subs[:], in1=rs[:],
                                                                 op0=ALU.mult, op1=ALU.mult),
                          reads=[ofB, rsB, subsB], writes=[oB[ob][qt]])

                pend = {}
                for i in range(n + 1):
                    if i < n:
                        pend[i] = qk(i)
                    if i >= 1:
                        for d_ in list(deferred):
                            d_[0] -= 1
                            if d_[0] <= 0:
                                deferred.remove(d_)
                                d_[1]()
                        if extra is not None:
                            next(extra, None)
                        av(i - 1, pend.pop(i - 1))
                if extra is not None:
                    for _ in extra:
                        pass

            def outp(hd):
                hs = [hd - 1, hd]
                for tt in range(NTT):
                    for f2 in range(NFT // 2):
                        pbs = [0, 1]
                        fns = []
                        for q, pb in enumerate(pbs):
                            fo = 2 * f2 + q
                            for i_, h_ in enumerate(hs):
                                bb = h_ % 4
                                fns.append(lambda e, pb=pb, fo=fo, bb=bb, i_=i_: e.matmul(
                                    self.ps[pb][:], lhsT=woh[bb][:, fo * 128:(fo + 1) * 128], rhs=oth[bb][:, tt * TT:(tt + 1) * TT],
                                    start=(i_ == 0), stop=(i_ == 1)))
                        sy.group("pe", fns, reads=[woB[h_ % 4] for h_ in hs] + [oB[h_ % 4][tt] for h_ in hs], writes=[self.psB[0], self.psB[1]])
                        for q, pb in enumerate(pbs):
                            fo = 2 * f2 + q
                            sy.op("dve", lambda e, pb=pb, fo=fo: e.tensor_tensor(out=self.xt(fo, tt), in0=self.xt(fo, tt), in1=self.ps[pb][:], op=ALU.add),
                                  reads=[self.psB[pb], self.xB[fo][tt]], writes=[self.xB[fo][tt]])
                        yield

            def chain(*gens):
                for g_ in gens:
                    if g_ is not None:
                        yield from g_

            for _ in proj(0):
                pass
            load_w(2)
            for hd in range(NH):
                pj = proj(hd + 1) if hd + 1 < NH else None
                op_ = outp(hd - 1) if (hd % 2 == 0 and hd >= 2) else None
                core(hd, chain(pj, op_))
                if hd + 3 < NH:
                    load_w(hd + 3)
                if op_ is not None and hd + 2 < NH:
                    load_wo(hd + 2)
                    load_wo(hd + 3)
            for d_ in deferred:
                d_[1]()
            for _ in outp(NH - 1):
                pass

    def s5(self, li, j):
        sy = self.sy
        NG = 64
        NP = 32
        NC = S // 8
        CT = 64
        TWO_PI = 2.0 * math.pi
        with self.scope() as es:
            U2 = self.sb(es, "U2", [128, NG, NC], BF16)
            U2B = [[Buf(f"U2_{gb}_{tt}") for tt in range(NTT)] for gb in range(8)]
            U2all = [U2B[gb][tt] for gb in range(8) for tt in range(NTT)]
            with self.scope() as es1:
                h = self.sb(es1, "hs", [128, NFT, S], BF16)
                hB = [[Buf(f"hs{ft}_{tt}") for tt in range(NTT)] for ft in range(NFT)]
                win = self.sb(es1, "win", [128, NFT, D], BF16)
                winB = Buf("win")
                sy.dma("pool", [lambda e, kt=kt: e.dma_start(out=win[:, kt, :], in_=self.s5w[j, 0, kt * 128:(kt + 1) * 128, :]) for kt in range(NFT)], writes=[winB])
                self.rmsnorm(es1, li, h, hB, inline=True)
                utm = [self.sb(es1, "utm", [128, 8, D], BF16) for _ in range(2)]
                utmB = [[Buf(f"utm{k}_{s_}") for s_ in range(8)] for k in range(2)]
                cnt = 0
                ev = 0
                for cb in range(2):
                    k = cb % 2
                    for s_ in range(8):
                        for fh in range(2):
                            pb = cnt % 4
                            cnt += 1
                            t0 = cb * 1024 + s_
                            sy.group("pe", [lambda e, kt=kt, pb=pb, fh=fh, t0=t0: e.matmul(
                                self.ps[pb][:], lhsT=h[:, kt, t0:t0 + 1017:8], rhs=win[:, kt, fh * 512:(fh + 1) * 512],
                                start=(kt == 0), stop=(kt == NFT - 1)) for kt in range(NFT)],
                                reads=[winB] + [hB[kt][2 * cb] for kt in range(NFT)] + [hB[kt][2 * cb + 1] for kt in range(NFT)], writes=[self.psB[pb]])
                            sy.op("act", lambda e, pb=pb, k=k, s_=s_, fh=fh: e.activation(
                                out=utm[k][:].rearrange("p s (g h) -> p (s g h)", h=16).rearrange("p (g s h) -> p g s h", s=8, h=16)[:, fh * 32:(fh + 1) * 32, s_, :],
                                in_=self.ps[pb][:].rearrange("p (g h) -> p g h", h=16), func=AF.Identity),
                                  reads=[self.psB[pb]], writes=[utmB[k][s_]])
                for cb in range(2):
                    k = cb % 2
                    for gq in range(16):
                        pb = 4 + gq % 2
                        psb = self.ps[pb][:].bitcast(BF16)
                        sy.group("pe", [lambda e, q=q, psb=psb, gq=gq, k=k: e.transpose(
                            psb[:, q * 128:(q + 1) * 128], utm[k][:].rearrange("p s f -> p (s f)")[:, (4 * gq + q) * 128:(4 * gq + q + 1) * 128], self.idb[:]) for q in range(4)],
                            reads=utmB[k] + [self.idbB], writes=[self.psB[pb]])
                        eng = "act" if ev % 2 == 0 else "dve"
                        ev += 1
                        dst = U2[:, 4 * gq:4 * gq + 4, cb * 128:(cb + 1) * 128]
                        src = psb[:, 0:512].rearrange("p (g c) -> p g c", g=4)
                        if eng == "act":
                            sy.op("act", lambda e, dst=dst, src=src: e.activation(out=dst, in_=src, func=AF.Identity),
                                  reads=[self.psB[pb]], writes=[U2B[gq // 2][2 * cb], U2B[gq // 2][2 * cb + 1]])
                        else:
                            sy.op("dve", lambda e, dst=dst, src=src: e.tensor_copy(out=dst, in_=src),
                                  reads=[self.psB[pb]], writes=[U2B[gq // 2][2 * cb], U2B[gq // 2][2 * cb + 1]])
            import os
            dbg = int(os.environ.get("S5DBG", "9"))
            if dbg < 2:
                return
            with self.scope() as es2:
                Tm = self.sb(es2, "Tm", [128, NG, 128], BF16)
                TmB = [Buf(f"Tm{i}") for i in range(16)]
                Btm = self.sb(es2, "Btm", [128, NP, 2, 128], BF16)
                BtB = [Buf(f"Btm{i}") for i in range(16)]
                CtR = self.sb(es2, "CtR", [128, NP, 128], BF16)
                CtI = self.sb(es2, "CtI", [128, NP, 128], BF16)
                CtB = [Buf(f"Ct{i}") for i in range(8)]
                PP1 = self.sb(es2, "PP1", [128, 8, 2, NP])
                PP2 = self.sb(es2, "PP2", [128, 8, 2, NP])
                AB = Buf("A12")
                A1, A2 = PP1[:, 0, :, :], PP2[:, 0, :, :]
                sc = self.s5c[j]
                for h_ in range(4):
                    sy.dma("sp", [lambda e, h_=h_: e.dma_start(out=Btm[:, h_ * 8:(h_ + 1) * 8, :, :].rearrange("p j r c -> p (j r c)"), in_=sc["Bt"][:, h_ * 2048:(h_ + 1) * 2048])],
                           writes=BtB[4 * h_:4 * h_ + 4])
                sy.dma("sp", [lambda e: e.dma_start(out=PP1[:].rearrange("p l a b -> p (l a b)"), in_=sc["A"][:, 0:512]),
                              lambda e: e.dma_start(out=PP2[:].rearrange("p l a b -> p (l a b)"), in_=sc["A"][:, 512:1024])], writes=[AB])

                sy.dma("sp", [lambda e: e.dma_start(out=CtR[:].rearrange("p j c -> p (j c)"), in_=sc["CtR"]),
                              lambda e: e.dma_start(out=CtI[:].rearrange("p j c -> p (j c)"), in_=sc["CtI"])], writes=CtB)
                sy.dma("sp", [lambda e, h_=h_: e.dma_start(out=Tm[:, h_ * 16:(h_ + 1) * 16, :].rearrange("p g c -> p (g c)"), in_=sc["Tm"][:, h_ * 2048:(h_ + 1) * 2048]) for h_ in range(4)], writes=TmB)
                if dbg < 3:
                    return
                St = [self.sb(es2, "St", [128, CT + 1, 2, NP]) for _ in range(2)]
                StB = [Buf("St0"), Buf("St1")]
                Sb = [self.sb(es2, "Sb", [128, CT, 2, NP], BF16) for _ in range(2)]
                SbB = [Buf("Sb0"), Buf("Sb1")]
                tA = self.sb(es2, "tA", [128, 8, 2, NP])
                tB_ = self.sb(es2, "tB", [128, 8, 2, NP])
                tAB, tBB = Buf("tA"), Buf("tB")
                tC = [tA[:, 0:7, :, :], tA[:, 0:7, :, :]]
                tD = [tB_[:, 0:7, :, :], tB_[:, 0:7, :, :]]
                tCB, tDB = [tAB, tAB], [tBB, tBB]
                XlB = [[Buf(f"Xl{k_}_{l_}") for l_ in range(8)] for k_ in range(2)]
                cnt = {"b": 0, "y": 0, "g": 0}

                def stage_a(tt):
                    c0 = tt * CT
                    k = tt % 2
                    for pq in range(NP // 4):
                        pb = cnt["b"] % 2
                        cnt["b"] += 1
                        fns = []
                        for jl in range(4):
                            jp = pq * 4 + jl
                            for e_ in range(2):
                                for ri in range(2):
                                    col = (jl * 2 + ri) * CT
                                    fns.append(lambda e, jp=jp, e_=e_, ri=ri, col=col, pb=pb: e.matmul(
                                        self.ps[pb][64 * e_:64 * e_ + 64, col:col + CT], lhsT=Btm[:, jp, ri, 64 * e_:64 * e_ + 64],
                                        rhs=U2[:, 2 * jp + e_, c0:c0 + CT], start=True, stop=True))
                        sy.group("pe", fns, reads=[BtB[pq * 2], BtB[pq * 2 + 1], U2B[pq][tt]], writes=[self.psB[pb]])
                        sy.op("act", lambda e, pb=pb, pq=pq: e.activation(
                            out=St[k][:, 1:CT + 1, :, pq * 4:pq * 4 + 4].rearrange("p c r j -> p j r c"),
                            in_=self.ps[pb][:].rearrange("p (j r c) -> p j r c", j=4, r=2), func=AF.Identity),
                            reads=[self.psB[pb]], writes=XlB[k])

                def stage_b(tt):
                    k = tt % 2
                    Sk = St[k]
                    Sv = Sk[:, 1:CT + 1, :, :].rearrange("p (b l) r j -> p b l r j", l=8)

                    def swp(ap_slot1_r1, nb):
                        t_ = ap_slot1_r1
                        if nb == 1:
                            return bass.AP(tensor=t_.tensor, offset=t_.offset, ap=[list(t_.ap[0]), [-NP, 2], [1, NP]])
                        return bass.AP(tensor=t_.tensor, offset=t_.offset, ap=[list(t_.ap[0]), [8 * 2 * NP, nb], [-NP, 2], [1, NP]])

                    def step(prev, prev_sw, cur, c1, c2, ta_, tb_, rB, wB):
                        sy.op("dve", lambda e: e.tensor_tensor(out=tb_, in0=prev_sw, in1=c2, op=ALU.mult), reads=rB + [AB], writes=[tBB])
                        sy.op("dve", lambda e: e.tensor_tensor(out=ta_, in0=prev, in1=c1, op=ALU.mult), reads=rB + [AB], writes=[tAB])
                        sy.op("dve", lambda e: e.tensor_tensor(out=tb_, in0=tb_, in1=cur, op=ALU.add), reads=[tBB] + wB, writes=[tBB])
                        sy.op("dve", lambda e: e.tensor_tensor(out=cur, in0=ta_, in1=tb_, op=ALU.add), reads=[tAB, tBB], writes=wB)

                    c0B = StB[k]
                    if tt == 0:
                        sy.op("dve", lambda e: e.memset(Sk[:, 0, :, :], 0.0), writes=[c0B])
                    else:
                        sy.op("dve", lambda e: e.tensor_copy(out=Sk[:, 0, :, :], in_=St[1 - k][:, CT, :, :]), reads=[XlB[1 - k][7]], writes=[c0B])
                    step(Sk[:, 0, :, :], swp(Sk[:, 0, 1, :], 1), Sk[:, 1, :, :], A1, A2, tA[:, 0, :, :], tB_[:, 0, :, :], [c0B], [XlB[k][0]])
                    A1b = A1.unsqueeze(1).broadcast_to([128, 8, 2, NP])
                    A2b = A2.unsqueeze(1).broadcast_to([128, 8, 2, NP])
                    for l in range(1, 8):
                        step(Sv[:, :, l - 1, :, :], swp(Sv[:, 0, l - 1, 1, :], 8), Sv[:, :, l, :, :], A1b, A2b, tA[:], tB_[:], [XlB[k][l - 1]], [XlB[k][l]])
                    for blk in range(1, 8):
                        step(Sv[:, blk - 1, 7, :, :], swp(Sv[:, blk - 1, 7, 1, :], 1), Sv[:, blk, 7, :, :], PP1[:, 7, :, :], PP2[:, 7, :, :],
                             tA[:, 0, :, :], tB_[:, 0, :, :], [XlB[k][7]], [XlB[k][7]])
                    Cv = Sv[:, 0:7, 7, :, :]
                    Cs = swp(Sv[:, 0, 7, 1, :], 7)
                    for l in range(7):
                        q = l % 2
                        p1 = PP1[:, l, :, :].unsqueeze(1).broadcast_to([128, 7, 2, NP])
                        p2 = PP2[:, l, :, :].unsqueeze(1).broadcast_to([128, 7, 2, NP])
                        cur = Sv[:, 1:8, l, :, :]
                        sy.op("dve", lambda e, q=q, p2=p2: e.tensor_tensor(out=tD[q], in0=Cs, in1=p2, op=ALU.mult), reads=[XlB[k][7], AB], writes=[tDB[q]])
                        sy.op("dve", lambda e, q=q, p1=p1: e.tensor_tensor(out=tC[q], in0=Cv, in1=p1, op=ALU.mult), reads=[XlB[k][7], AB], writes=[tCB[q]])
                        sy.op("dve", lambda e, q=q, cur=cur: e.tensor_tensor(out=tD[q], in0=tD[q], in1=cur, op=ALU.add), reads=[tDB[q], XlB[k][l]], writes=[tDB[q]])
                        sy.op("dve", lambda e, q=q, cur=cur: e.tensor_tensor(out=cur, in0=tC[q], in1=tD[q], op=ALU.add), reads=[tCB[q], tDB[q]], writes=[XlB[k][l]])

                def stage_c(tt):
                    c0 = tt * CT
                    k = tt % 2
                    sy.op("act", lambda e: e.activation(out=Sb[k][:], in_=St[k][:, 0:CT, :, :], func=AF.Identity), reads=[StB[k]] + XlB[k], writes=[SbB[k]])
                    for gb in range(8):
                        pb = 2 + cnt["y"] % 2
                        cnt["y"] += 1
                        fns = []
                        for gl in range(8):
                            g = gb * 8 + gl
                            jp, e_ = g // 2, g % 2
                            out = self.ps[pb][:, gl * CT:(gl + 1) * CT]
                            fns.append(lambda e, g=g, out=out: e.matmul(out, lhsT=Tm[:, g, :], rhs=U2[:, g, c0:c0 + CT], start=True, stop=False))
                            fns.append(lambda e, jp=jp, e_=e_, out=out: e.matmul(out, lhsT=CtR[64 * e_:64 * e_ + 64, jp, :], rhs=Sb[k][64 * e_:64 * e_ + 64, :, 0, jp],
                                                                              start=False, stop=False))
                            fns.append(lambda e, jp=jp, e_=e_, out=out: e.matmul(out, lhsT=CtI[64 * e_:64 * e_ + 64, jp, :], rhs=Sb[k][64 * e_:64 * e_ + 64, :, 1, jp],
                                                                              start=False, stop=True))
                        sy.group("pe", fns, reads=[TmB[gb * 2], TmB[gb * 2 + 1], CtB[gb], U2B[gb][tt], SbB[k]], writes=[self.psB[pb]])
                        ps = self.ps[pb]
                        sy.op("act", lambda e, ps=ps, gb=gb: e.activation(
                            out=U2[:, gb * 8:(gb + 1) * 8, c0:c0 + CT], in_=ps[:].rearrange("p (g c) -> p g c", g=8), func=AF.Gelu_apprx_tanh),
                            reads=[self.psB[pb]], writes=[U2B[gb][tt]])

                stage_a(0)
                stage_b(0)
                for tt in range(NTT):
                    if tt + 1 < NTT:
                        stage_a(tt + 1)
                        stage_b(tt + 1)
                    stage_c(tt)
            if dbg < 5:
                return
            with self.scope() as es3:
                zfm = self.sb(es3, "zfm", [128, NFT, 2, 8, 128], BF16)
                zfB = [[Buf(f"zfm{fo}_{cb}") for cb in range(2)] for fo in range(NFT)]
                ztm = [self.sb(es3, "ztm", [128, D], BF16) for _ in range(2)]
                ztmB = [Buf("ztm0"), Buf("ztm1")]
                czc = {"n": 0}

                def tr(cb, fo):
                    cz = czc["n"]
                    czc["n"] += 1
                    k = cz % 2
                    pa = 4 + cz % 2
                    pbk = 6 + cz % 2
                    psa = self.ps[pa][:].bitcast(BF16)
                    psb = self.ps[pbk][:].bitcast(BF16)
                    sy.group("pe", [lambda e, g8=g8: e.transpose(
                        psa[:, g8 * 128:(g8 + 1) * 128], U2[:, fo * 8 + g8, cb * 128:(cb + 1) * 128], self.idb[:]) for g8 in range(8)],
                        reads=[U2B[fo][2 * cb], U2B[fo][2 * cb + 1], self.idbB], writes=[self.psB[pa]])
                    sy.op("act", lambda e: e.activation(out=ztm[k][:].rearrange("p (t g h) -> p g t h", g=8, t=8),
                                                       in_=psa.rearrange("p (g t h) -> p g t h", g=8, t=8), func=AF.Identity),
                          reads=[self.psB[pa]], writes=[ztmB[k]])
                    sy.group("pe", [lambda e, t=t: e.transpose(psb[:, t * 128:(t + 1) * 128], ztm[k][:, t * 128:(t + 1) * 128], self.idb[:]) for t in range(8)],
                             reads=[ztmB[k], self.idbB], writes=[self.psB[pbk]])
                    sy.op("dve", lambda e: e.tensor_copy(out=zfm[:, fo, cb, :, :].rearrange("p t c -> p (t c)"), in_=psb),
                          reads=[self.psB[pbk]], writes=[zfB[fo][cb]])

                for fo in range(NFT):
                    tr(0, fo)
                wg = self.sb(es3, "wg", [128, NFT, D], BF16)
                wo_ = self.sb(es3, "wo5", [128, NFT, D], BF16)
                wgB, woB = Buf("wg"), Buf("wo5")
                sy.dma("pool", [lambda e, kt=kt: e.dma_start(out=wg[:, kt, :], in_=self.s5w[j, 1, kt * 128:(kt + 1) * 128, :]) for kt in range(NFT)], writes=[wgB])
                sy.dma("pool", [lambda e, kt=kt: e.dma_start(out=wo_[:, kt, :], in_=self.s5w[j, 2, kt * 128:(kt + 1) * 128, :]) for kt in range(NFT)], writes=[woB])
                z2 = [self.sb(es3, "z2", [128, NFT, TT], BF16) for _ in range(2)]
                z2B = [[Buf(f"z2_{b}_{fo}") for fo in range(NFT)] for b in range(2)]
                sg = [self.sb(es3, "sg", [128, TT]) for _ in range(2)]
                sgB = [Buf("sg0"), Buf("sg1")]
                cnt = {"a": 0, "b": 0, "s": 0}

                tiles = [(0, 0), (0, 1), (1, 0), (1, 1)]

                def zcols(kt, ti):
                    cb, th = tiles[ti]
                    return zfm[:, kt, cb, 4 * th:4 * th + 4, :].rearrange("p t c -> p (t c)")

                def gate(ti):
                    b = ti % 2
                    cb, th = tiles[ti]
                    for fo in range(NFT):
                        pb = cnt["a"] % 2
                        cnt["a"] += 1
                        k = cnt["s"] % 2
                        cnt["s"] += 1
                        sy.group("pe", [lambda e, kt=kt, pb=pb, fo=fo: e.matmul(
                            self.ps[pb][:], lhsT=wg[:, kt, fo * 128:(fo + 1) * 128], rhs=zcols(kt, ti),
                            start=(kt == 0), stop=(kt == NFT - 1)) for kt in range(NFT)],
                            reads=[wgB] + [zfB[kt][cb] for kt in range(NFT)], writes=[self.psB[pb]])
                        sy.op("act", lambda e, pb=pb, k=k: e.activation(out=sg[k][:], in_=self.ps[pb][:], func=AF.Sigmoid), reads=[self.psB[pb]], writes=[sgB[k]])
                        sy.op("pool", lambda e, k=k, fo=fo: e.tensor_tensor(out=z2[b][:, fo, :], in0=zcols(fo, ti), in1=sg[k][:], op=ALU.mult),
                              reads=[sgB[k], zfB[fo][cb]], writes=[z2B[b][fo]])

                def outp(ti):
                    b = ti % 2
                    cb, th = tiles[ti]
                    xv = self.x[:].rearrange("p f (c s) -> p f s c", s=8)
                    for fo in range(NFT):
                        pb = 2 + cnt["b"] % 2
                        cnt["b"] += 1
                        sy.group("pe", [lambda e, kt=kt, pb=pb, fo=fo: e.matmul(
                            self.ps[pb][:], lhsT=wo_[:, kt, fo * 128:(fo + 1) * 128], rhs=z2[b][:, kt, :], start=(kt == 0), stop=(kt == NFT - 1))
                            for kt in range(NFT)], reads=[woB] + z2B[b], writes=[self.psB[pb]])
                        xs = xv[:, fo, 4 * th:4 * th + 4, cb * 128:(cb + 1) * 128]
                        sy.op("dve", lambda e, pb=pb, xs=xs: e.tensor_tensor(out=xs, in0=xs, in1=self.ps[pb][:].rearrange("p (t c) -> p t c", t=4), op=ALU.add),
                              reads=[self.psB[pb]] + self.xB[fo], writes=self.xB[fo])

                gate(0)
                for fo in range(0, 4):
                    tr(1, fo)
                gate(1)
                outp(0)
                for fo in range(4, 8):
                    tr(1, fo)
                gate(2)
                outp(1)
                gate(3)
                outp(2)
                outp(3)

    def s5_precompute(self, j):
        nc, sy = self.nc, self.sy
        NG, NP = 64, 32
        sc = {
            "Tm": nc.dram_tensor(f"s5c_Tm{j}", [128, NG * 128], BF16, kind="Internal").ap(),
            "Bt": nc.dram_tensor(f"s5c_Bt{j}", [128, NP * 2 * 128], BF16, kind="Internal").ap(),
            "CtR": nc.dram_tensor(f"s5c_CtR{j}", [128, NP * 128], BF16, kind="Internal").ap(),
            "CtI": nc.dram_tensor(f"s5c_CtI{j}", [128, NP * 128], BF16, kind="Internal").ap(),
            "A": nc.dram_tensor(f"s5c_A{j}", [128, 1024], F32, kind="Internal").ap(),
        }
        self.s5c[j] = sc
        with self.scope() as es2:
            Tm = self.sb(es2, "Tm", [128, NG, 128], BF16)
            TmB = [Buf(f"Tm{i}") for i in range(16)]
            Btm = self.sb(es2, "Btm", [128, NP, 2, 128], BF16)
            BtB = [Buf(f"Btm{i}") for i in range(16)]
            CtR = self.sb(es2, "CtR", [128, NP, 128], BF16)
            CtI = self.sb(es2, "CtI", [128, NP, 128], BF16)
            CtB = [Buf(f"Ct{i}") for i in range(8)]
            A1 = self.sb(es2, "A1", [128, 8, 2, NP])
            A2 = self.sb(es2, "A2", [128, 8, 2, NP])
            AB = Buf("A12")
            self.s5_prep(es2, j, Tm, TmB, Btm, BtB, CtR, CtI, CtB, A1, A2, AB)
            sy.dma("sp", [lambda e, h_=h_: e.dma_start(out=sc["Tm"][:, h_ * 2048:(h_ + 1) * 2048], in_=Tm[:, h_ * 16:(h_ + 1) * 16, :].rearrange("p g c -> p (g c)")) for h_ in range(4)], reads=TmB)
            sy.dma("sp", [lambda e, h_=h_: e.dma_start(out=sc["Bt"][:, h_ * 2048:(h_ + 1) * 2048], in_=Btm[:, h_ * 8:(h_ + 1) * 8, :, :].rearrange("p j r c -> p (j r c)")) for h_ in range(4)], reads=BtB)
            sy.dma("sp", [lambda e: e.dma_start(out=sc["CtR"], in_=CtR[:].rearrange("p j c -> p (j c)")),
                          lambda e: e.dma_start(out=sc["CtI"], in_=CtI[:].rearrange("p j c -> p (j c)"))], reads=CtB)
            sy.dma("sp", [lambda e: e.dma_start(out=sc["A"][:, 0:512], in_=A1[:].rearrange("p l a b -> p (l a b)")),
                          lambda e: e.dma_start(out=sc["A"][:, 512:1024], in_=A2[:].rearrange("p l a b -> p (l a b)"))], reads=[AB])

    def s5_prep(self, es_out, j, Tm, TmB, Btm, BtB, CtR, CtI, CtB, A1, A2, AB):
        sy = self.sy
        NP = 32
        TWO_PI = 2.0 * math.pi
        with self.scope() as es:
            prm = self.sb(es, "prm", [128, S5NC])
            prmB = Buf("prm")
            sy.dma("sp", [lambda e: e.dma_start(out=prm[:], in_=self.s5p[j])], writes=[prmB])
            lre, lim, lst = prm[:, 0:32], prm[:, 32:64], prm[:, 64:96]
            bre = prm[:, 96:608].rearrange("p (j h) -> p j h", h=16)
            bim = prm[:, 608:1120].rearrange("p (j h) -> p j h", h=16)
            cre = prm[:, 1120:1632].rearrange("p (j h) -> p j h", h=16)
            cim = prm[:, 1632:2144].rearrange("p (j h) -> p j h", h=16)
            dvec = prm[:, 2144:2208]
            names = ["lr", "dt", "lrdt", "lidt", "nr", "den", "fr", "fi", "w1", "w2"]
            sm = {n: self.sb(es, "s5" + n, [128, NP]) for n in names}
            smB = {n: Buf("s5" + n) for n in names}

            def tt_(eng, out, oB, a, aB, b, bB, op):
                sy.op(eng, lambda e: e.tensor_tensor(out=out, in0=a, in1=b, op=op), reads=[aB, bB], writes=[oB])

            def V(n):
                return sm[n][:]
            sy.op("dve", lambda e: e.tensor_scalar(out=V("lr"), in0=lre, scalar1=-1e-4, scalar2=None, op0=ALU.min), reads=[prmB], writes=[smB["lr"]])
            sy.op("act", lambda e: e.activation(out=V("dt"), in_=lst, func=AF.Exp), reads=[prmB], writes=[smB["dt"]])
            tt_("dve", V("lrdt"), smB["lrdt"], V("lr"), smB["lr"], V("dt"), smB["dt"], ALU.mult)
            tt_("dve", V("lidt"), smB["lidt"], lim, prmB, V("dt"), smB["dt"], ALU.mult)
            NK = 24
            kv = self.sb(es, "kv", [128, NK])
            kvB = Buf("kv")
            sy.op("pool", lambda e: e.iota(kv[:, 0:16], pattern=[[1, 16]], base=-7, channel_multiplier=0, allow_small_or_imprecise_dtypes=True), writes=[kvB])
            sy.op("pool", lambda e: e.iota(kv[:, 16:NK], pattern=[[8, NK - 16]], base=16, channel_multiplier=0, allow_small_or_imprecise_dtypes=True), writes=[kvB])
            big = {n: self.sb(es, "s5" + n, [128, NK, NP]) for n in ["mag", "ang", "nf", "AR", "AI"]}
            bigB = {n: Buf("s5" + n) for n in big}
            ni = self.sb(es, "s5ni", [128, NK, NP], I32)
            niB = Buf("s5ni")
            kvb = kv[:].unsqueeze(2).broadcast_to([128, NK, NP])
            sy.op("dve", lambda e: e.tensor_tensor(out=big["mag"][:], in0=kvb, in1=V("lrdt").unsqueeze(1).broadcast_to([128, NK, NP]), op=ALU.mult),
                  reads=[kvB, smB["lrdt"]], writes=[bigB["mag"]])
            sy.op("act", lambda e: e.activation(out=big["mag"][:], in_=big["mag"][:], func=AF.Exp), reads=[bigB["mag"]], writes=[bigB["mag"]])
            sy.op("dve", lambda e: e.tensor_tensor(out=big["ang"][:], in0=kvb, in1=V("lidt").unsqueeze(1).broadcast_to([128, NK, NP]), op=ALU.mult),
                  reads=[kvB, smB["lidt"]], writes=[bigB["ang"]])

            def sin_of(dst, dstB, shift):
                nf, nfB, ang, angB = big["nf"], bigB["nf"], big["ang"], bigB["ang"]
                sy.op("dve", lambda e: e.tensor_scalar(out=nf[:], in0=ang[:], scalar1=shift, scalar2=1.0 / TWO_PI, op0=ALU.add, op1=ALU.mult), reads=[angB], writes=[nfB])
                sy.op("dve", lambda e: e.tensor_copy(out=ni[:], in_=nf[:]), reads=[nfB], writes=[niB])
                sy.op("dve", lambda e: e.tensor_copy(out=nf[:], in_=ni[:]), reads=[niB], writes=[nfB])
                sy.op("dve", lambda e: e.scalar_tensor_tensor(out=nf[:], in0=nf[:], scalar=-TWO_PI, in1=ang[:], op0=ALU.mult, op1=ALU.add), reads=[nfB, angB], writes=[nfB])
                sy.op("dve", lambda e: e.tensor_scalar(out=nf[:], in0=nf[:], scalar1=shift, scalar2=math.pi, op0=ALU.add, op1=ALU.min), reads=[nfB], writes=[nfB])
                sy.op("dve", lambda e: e.tensor_scalar(out=nf[:], in0=nf[:], scalar1=-math.pi, scalar2=None, op0=ALU.max), reads=[nfB], writes=[nfB])
                sy.op("act", lambda e: e.activation(out=dst[:], in_=nf[:], func=AF.Sin), reads=[nfB], writes=[dstB])
            sin_of(big["AI"], bigB["AI"], 0.0)
            sin_of(big["AR"], bigB["AR"], math.pi / 2)
            for n in ("AI", "AR"):
                sy.op("dve", lambda e, n=n: e.tensor_tensor(out=big[n][:], in0=big[n][:], in1=big["mag"][:], op=ALU.mult), reads=[bigB[n], bigB["mag"]], writes=[bigB[n]])
            AR, AI, ARB, AIB = big["AR"], big["AI"], bigB["AR"], bigB["AI"]
            import os
            pd_ = int(os.environ.get("PREPDBG", "9"))
            if pd_ < 2:
                return
            for r_ in range(2):
                sy.op("dve", lambda e, r_=r_: e.tensor_copy(out=A1[:, :, r_, :], in_=AR[:, 15:23, :]), reads=[ARB], writes=[AB])
            sy.op("dve", lambda e: e.tensor_copy(out=A2[:, :, 1, :], in_=AI[:, 15:23, :]), reads=[AIB], writes=[AB])
            sy.op("dve", lambda e: e.tensor_scalar(out=A2[:, :, 0, :], in0=AI[:, 15:23, :], scalar1=-1.0, scalar2=None, op0=ALU.mult), reads=[AIB], writes=[AB])
            a1r, a1i = AR[:, 8, :], AI[:, 8, :]
            sy.op("dve", lambda e: e.tensor_scalar(out=V("nr"), in0=a1r, scalar1=-1.0, scalar2=None, op0=ALU.add), reads=[ARB], writes=[smB["nr"]])
            tt_("dve", V("den"), smB["den"], V("lr"), smB["lr"], V("lr"), smB["lr"], ALU.mult)
            tt_("dve", V("w1"), smB["w1"], lim, prmB, lim, prmB, ALU.mult)
            tt_("dve", V("den"), smB["den"], V("den"), smB["den"], V("w1"), smB["w1"], ALU.add)
            sy.op("dve", lambda e: e.reciprocal(out=V("den"), in_=V("den")), reads=[smB["den"]], writes=[smB["den"]])
            tt_("dve", V("w1"), smB["w1"], V("nr"), smB["nr"], V("lr"), smB["lr"], ALU.mult)
            tt_("dve", V("w2"), smB["w2"], a1i, AIB, lim, prmB, ALU.mult)
            tt_("dve", V("fr"), smB["fr"], V("w1"), smB["w1"], V("w2"), smB["w2"], ALU.add)
            tt_("dve", V("fr"), smB["fr"], V("fr"), smB["fr"], V("den"), smB["den"], ALU.mult)
            tt_("dve", V("w1"), smB["w1"], a1i, AIB, V("lr"), smB["lr"], ALU.mult)
            tt_("dve", V("w2"), smB["w2"], V("nr"), smB["nr"], lim, prmB, ALU.mult)
            tt_("dve", V("fi"), smB["fi"], V("w1"), smB["w1"], V("w2"), smB["w2"], ALU.subtract)
            tt_("dve", V("fi"), smB["fi"], V("fi"), smB["fi"], V("den"), smB["den"], ALU.mult)
            BBr = self.sb(es, "BBr", [128, NP, 16])
            BBi = self.sb(es, "BBi", [128, NP, 16])
            BBrB, BBiB = Buf("BBr"), Buf("BBi")
            IZ = self.sb(es, "IZ", [128, 2, 256])
            IZB = Buf("IZ")
            w3a_t = IZ[:].rearrange("p a b -> p (a b)").rearrange("p (j h) -> p j h", h=16)

            class _V:
                def __init__(self, ap): self.ap_ = ap
                def __getitem__(self, k): return self.ap_
            w3a = _V(w3a_t)
            w3b = self.sb(es, "w3b", [128, NP, 16])
            w3aB, w3bB = IZB, Buf("w3b")
            frb = V("fr").unsqueeze(2).broadcast_to([128, NP, 16])
            fib = V("fi").unsqueeze(2).broadcast_to([128, NP, 16])
            sy.op("dve", lambda e: e.tensor_tensor(out=w3a[:], in0=bre, in1=frb, op=ALU.mult), reads=[prmB, smB["fr"]], writes=[w3aB])
            sy.op("dve", lambda e: e.tensor_tensor(out=w3b[:], in0=bim, in1=fib, op=ALU.mult), reads=[prmB, smB["fi"]], writes=[w3bB])
            tt_("dve", BBr[:], BBrB, w3a[:], w3aB, w3b[:], w3bB, ALU.subtract)
            sy.op("dve", lambda e: e.tensor_tensor(out=w3a[:], in0=bim, in1=frb, op=ALU.mult), reads=[prmB, smB["fr"]], writes=[w3aB])
            sy.op("dve", lambda e: e.tensor_tensor(out=w3b[:], in0=bre, in1=fib, op=ALU.mult), reads=[prmB, smB["fi"]], writes=[w3bB])
            tt_("dve", BBi[:], BBiB, w3a[:], w3aB, w3b[:], w3bB, ALU.add)
            mask = self.sb(es, "msk", [128, 4, 8, 16])
            maskB = Buf("msk")
            sy.op("pool", lambda e: e.memset(mask[:], 1.0), writes=[maskB])
            sy.op("pool", lambda e: e.affine_select(out=mask[:], in_=mask[:], pattern=[[0, 4], [16, 8], [0, 16]], compare_op=ALU.is_ge, fill=0.0,
                                                   base=15, channel_multiplier=-1), reads=[maskB], writes=[maskB])
            ident = self.sb(es, "idn", [128, 128])
            identB = Buf("idn")
            sy.op("pool", lambda e: e.memset(ident[:], 0.0), writes=[identB])
            sy.op("pool", lambda e: e.affine_select(out=ident[:], in_=ident[:], pattern=[[-1, 128]], compare_op=ALU.not_equal, fill=1.0, base=0, channel_multiplier=1),
                  reads=[identB], writes=[identB])
            if pd_ < 3:
                return
            PBN = 4
            shp = [128, PBN, 8, 16]
            shp9 = [128, PBN, 9, 16]
            tmp_sets = [{n: self.sb(es, "s5t" + n, shp9 if n in ("CpR", "CpI") else shp) for n in ["BnR", "BnI", "BtR", "BtI", "CpR", "CpI"]}
                        for _ in range(2)]
            tmpB_sets = [{n: Buf("s5t" + n) for n in tmp_sets[0]} for _ in range(2)]
            scr = {n: self.sb(es, "s5t" + n, shp9 if n in ("x1", "x2") else shp) for n in ["x1", "x2", "y1", "y2"]}
            scrB = {n: Buf("s5t" + n) for n in scr}
            for d_, dB_ in zip(tmp_sets, tmpB_sets):
                d_.update(scr)
                dB_.update(scrB)
            tmp, tmpB = dict(tmp_sets[0]), dict(tmpB_sets[0])
            CpZ_sets = []
            for _ in range(2):
                zr = self.sb(es, "CpZR", [128, PBN, 2, 128])
                zi = self.sb(es, "CpZI", [128, PBN, 2, 128])
                zb = Buf("CpZ")
                sy.op("pool", lambda e, zr=zr: e.memset(zr[:], 0.0), writes=[zb])
                sy.op("pool", lambda e, zi=zi: e.memset(zi[:], 0.0), writes=[zb])
                CpZ_sets.append((zr, zi, zb))
            cnt = 0

            def cprod(eng, x1, x2, k0, Zr, Zi, ZB, p0, outR, outI, negI, outRB, outIB, npow=8):
                kk0, sgn = k0
                def pw(A):
                    t = A[:, kk0, p0:p0 + PBN]
                    return bass.AP(tensor=t.tensor, offset=t.offset, ap=[list(t.ap[0]), [1, PBN], [sgn * NP, npow], [0, 16]])
                def zz(Z):
                    t = Z[:, p0:p0 + PBN, :]
                    return bass.AP(tensor=t.tensor, offset=t.offset, ap=[list(t.ap[0]), [16, PBN], [0, npow], [1, 16]])
                X1, X2 = tmp[x1][:, :, 0:npow, :], tmp[x2][:, :, 0:npow, :]
                sy.op(eng, lambda e: e.tensor_tensor(out=X1[:], in0=pw(AR), in1=zz(Zr), op=ALU.mult), reads=[ARB] + ZB, writes=[tmpB[x1]])
                sy.op(eng, lambda e: e.tensor_tensor(out=X2[:], in0=pw(AI), in1=zz(Zi), op=ALU.mult), reads=[AIB] + ZB, writes=[tmpB[x2]])
                sy.op(eng, lambda e: e.tensor_tensor(out=outR, in0=X1[:], in1=X2[:], op=ALU.subtract), reads=[tmpB[x1], tmpB[x2]], writes=[outRB])
                sy.op(eng, lambda e: e.tensor_tensor(out=X1[:], in0=pw(AR), in1=zz(Zi), op=ALU.mult), reads=[ARB] + ZB, writes=[tmpB[x1]])
                sy.op(eng, lambda e: e.tensor_tensor(out=X2[:], in0=pw(AI), in1=zz(Zr), op=ALU.mult), reads=[AIB] + ZB, writes=[tmpB[x2]])
                if negI:
                    sy.op(eng, lambda e: e.tensor_scalar(out=X1[:], in0=X1[:], scalar1=-1.0, scalar2=None, op0=ALU.mult), reads=[tmpB[x1]], writes=[tmpB[x1]])
                    sy.op(eng, lambda e: e.tensor_tensor(out=outI, in0=X1[:], in1=X2[:], op=ALU.subtract), reads=[tmpB[x1], tmpB[x2]], writes=[outIB])
                else:
                    sy.op(eng, lambda e: e.tensor_tensor(out=outI, in0=X1[:], in1=X2[:], op=ALU.add), reads=[tmpB[x1], tmpB[x2]], writes=[outIB])

            CB = [prmB]
            sy.op("pool", lambda e: e.memset(IZ[:], 0.0), writes=[IZB])
            sy.op("pool", lambda e: e.tensor_copy(out=IZ[:, 0, 0:128], in_=ident[:]), reads=[identB], writes=[IZB])
            sy.op("pool", lambda e: e.tensor_copy(out=IZ[:, 1, 128:256], in_=ident[:]), reads=[identB], writes=[IZB])
            Lb = [self.sb(es, "Lb", [128, 128]) for _ in range(4)]
            LbB = [Buf(f"Lb{i}") for i in range(4)]
            prev_evacs = []
            for bi in range(NP // PBN):
                p0 = bi * PBN
                tmp.update(tmp_sets[bi % 2])
                tmpB.update(tmpB_sets[bi % 2])
                CpZR, CpZI, CpZB = CpZ_sets[bi % 2]
                POOL_ = "pool" if pd_ != 5 else "dve"
                cprod("dve", "x1", "x2", (7, -1), BBr, BBi, [BBrB, BBiB], p0, tmp["BnR"][:], tmp["BnI"][:], False, tmpB["BnR"], tmpB["BnI"])
                if pd_ < 4:
                    continue
                cprod(POOL_, "y1", "y2", (14, -1), BBr, BBi, [BBrB, BBiB], p0, tmp["BtR"][:], tmp["BtI"][:], False, tmpB["BtR"], tmpB["BtI"])
                cprod("dve", "x1", "x2", (7, 1), cre, cim, CB, p0, tmp["CpR"][:], tmp["CpI"][:], True, tmpB["CpR"], tmpB["CpI"], npow=9)
                ctb = CtB[0]
                for nm, dst in (("CpR", CtR), ("CpI", CtI)):
                    sy.op("act", lambda e, nm=nm, dst=dst: e.activation(out=dst[:, p0:p0 + PBN, :].rearrange("p j (t h) -> p j t h", h=16),
                                                                      in_=tmp[nm][:, :, 1:9, :], func=AF.Identity),
                          reads=[tmpB[nm]], writes=[ctb])
                if pd_ < 6:
                    continue
                for nm, zn in (("CpR", CpZR), ("CpI", CpZI)):
                    for e_ in range(2):
                        r0, r1 = 64 * e_, 64 * e_ + 64
                        sy.op("act", lambda e, nm=nm, zn=zn, e_=e_, r0=r0, r1=r1: e.activation(
                            out=zn[r0:r1, :, e_, :].rearrange("p j (s h) -> p j s h", h=16), in_=tmp[nm][r0:r1, :, 0:8, :], func=AF.Identity),
                            reads=[tmpB[nm]], writes=[CpZB])
                evacs = []
                for half in range(PBN // 2):
                    pb = 2 * (bi % 2) + half
                    fns = []
                    gs = []
                    for q in range(2):
                        jl = half * 2 + q
                        gs += [2 * (p0 + jl), 2 * (p0 + jl) + 1]
                        out = self.ps[pb][:, q * 256:(q + 1) * 256]
                        fns.append(lambda e, jl=jl, out=out: e.matmul(
                            out, lhsT=tmp["BnR"][:, jl, :, :].rearrange("p s h -> p (s h)"), rhs=CpZR[:, jl, :, :].rearrange("p e c -> p (e c)"),
                            start=True, stop=False))
                        fns.append(lambda e, jl=jl, out=out: e.matmul(
                            out, lhsT=tmp["BnI"][:, jl, :, :].rearrange("p s h -> p (s h)"), rhs=CpZI[:, jl, :, :].rearrange("p e c -> p (e c)"),
                            start=False, stop=False))
                        for e_ in range(2):
                            g = 2 * (p0 + jl) + e_
                            li_ = q * 2 + e_
                            sy.op("act", lambda e, g=g, li_=li_: e.activation(out=Lb[li_][:], in_=ident[:], func=AF.Identity, scale=dvec[:, g:g + 1]),
                                  reads=[identB, prmB], writes=[LbB[li_]])
                            fns.append(lambda e, out=out, li_=li_, e_=e_: e.matmul(out, lhsT=Lb[li_][:], rhs=IZ[:, e_, :], start=False, stop=(e_ == 1)))
                    sy.group("pe", fns, reads=[tmpB["BnR"], tmpB["BnI"], CpZB, IZB] + LbB, writes=[self.psB[pb]])

                    def ev_t(pb=pb, gs=gs):
                        sy.op("dve", lambda e: e.tensor_tensor(out=Tm[:, gs[0]:gs[0] + 4, :].rearrange("p g c -> p (g c)"), in0=self.ps[pb][:],
                                                               in1=mask[:].rearrange("p a t h -> p (a t h)"), op=ALU.mult),
                              reads=[self.psB[pb], maskB], writes=[TmB[gs[0] // 4]])
                    evacs.append(ev_t)
                for hp in range(PBN // 2):
                    pb = 4 + 2 * (bi % 2) + hp
                    fns = []
                    for q in range(2):
                        jl = hp * 2 + q
                        for ri, nm in enumerate(("BtR", "BtI")):
                            col = (q * 2 + ri) * 128
                            fns.append(lambda e, jl=jl, nm=nm, col=col, pb=pb: e.transpose(
                                self.ps[pb][:, col:col + 128], tmp[nm][:, jl, :, :].rearrange("p s h -> p (s h)"), ident[:]))
                    sy.group("pe", fns, reads=[tmpB["BtR"], tmpB["BtI"], identB], writes=[self.psB[pb]])
                    jp0 = p0 + hp * 2

                    def ev_b(pb=pb, jp0=jp0):
                        sy.op("act", lambda e: e.activation(out=Btm[:, jp0:jp0 + 2, :, :].rearrange("p j r c -> p (j r c)"), in_=self.ps[pb][:], func=AF.Identity),
                              reads=[self.psB[pb]], writes=[BtB[jp0 // 2]])
                    evacs.append(ev_b)
                for f_ in prev_evacs:
                    f_()
                prev_evacs = evacs
            for f_ in prev_evacs:
                f_()

    def final(self, s, next_s=None):
        sy = self.sy
        with self.scope() as es:
            sq = [self.sb(es, "fsq", [128, NFT, TT], BF16) for _ in range(2)]
            sqB = [Buf("fsq0"), Buf("fsq1")]
            rs = [self.sb(es, "frs", [128, TT]) for _ in range(2)]
            rsB = [Buf("frs0"), Buf("frs1")]
            ob = [self.sb(es, "fo", [128, NFT, TT]) for _ in range(2)]
            obB = [Buf("fo0"), Buf("fo1")]

            def square(tt):
                b = tt % 2
                xin = self.x[:, :, tt * TT:(tt + 1) * TT]
                sy.op("act", lambda e: e.activation(out=sq[b][:], in_=xin, func=AF.Square),
                      reads=[self.xB[ft][tt] for ft in range(NFT)], writes=[sqB[b]])

            square(0)
            for tt in range(NTT):
                b = tt % 2
                if tt + 1 < NTT:
                    square(tt + 1)
                pb = 6 + b
                sy.group("pe", [lambda e, ft=ft, pb=pb, b=b: e.matmul(self.ps[pb][:], lhsT=self.ones_bf[:], rhs=sq[b][:, ft, :],
                                                                      start=(ft == 0), stop=(ft == NFT - 1)) for ft in range(NFT)],
                         reads=[sqB[b], self.onesB], writes=[self.psB[pb]])
                sy.op("act", lambda e, pb=pb, b=b: e.activation(out=rs[b][:], in_=self.ps[pb][:], func=AF.Ln, scale=1.0 / D, bias=self.epsT[:]),
                      reads=[self.psB[pb], self.epsB], writes=[rsB[b]])
                sy.op("act", lambda e, b=b: e.activation(out=rs[b][:], in_=rs[b][:], func=AF.Exp, scale=-0.5), reads=[rsB[b]], writes=[rsB[b]])
                for ft in range(NFT):
                    sy.op("dve", lambda e, b=b, ft=ft, tt=tt: e.scalar_tensor_tensor(
                        out=ob[b][:, ft, :], in0=self.xt(ft, tt), scalar=self.gcol(8, ft), in1=rs[b][:],
                        op0=ALU.mult, op1=ALU.mult),
                        reads=[self.xB[ft][tt], rsB[b], self.gB], writes=[obB[b]])
                sy.dma("sp", [lambda e, b=b, tt=tt: e.dma_start(
                    out=self.yT[s, :, tt * TT:(tt + 1) * TT].rearrange("(ft p) t -> p ft t", p=128), in_=ob[b][:])],
                    reads=[obB[b]])
                if next_s is not None:
                    for ft in range(NFT):
                        sy.dma("sp", [lambda e, ft=ft, tt=tt: e.dma_start(out=self.xt(ft, tt), in_=self.xT[next_s, ft * 128:(ft + 1) * 128, tt * TT:(tt + 1) * TT])],
                               writes=[self.xB[ft][tt]])


def _vecs(inp):
    allg = np.concatenate([inp["norm_mix"], inp["norm_mlp"], np.asarray(inp["norm_final"])[None]], 0)
    v = np.zeros((128, NV), np.float32)
    v[:, 0:72] = allg.reshape(9, NFT, 128).transpose(2, 0, 1).reshape(128, 72)
    v[:, V_PSCALE:V_PSCALE + 8] = np.asarray(inp["pool_scale"])[0].reshape(NFT, 128).T
    v[:, V_SUBLN] = np.asarray(inp["da_subln"])[0]
    lam = np.concatenate([np.asarray(inp[k])[0] for k in ("da_lam_q1", "da_lam_k1", "da_lam_q2", "da_lam_k2")])
    v[:, V_LAM:V_LAM + 256] = lam[None, :]
    return v


def _s5p(inp):
    out = np.zeros((2, 128, S5NC), np.float32)
    for j in range(2):
        def pl(a):
            a = np.asarray(a)
            sh = a.shape[2:]
            a = a.reshape((32, 2, 64) + sh)
            a = np.moveaxis(a, 0, 2)
            return a.reshape((128, 32) + sh)
        lre = pl(inp["s5_lam_re"][j])
        lim = pl(inp["s5_lam_im"][j])
        lst = pl(np.repeat(np.asarray(inp["s5_log_step"][j])[:, None], 64, 1))
        bre = pl(inp["s5_b_re"][j])
        bim = pl(inp["s5_b_im"][j])
        cre = pl(np.asarray(inp["s5_c_re"][j]).transpose(0, 2, 1))
        cim = pl(np.asarray(inp["s5_c_im"][j]).transpose(0, 2, 1))
        dv = np.tile(np.asarray(inp["s5_d"][j]).reshape(64, 16).T, (8, 1))
        o = out[j]
        o[:, 0:32] = lre
        o[:, 32:64] = lim
        o[:, 64:96] = lst
        o[:, 96:608] = bre.reshape(128, 512)
        o[:, 608:1120] = bim.reshape(128, 512)
        o[:, 1120:1632] = cre.reshape(128, 512)
        o[:, 1632:2144] = cim.reshape(128, 512)
        o[:, 2144:2208] = dv
    return out


def host_shared(inp):
    f = lambda a: np.ascontiguousarray(np.asarray(a, np.float32))
    return {
        "vecs": _vecs(inp),
        "w1": f(inp["mlp_w1"]),
        "w2": f(inp["mlp_w2"]),
        "poolw": f(inp["pool_w"][0]),
        "wqkv": f(inp["da_w_qkv"][0]),
        "wo": f(inp["da_w_o"][0]),
        "s5w": f(np.stack([inp["s5_w_in"], inp["s5_w_gate"], inp["s5_w_out"]], 1)),
        "s5p": _s5p(inp),
    }


def kernel(**inputs):
    x = np.asarray(inputs["x"], np.float32)
    prog = Prog()
    nc = prog.build()
    shared = host_shared(inputs)
    in_maps = []
    for c in range(N_CORES):
        xs = x[c * SEQ_PER_CORE:(c + 1) * SEQ_PER_CORE]
        m = dict(shared)
        m["xT"] = np.ascontiguousarray(xs.transpose(0, 2, 1))
        in_maps.append(m)
    res = run_bass_kernel_spmd(nc, in_maps, core_ids=list(range(N_CORES)))
    out = np.empty_like(x)
    for c in range(N_CORES):
        out[c * SEQ_PER_CORE:(c + 1) * SEQ_PER_CORE] = res.results[c]["yT"].transpose(0, 2, 1)
    return out
```
